# Optimizing a Trainium2 kernel written in Bass

```python
import math
import jax
import jax.numpy as jnp
from jax import lax
import numpy as np

D_MODEL = 1024
BATCH = 4
SEQ = 8192
DEPTH = 2

N_BRANCH = 4
BRANCH_WIDTH = 256
MLA_HEADS = 4
MLA_Q_LORA = 384
MLA_KV_LORA = 128
MLA_NOPE = 64
MLA_ROPE = 32
MLA_V = 64
FNET_GROUPS = 4
FNET_GROUP_DIM = 64
DIFF_HEADS = 4
DIFF_HEAD_DIM = 32
HGRN_HEADS = 4
HGRN_KEY_DIM = 64
HGRN_VAL_DIM = 64
HGRN_CHUNK = 64
D_FF = -(-8 * D_MODEL // (3 * 256)) * 256

ROPE_THETA = 10000.0
Q_BLOCK = 128
EPS = 1e-6

IN_WIDTHS = (
    MLA_Q_LORA,
    MLA_KV_LORA,
    MLA_ROPE,
    FNET_GROUPS * FNET_GROUP_DIM,
    2 * DIFF_HEADS * DIFF_HEAD_DIM,
    2 * DIFF_HEADS * DIFF_HEAD_DIM,
    DIFF_HEADS * 2 * DIFF_HEAD_DIM,
    HGRN_HEADS * HGRN_KEY_DIM,
    HGRN_HEADS * HGRN_VAL_DIM,
    HGRN_HEADS * HGRN_KEY_DIM,
    HGRN_HEADS * HGRN_KEY_DIM,
    HGRN_HEADS * HGRN_VAL_DIM,
    N_BRANCH * D_MODEL,
)
IN_TOTAL = sum(IN_WIDTHS)

kernel_name = 'hybrid_mla_fnet_diffattn_hgrn2_encoder'


def rmsnorm(x, g):
    xf = x.astype(jnp.float32)
    y = xf * lax.rsqrt(jnp.mean(xf * xf, axis=-1, keepdims=True) + EPS)
    return (y * g.astype(jnp.float32)).astype(x.dtype)


def split_cols(z, widths):
    offsets = [int(o) for o in np.cumsum(widths)[:-1]]
    return jnp.split(z, offsets, axis=-1)


def rope_tables(seq_len, dim):
    inv_freq = 1.0 / (ROPE_THETA ** (jnp.arange(0, dim, 2, dtype=jnp.float32) / dim))
    ang = jnp.arange(seq_len, dtype=jnp.float32)[:, None] * inv_freq[None, :]
    return jnp.cos(ang), jnp.sin(ang)


def apply_rope(t, cos, sin):
    tf = t.astype(jnp.float32)
    t1, t2 = jnp.split(tf, 2, axis=-1)
    c = cos[None, :, None, :]
    s = sin[None, :, None, :]
    return jnp.concatenate([t1 * c - t2 * s, t1 * s + t2 * c], axis=-1).astype(t.dtype)


def to_query_blocks(t):
    b, s, h, d = t.shape
    return t.reshape(b, s // Q_BLOCK, Q_BLOCK, h, d).transpose(1, 0, 2, 3, 4)


def from_query_blocks(t):
    nb, b, qb, h, d = t.shape
    return t.transpose(1, 0, 2, 3, 4).reshape(b, nb * qb, h, d)


def block_softmax_attention(q, k, v, scale):
    def one_block(qb):
        s = jnp.einsum('bqhd,bkhd->bhqk', qb, k).astype(jnp.float32) * scale
        p = jax.nn.softmax(s, axis=-1)
        return jnp.einsum('bhqk,bkhd->bqhd', p.astype(v.dtype), v)
    return from_query_blocks(lax.map(one_block, to_query_blocks(q)))


def block_diff_attention(q1, q2, k1, k2, v, lam, scale):
    def one_block(qs):
        qb1, qb2 = qs
        s1 = jnp.einsum('bqhd,bkhd->bhqk', qb1, k1).astype(jnp.float32) * scale
        s2 = jnp.einsum('bqhd,bkhd->bhqk', qb2, k2).astype(jnp.float32) * scale
        p = jax.nn.softmax(s1, axis=-1) - lam * jax.nn.softmax(s2, axis=-1)
        return jnp.einsum('bhqk,bkhd->bqhd', p.astype(v.dtype), v)
    return from_query_blocks(lax.map(one_block, (to_query_blocks(q1), to_query_blocks(q2))))


def mla_mixer(c_q, c_kv, k_rope, g_qa, w_uq, g_kva, w_ukv, g_qn, g_kn, cos, sin):
    b, s, _ = c_q.shape
    qk_dim = MLA_NOPE + MLA_ROPE
    q = (rmsnorm(c_q, g_qa) @ w_uq).reshape(b, s, MLA_HEADS, qk_dim)
    kv = (rmsnorm(c_kv, g_kva) @ w_ukv).reshape(b, s, MLA_HEADS, MLA_NOPE + MLA_V)
    k_nope, v = jnp.split(kv, [MLA_NOPE], axis=-1)
    k_pe = jnp.broadcast_to(k_rope[:, :, None, :], (b, s, MLA_HEADS, MLA_ROPE))
    k = jnp.concatenate([k_nope, k_pe], axis=-1)
    q = rmsnorm(q, g_qn)
    k = rmsnorm(k, g_kn)
    q = jnp.concatenate([q[..., :MLA_NOPE], apply_rope(q[..., MLA_NOPE:], cos, sin)], axis=-1)
    k = jnp.concatenate([k[..., :MLA_NOPE], apply_rope(k[..., MLA_NOPE:], cos, sin)], axis=-1)
    o = block_softmax_attention(q, k, v, qk_dim ** -0.5)
    return o.reshape(b, s, MLA_HEADS * MLA_V)


def fnet_mixer(u):
    b, s, _ = u.shape
    uf = u.astype(jnp.float32).reshape(b, s, FNET_GROUPS, FNET_GROUP_DIM)
    y = jnp.fft.fft2(uf, axes=(1, 3), norm='ortho').real
    return y.reshape(b, s, FNET_GROUPS * FNET_GROUP_DIM).astype(u.dtype)


def diff_mixer(q_in, k_in, v_in, g_qn, g_kn, lq1, lk1, lq2, lk2, g_sub, layer_idx, cos, sin):
    b, s, _ = q_in.shape
    d = DIFF_HEAD_DIM
    q = apply_rope(rmsnorm(q_in.reshape(b, s, 2 * DIFF_HEADS, d), g_qn), cos, sin)
    k = apply_rope(rmsnorm(k_in.reshape(b, s, 2 * DIFF_HEADS, d), g_kn), cos, sin)
    q = q.reshape(b, s, DIFF_HEADS, 2, d)
    k = k.reshape(b, s, DIFF_HEADS, 2, d)
    v = v_in.reshape(b, s, DIFF_HEADS, 2 * d)
    lambda_init = 0.8 - 0.6 * math.exp(-0.3 * layer_idx)
    lam = (jnp.exp(jnp.sum(lq1.astype(jnp.float32) * lk1.astype(jnp.float32)))
           - jnp.exp(jnp.sum(lq2.astype(jnp.float32) * lk2.astype(jnp.float32)))
           + lambda_init)
    o = block_diff_attention(q[:, :, :, 0], q[:, :, :, 1], k[:, :, :, 0], k[:, :, :, 1], v, lam, d ** -0.5)
    o = rmsnorm(o, g_sub) * (1.0 - lambda_init)
    return o.reshape(b, s, DIFF_HEADS * 2 * d)


def hgrn2_chunk_scan(q, k, v, log_f):
    b, s, h, dk = q.shape
    dv = v.shape[-1]
    c = HGRN_CHUNK
    n_chunks = s // c

    def chunks(t):
        return t.reshape(b, n_chunks, c, h, t.shape[-1]).transpose(1, 0, 3, 2, 4)

    lower_tri = jnp.tril(jnp.ones((c, c), dtype=bool))[:, :, None]

    def step(state, inp):
        qc, kc, vc, lc = inp
        cum = jnp.cumsum(lc, axis=2)
        inter = jnp.einsum('bhtk,bhkv->bhtv', qc * jnp.exp(cum), state)
        rel = cum[:, :, :, None, :] - cum[:, :, None, :, :]
        decay = jnp.exp(jnp.where(lower_tri, rel, -jnp.inf))
        scores = jnp.einsum('bhtk,bhtsk,bhsk->bhts', qc, decay, kc)
        intra = jnp.einsum('bhts,bhsv->bhtv', scores, vc)
        last = cum[:, :, -1:, :]
        new_state = (jnp.exp(last[:, :, 0, :])[..., None] * state
                     + jnp.einsum('bhsk,bhsv->bhkv', kc * jnp.exp(last - cum), vc))
        return new_state, inter + intra

    init = jnp.zeros((b, h, dk, dv), jnp.float32)
    _, o = lax.scan(step, init, (chunks(q), chunks(k), chunks(v), chunks(log_f)))
    return o.transpose(1, 0, 3, 2, 4).reshape(b, s, h, dv)


def hgrn2_mixer(q_in, i_in, f_fwd_in, f_bwd_in, g_in, lb_fwd, lb_bwd, g_on):
    b, s, _ = q_in.shape
    h, dk, dv = HGRN_HEADS, HGRN_KEY_DIM, HGRN_VAL_DIM
    q = q_in.astype(jnp.float32).reshape(b, s, h, dk)
    v = i_in.astype(jnp.float32).reshape(b, s, h, dv)

    def forget(z, lb):
        lbh = lb.reshape(h, dk)
        f = lbh + (1.0 - lbh) * jax.nn.sigmoid(z.astype(jnp.float32).reshape(b, s, h, dk))
        return 1.0 - f, jnp.log(f)

    k_f, lf_f = forget(f_fwd_in, lb_fwd)
    k_b, lf_b = forget(f_bwd_in, lb_bwd)
    o_fwd = hgrn2_chunk_scan(q, k_f, v, lf_f)
    o_bwd = jnp.flip(hgrn2_chunk_scan(jnp.flip(q, 1), jnp.flip(k_b, 1), jnp.flip(v, 1), jnp.flip(lf_b, 1)), 1)
    o = rmsnorm(o_fwd + o_bwd, g_on) * jax.nn.sigmoid(g_in.astype(jnp.float32).reshape(b, s, h, dv))
    return o.reshape(b, s, h * dv).astype(q_in.dtype)


def setup_inputs(seed: int = 0) -> dict:
    key = jax.random.key(seed)
    ks = jax.random.split(key, 24)
    L = DEPTH
    qk_m = MLA_NOPE + MLA_ROPE

    def nrm(k, shape, scale):
        return jax.random.normal(k, shape, jnp.float32) * scale

    def gain(k, shape):
        return 1.0 + 0.02 * jax.random.normal(k, shape, jnp.float32)

    return {
        'x': nrm(ks[0], (BATCH, SEQ, D_MODEL), 1.0),
        'ln_mix': gain(ks[1], (L, D_MODEL)),
        'w_in': nrm(ks[2], (L, D_MODEL, IN_TOTAL), D_MODEL ** -0.5),
        'mla_g_qa': gain(ks[3], (L, MLA_Q_LORA)),
        'mla_w_uq': nrm(ks[4], (L, MLA_Q_LORA, MLA_HEADS * qk_m), MLA_Q_LORA ** -0.5),
        'mla_g_kva': gain(ks[5], (L, MLA_KV_LORA)),
        'mla_w_ukv': nrm(ks[6], (L, MLA_KV_LORA, MLA_HEADS * (MLA_NOPE + MLA_V)), MLA_KV_LORA ** -0.5),
        'mla_g_qn': gain(ks[7], (L, qk_m)),
        'mla_g_kn': gain(ks[8], (L, qk_m)),
        'diff_g_qn': gain(ks[9], (L, DIFF_HEAD_DIM)),
        'diff_g_kn': gain(ks[10], (L, DIFF_HEAD_DIM)),
        'diff_lq1': nrm(ks[11], (L, DIFF_HEAD_DIM), 0.1),
        'diff_lk1': nrm(ks[12], (L, DIFF_HEAD_DIM), 0.1),
        'diff_lq2': nrm(ks[13], (L, DIFF_HEAD_DIM), 0.1),
        'diff_lk2': nrm(ks[14], (L, DIFF_HEAD_DIM), 0.1),
        'diff_g_sub': gain(ks[15], (L, 2 * DIFF_HEAD_DIM)),
        'hgrn_lb_logits': nrm(ks[16], (2, L, HGRN_HEADS * HGRN_KEY_DIM), 0.5),
        'hgrn_g_on': gain(ks[17], (L, HGRN_VAL_DIM)),
        'w_branch': nrm(ks[18], (L, N_BRANCH, BRANCH_WIDTH, D_MODEL), BRANCH_WIDTH ** -0.5),
        'w_out': nrm(ks[19], (L, D_MODEL, D_MODEL), D_MODEL ** -0.5),
        'ln_ffn': gain(ks[20], (L, D_MODEL)),
        'w_gate_up': nrm(ks[21], (L, D_MODEL, 2 * D_FF), D_MODEL ** -0.5),
        'w_down': nrm(ks[22], (L, D_FF, D_MODEL), D_FF ** -0.5),
    }


def reference(x, ln_mix, w_in, mla_g_qa, mla_w_uq, mla_g_kva, mla_w_ukv, mla_g_qn, mla_g_kn,
              diff_g_qn, diff_g_kn, diff_lq1, diff_lk1, diff_lq2, diff_lk2, diff_g_sub,
              hgrn_lb_logits, hgrn_g_on, w_branch, w_out, ln_ffn, w_gate_up, w_down):
    seq_len = x.shape[1]
    cos_m, sin_m = rope_tables(seq_len, MLA_ROPE)
    cos_d, sin_d = rope_tables(seq_len, DIFF_HEAD_DIM)
    lb_p = jax.nn.softmax(hgrn_lb_logits.astype(jnp.float32), axis=1)
    lower_bounds = jnp.cumsum(lb_p, axis=1) - lb_p[:, :1]

    for l in range(DEPTH):
        h = rmsnorm(x, ln_mix[l])
        z = h @ w_in[l]
        (c_q, c_kv, k_rope, fnet_in, dq, dk, dv,
         hq, hi, hf_fwd, hf_bwd, hg, gate_logits) = split_cols(z, IN_WIDTHS)

        o_mla = mla_mixer(c_q, c_kv, k_rope, mla_g_qa[l], mla_w_uq[l], mla_g_kva[l], mla_w_ukv[l],
                          mla_g_qn[l], mla_g_kn[l], cos_m, sin_m)
        o_fnet = fnet_mixer(fnet_in)
        o_diff = diff_mixer(dq, dk, dv, diff_g_qn[l], diff_g_kn[l], diff_lq1[l], diff_lk1[l],
                            diff_lq2[l], diff_lk2[l], diff_g_sub[l], l, cos_d, sin_d)
        o_hgrn = hgrn2_mixer(hq, hi, hf_fwd, hf_bwd, hg, lower_bounds[0, l], lower_bounds[1, l],
                             hgrn_g_on[l])

        merged = jnp.zeros_like(x)
        for n, o_branch in enumerate((o_mla, o_fnet, o_diff, o_hgrn)):
            gate = jax.nn.sigmoid(gate_logits[..., n * D_MODEL:(n + 1) * D_MODEL].astype(jnp.float32))
            merged = merged + gate.astype(x.dtype) * (o_branch @ w_branch[l, n])
        x = x + merged @ w_out[l]

        h2 = rmsnorm(x, ln_ffn[l])
        g_part, u_part = jnp.split(h2 @ w_gate_up[l], 2, axis=-1)
        x = x + (jax.nn.silu(g_part) * u_part) @ w_down[l]
    return x
```

```python
LIM = 99
SUB = 99

import concourse.bass as bass
import concourse.mybir as mybir

ENGS = ("pe", "act", "dve", "pool", "sp")
SKIP_PE_SELF = [False]
ATTACH = ["none"]


class Op:
    __slots__ = ("eng", "fn", "deps", "signal", "cnt", "is_dma", "chan", "chan_idx", "idx")

    def __init__(self, eng, fn, is_dma=False, chan=None):
        self.eng = eng
        self.fn = fn
        self.deps = []
        self.signal = False
        self.cnt = None
        self.is_dma = is_dma
        self.chan = chan
        self.chan_idx = None
        self.idx = None


class Sched:
    _uid = [0]

    def __init__(self, nc):
        self.nc = nc
        Sched._uid[0] += 1
        self.uid = Sched._uid[0]
        self.ops = {e: [] for e in ENGS}
        self.last_w = {}
        self.readers = {}
        self.chan_ops = {}
        self.final_dma = []

    def _track(self, op, reads, writes, deps):
        ds = list(deps)
        for k in reads:
            w = self.last_w.get(k)
            if w is not None:
                ds.append(w)
        for k in writes:
            w = self.last_w.get(k)
            if w is not None:
                ds.append(w)
            ds.extend(self.readers.get(k, ()))
        for k in writes:
            self.last_w[k] = op
            self.readers[k] = []
        for k in reads:
            self.readers.setdefault(k, []).append(op)
        seen = set()
        for d in ds:
            if d is op or id(d) in seen:
                continue
            seen.add(id(d))
            op.deps.append(d)
            if not (SKIP_PE_SELF[0] and op.eng == "pe" and d.eng == "pe" and not d.is_dma):
                d.signal = True

    def op(self, eng, fn, reads=(), writes=(), deps=()):
        o = Op(eng, fn)
        o.idx = len(self.ops[eng])
        self.ops[eng].append(o)
        self._track(o, reads, writes, deps)
        return o

    def dma(self, eng, out, in_, chan, reads=(), writes=(), deps=(), final=False, **kw):
        def fn(e, out=out, in_=in_, kw=kw):
            return e.dma_start(out=out, in_=in_, **kw)
        o = Op(eng, fn, is_dma=True, chan=chan)
        o.idx = len(self.ops[eng])
        lst = self.chan_ops.setdefault(chan, [])
        o.chan_idx = len(lst)
        prev = lst[-1] if lst else None
        lst.append(o)
        self.ops[eng].append(o)
        self._track(o, reads, writes, list(deps) + ([prev] if prev is not None else []))
        o.signal = True
        if final:
            self.final_dma.append(o)
        return o

    def emit(self):
        nc = self.nc
        import contextlib
        with contextlib.ExitStack() as st:
            esem = {e: st.enter_context(nc.semaphore("s%d_%s" % (self.uid, e))) for e in ENGS if e != "sp"}
            csem = {c: st.enter_context(nc.semaphore("c%d_%s" % (self.uid, c))) for c in self.chan_ops}
            for e in ENGS:
                c = 0
                for o in self.ops[e]:
                    if o.is_dma:
                        continue
                    if o.signal:
                        c += 1
                        o.cnt = c
            with nc.Block() as b0:
                @b0.gpsimd
                def _(g):
                    for s_ in list(esem.values()) + list(csem.values()):
                        g.sem_clear(s_)
            block = st.enter_context(nc.Block())

            def event(o):
                if o.is_dma:
                    return csem[o.chan], 16 * (o.chan_idx + 1)
                return esem[o.eng], o.cnt

            def run(e, engobj):
                waited = {}
                for o in self.ops[e]:
                    need = {}
                    for d in o.deps:
                        if SKIP_PE_SELF[0] and e == "pe" and d.eng == "pe" and not d.is_dma:
                            continue
                        s, v = event(d)
                        if waited.get(s.name, 0) >= v:
                            continue
                        if s.name not in need or need[s.name][1] < v:
                            need[s.name] = (s, v)
                    lst = list(need.values())
                    attach = None
                    if lst and not o.is_dma and (ATTACH[0] == "all" or (ATTACH[0] == "pe" and e == "pe")):
                        attach = lst.pop()
                    for i, (s, v) in enumerate(lst):
                        engobj.wait_ge(s, v)
                        waited[s.name] = v
                        rem = len(lst) - (i + 1)
                        if rem >= 1 and (i % 2 == 1):
                            engobj.nop()
                    ins = o.fn(engobj)
                    if attach is not None:
                        ins._wait_ge(attach[0], attach[1])
                        waited[attach[0].name] = attach[1]
                    if o.is_dma:
                        ins.then_inc(csem[o.chan], 16)
                    elif o.signal:
                        ins.then_inc(esem[e], 1)
                if e == "sp":
                    for o in self.final_dma:
                        s, v = event(o)
                        engobj.wait_ge(s, v)

            @block.tensor
            def _(t):
                run("pe", t)

            @block.scalar
            def _(t):
                run("act", t)

            @block.vector
            def _(t):
                run("dve", t)

            @block.gpsimd
            def _(t):
                run("pool", t)

            @block.sync
            def _(t):
                run("sp", t)

import contextlib
import numpy as np
import concourse.bass as bass
import concourse.mybir as mybir

F32 = mybir.dt.float32
BF16 = mybir.dt.bfloat16
AF = mybir.ActivationFunctionType
ALU = mybir.AluOpType

D = 1024
DFF = 2816
NT_R = 4096
ST = 512
EPS = 1e-6
TAG = [""]


def load_w_bf16(S, nc, st, name, dram, kc, n, wb, stage, q="sp", cast="pool", nblk=256):
    v = dram.rearrange("(c p) n -> p c n", p=128)
    i = 0
    for n0 in range(0, n, nblk):
        nb = min(nblk, n - n0)
        sl = i % 2
        sv = stage[sl][:, 0:kc * nb].rearrange("p (c n) -> p c n", c=kc)
        S.dma(q if sl == 0 else "act", sv, v[:, :, n0:n0 + nb], "ldw_%d" % sl, writes=["stg%d" % sl])
        S.op(cast, lambda e, sv=sv, n0=n0, nb=nb: e.tensor_copy(out=wb[:, :, n0:n0 + nb], in_=sv),
             reads=["stg%d" % sl], writes=["w_" + name])
        i += 1


def rmsnorm_tile(S, tag, x_ap, xkey, gB, h_ap, hkey, junk, stat, col):
    sk = "stat_%s_%d" % (tag, col)
    S.op("act", lambda e: e.activation(out=junk[:], in_=x_ap, func=AF.Square, accum_out=stat[:, col:col + 1]),
         reads=[xkey], writes=["junk_" + tag, sk])
    S.op("dve", lambda e: e.tensor_scalar(out=stat[:, col:col + 1], in0=stat[:, col:col + 1], scalar1=1.0 / D, scalar2=EPS,
                                           op0=ALU.mult, op1=ALU.add), reads=[sk], writes=[sk])
    S.op("act", lambda e: e.activation(out=stat[:, col:col + 1], in_=stat[:, col:col + 1], func=AF.Ln), reads=[sk], writes=[sk])
    S.op("act", lambda e: e.activation(out=stat[:, col:col + 1], in_=stat[:, col:col + 1], func=AF.Exp, scale=-0.5),
         reads=[sk], writes=[sk])
    S.op("dve", lambda e: e.scalar_tensor_tensor(out=h_ap, in0=x_ap, scalar=stat[:, col:col + 1], in1=gB[:],
                                                  op0=ALU.mult, op1=ALU.mult), reads=[xkey, sk, "gB_" + tag], writes=[hkey])


def emit_R1(nc, x_in, oT_in, wg, wbr, wo, ln, xmid_out, ident_d, ntok=NT_R, gathered=None):
    nst = ntok // ST
    PFX = "r1_" + TAG[0]
    with contextlib.ExitStack() as st:
        T = lambda name, shape, dt: st.enter_context(nc.sbuf_tensor(PFX + name, shape, dt))
        Wg = T("Wg", [128, 8, 4096], BF16)
        Wb = T("Wb", [128, 8, 1024], BF16)
        Wo = T("Wo", [128, 8, 1024], BF16)
        stage = [T("stg0", [128, 2048], F32), T("stg1", [128, 2048], F32)]
        gB = T("gB", [128, D], F32)
        idf = T("idf", [128, 128], F32)
        idb = T("idb", [128, 128], BF16)
        xt = T("xt", [128, 4, D], F32)
        hb = T("hb", [128, D], BF16)
        hT = T("hT", [128, 8, ST], BF16)
        oT = T("oT", [128, 8, ST], BF16)
        if gathered is not None:
            Ga = T("Ga", [128, 8, ST], BF16)
            Gb = T("Gb", [128, 8, ST], BF16)
            msk = T("msk", [128, 2], F32)
        mg = T("mg", [128, 8, ST], F32)
        mgb = T("mgb", [128, 8, ST], BF16)
        gt = [T("gt0", [128, ST], F32), T("gt1", [128, ST], F32)]
        junk = T("junk", [128, D], BF16)
        stat = T("stat", [128, 4], F32)
        pT = st.enter_context(nc.psum_tensor(PFX + "pT", [128, 8, 128], BF16))
        pG = [st.enter_context(nc.psum_tensor(PFX + "pG%d" % i, [128, ST], F32)) for i in range(2)]
        pB = [st.enter_context(nc.psum_tensor(PFX + "pB%d" % i, [128, ST], F32)) for i in range(2)]
        pO = [st.enter_context(nc.psum_tensor(PFX + "pO%d" % i, [128, ST], F32)) for i in range(2)]
        S = Sched(nc)
        S.dma("act", idf[:], ident_d, "ld_id", writes=["idf"])
        S.op("dve", lambda e: e.tensor_copy(out=idb[:], in_=idf[:]), reads=["idf"], writes=["idb"])
        S.dma("act", gB[:], ln.rearrange("(o d) -> o d", o=1).partition_broadcast(128), "ld_g", writes=["gB_r1"])
        load_w_bf16(S, nc, st, "g", wg, 8, 4096, Wg, stage)
        load_w_bf16(S, nc, st, "b", wbr, 8, 1024, Wb, stage)
        load_w_bf16(S, nc, st, "o", wo, 8, 1024, Wo, stage)
        xv = x_in.rearrange("(s j p) d -> s p j d", p=128, j=4)
        ov = xmid_out.rearrange("(s j p) d -> s p j d", p=128, j=4)
        if gathered is None:
            oTv = oT_in.rearrange("(c p) t -> p c t", p=128)
        else:
            Gd, selmask = gathered
            S.dma("act", msk[:], selmask.rearrange("(o d) -> o d", o=1).partition_broadcast(128), "ld_g", writes=["msk"])
        k = 0
        for s in range(nst):
            S.dma("sp", xt[:], xv[s], "ld_x", writes=["xt"])
            if gathered is None:
                S.dma("act", oT[:], oTv[:, :, s * ST:(s + 1) * ST], "ld_o", writes=["oT"])
            else:
                for hf, Gt, gk in ((0, Ga, "Ga"), (1, Gb, "Gb")):
                    for n in range(4):
                        c0 = hf * ntok + s * ST
                        S.dma("act", Gt[:, 2 * n:2 * n + 2, :], Gd[n, :, c0:c0 + ST].rearrange("(r p) t -> p r t", p=128),
                              "ld_o%d" % hf, writes=[gk])
                S.op("dve", lambda e: e.tensor_scalar(out=oT[:], in0=Ga[:], scalar1=msk[:, 0:1], scalar2=None, op0=ALU.mult),
                     reads=["Ga", "msk"], writes=["oT"])
                S.op("dve", lambda e: e.scalar_tensor_tensor(out=oT[:], in0=Gb[:], scalar=msk[:, 1:2], in1=oT[:], op0=ALU.mult, op1=ALU.add),
                     reads=["Gb", "msk", "oT"], writes=["oT"])
            for j in range(4):
                rmsnorm_tile(S, "r1", xt[:, j, :], "xt", gB, hb[:], "hb", junk, stat, j)
                for c in range(8):
                    S.op("pe", lambda e, c=c: e.transpose(out=pT[:, c, :], in_=hb[:, c * 128:(c + 1) * 128], identity=idb[:]),
                         reads=["hb", "idb"], writes=["pT"] if c in (0, 7) else [])
                S.op("dve", lambda e, j=j: e.tensor_copy(out=hT[:, :, j * 128:(j + 1) * 128], in_=pT[:]),
                     reads=["pT"], writes=["hT"])
            for cc in range(8):
                for n in range(4):
                    b = k % 2
                    k += 1
                    col0 = n * 1024 + cc * 128
                    for c in range(8):
                        S.op("pe", lambda e, c=c, b=b, col0=col0: e.matmul(out=pG[b][:], lhsT=Wg[:, c, col0:col0 + 128], rhs=hT[:, c, :],
                                                                         start=(c == 0), stop=(c == 7)),
                             reads=["hT", "w_g"], writes=["pG%d" % b] if c in (0, 7) else [])
                    for c2 in range(2):
                        S.op("pe", lambda e, c2=c2, b=b, n=n, cc=cc: e.matmul(out=pB[b][:], lhsT=Wb[:, n * 2 + c2, cc * 128:(cc + 1) * 128],
                                                                            rhs=oT[:, n * 2 + c2, :], start=(c2 == 0), stop=(c2 == 1)),
                             reads=["oT", "w_b"], writes=["pB%d" % b])
                    S.op("act", lambda e, b=b: e.activation(out=gt[b][:], in_=pG[b][:], func=AF.Sigmoid),
                         reads=["pG%d" % b], writes=["gt%d" % b])
                    if n == 0:
                        S.op("dve", lambda e, b=b, cc=cc: e.tensor_tensor(out=mg[:, cc, :], in0=gt[b][:], in1=pB[b][:], op=ALU.mult),
                             reads=["gt%d" % b, "pB%d" % b], writes=["mg%d" % cc])
                    else:
                        S.op("dve", lambda e, b=b: e.tensor_tensor(out=gt[b][:], in0=gt[b][:], in1=pB[b][:], op=ALU.mult),
                             reads=["gt%d" % b, "pB%d" % b], writes=["gt%d" % b])
                        S.op("pool", lambda e, b=b, cc=cc: e.tensor_tensor(out=mg[:, cc, :], in0=mg[:, cc, :], in1=gt[b][:], op=ALU.add),
                             reads=["gt%d" % b, "mg%d" % cc], writes=["mg%d" % cc])
                S.op("pool", lambda e, cc=cc: e.tensor_copy(out=mgb[:, cc, :], in_=mg[:, cc, :]), reads=["mg%d" % cc], writes=["mgb%d" % cc])
            for j in range(4):
                for hf in range(2):
                    b = k % 2
                    k += 1
                    for c in range(8):
                        S.op("pe", lambda e, c=c, b=b, j=j, hf=hf: e.matmul(out=pO[b][:], lhsT=mgb[:, c, j * 128:(j + 1) * 128],
                                                                          rhs=Wo[:, c, hf * 512:(hf + 1) * 512], start=(c == 0), stop=(c == 7)),
                             reads=["mgb%d" % c, "w_o"], writes=["pO%d" % b] if c in (0, 7) else [])
                    S.op("dve", lambda e, b=b, j=j, hf=hf: e.tensor_tensor(out=xt[:, j, hf * 512:(hf + 1) * 512], in0=xt[:, j, hf * 512:(hf + 1) * 512],
                                                                         in1=pO[b][:], op=ALU.add),
                         reads=["pO%d" % b, "xt"], writes=["xt"])
            S.dma("sp", ov[s], xt[:], "st_x", reads=["xt"], final=True)
        S.emit()


def emit_R2(nc, xmid, wgu, wd, ln, x_out, ident_d, ntok=NT_R):
    nst = ntok // ST
    NF = DFF // 128
    PFX = "r2_" + TAG[0]
    with contextlib.ExitStack() as st:
        T = lambda name, shape, dt: st.enter_context(nc.sbuf_tensor(PFX + name, shape, dt))
        Wgu = T("Wgu", [128, 8, 2 * DFF], BF16)
        Wd = T("Wd", [128, NF, D], BF16)
        stage = [T("stg0", [128, 1408], F32), T("stg1", [128, 1408], F32)]
        gB = T("gB", [128, D], F32)
        idf = T("idf", [128, 128], F32)
        idb = T("idb", [128, 128], BF16)
        xt = T("xt", [128, 4, D], F32)
        hb = T("hb", [128, D], BF16)
        hT = T("hT", [128, 8, ST], BF16)
        aT = T("aT", [128, NF, ST], BF16)
        sg = [T("sg0", [128, ST], F32), T("sg1", [128, ST], F32)]
        junk = T("junk", [128, D], BF16)
        stat = T("stat", [128, 4], F32)
        pT = st.enter_context(nc.psum_tensor(PFX + "pT", [128, 8, 128], BF16))
        pG = [st.enter_context(nc.psum_tensor(PFX + "pG%d" % i, [128, ST], F32)) for i in range(2)]
        pU = [st.enter_context(nc.psum_tensor(PFX + "pU%d" % i, [128, ST], F32)) for i in range(2)]
        pO = [st.enter_context(nc.psum_tensor(PFX + "pO%d" % i, [128, ST], F32)) for i in range(2)]
        S = Sched(nc)
        S.dma("act", idf[:], ident_d, "ld_id", writes=["idf"])
        S.op("dve", lambda e: e.tensor_copy(out=idb[:], in_=idf[:]), reads=["idf"], writes=["idb"])
        S.dma("act", gB[:], ln.rearrange("(o d) -> o d", o=1).partition_broadcast(128), "ld_g", writes=["gB_r2"])
        load_w_bf16(S, nc, st, "gu", wgu, 8, 2 * DFF, Wgu, stage, nblk=176)
        load_w_bf16(S, nc, st, "d", wd, NF, D, Wd, stage, nblk=64)
        xv = xmid.rearrange("(s j p) d -> s p j d", p=128, j=4)
        ov = x_out.rearrange("(s j p) d -> s p j d", p=128, j=4)
        k = 0
        for s in range(nst):
            S.dma("sp", xt[:], xv[s], "ld_x", writes=["xt"])
            for j in range(4):
                rmsnorm_tile(S, "r2", xt[:, j, :], "xt", gB, hb[:], "hb", junk, stat, j)
                for c in range(8):
                    S.op("pe", lambda e, c=c: e.transpose(out=pT[:, c, :], in_=hb[:, c * 128:(c + 1) * 128], identity=idb[:]),
                         reads=["hb", "idb"], writes=["pT"] if c in (0, 7) else [])
                S.op("dve", lambda e, j=j: e.tensor_copy(out=hT[:, :, j * 128:(j + 1) * 128], in_=pT[:]),
                     reads=["pT"], writes=["hT"])
            for fc in range(NF):
                b = k % 2
                k += 1
                for c in range(8):
                    S.op("pe", lambda e, c=c, b=b, fc=fc: e.matmul(out=pG[b][:], lhsT=Wgu[:, c, fc * 128:(fc + 1) * 128], rhs=hT[:, c, :],
                                                                 start=(c == 0), stop=(c == 7)),
                         reads=["hT", "w_gu"], writes=["pG%d" % b] if c in (0, 7) else [])
                for c in range(8):
                    S.op("pe", lambda e, c=c, b=b, fc=fc: e.matmul(out=pU[b][:], lhsT=Wgu[:, c, DFF + fc * 128:DFF + (fc + 1) * 128], rhs=hT[:, c, :],
                                                                 start=(c == 0), stop=(c == 7)),
                         reads=["hT", "w_gu"], writes=["pU%d" % b] if c in (0, 7) else [])
                S.op("act", lambda e, b=b: e.activation(out=sg[b][:], in_=pG[b][:], func=AF.Silu), reads=["pG%d" % b], writes=["sg%d" % b])
                S.op("dve", lambda e, b=b, fc=fc: e.tensor_tensor(out=aT[:, fc, :], in0=sg[b][:], in1=pU[b][:], op=ALU.mult),
                     reads=["sg%d" % b, "pU%d" % b], writes=["aT%d" % fc])
            for j in range(4):
                for hf in range(2):
                    b = k % 2
                    k += 1
                    for fc in range(NF):
                        S.op("pe", lambda e, fc=fc, b=b, j=j, hf=hf: e.matmul(out=pO[b][:], lhsT=aT[:, fc, j * 128:(j + 1) * 128],
                                                                            rhs=Wd[:, fc, hf * 512:(hf + 1) * 512], start=(fc == 0), stop=(fc == NF - 1)),
                             reads=["aT%d" % fc, "w_d"], writes=["pO%d" % b] if fc in (0, NF - 1) else [])
                    S.op("dve", lambda e, b=b, j=j, hf=hf: e.tensor_tensor(out=xt[:, j, hf * 512:(hf + 1) * 512], in0=xt[:, j, hf * 512:(hf + 1) * 512],
                                                                         in1=pO[b][:], op=ALU.add),
                         reads=["pO%d" % b, "xt"], writes=["xt"])
            S.dma("sp", ov[s], xt[:], "st_x", reads=["xt"], final=True)
        S.emit()


def build_R(ntok=NT_R):
    nc = bass.Bass("TRN2", target_bir_lowering=False)
    nc.allow_low_precision("bf16 matmul operands, fp32 accumulate (reference tolerance is bf16-level)")
    x_in = nc.dram_tensor("x_in", [ntok, D], F32, kind="ExternalInput").ap()
    oT_in = nc.dram_tensor("oT_in", [1024, ntok], BF16, kind="ExternalInput").ap()
    wg = nc.dram_tensor("wg", [D, 4096], F32, kind="ExternalInput").ap()
    wbr = nc.dram_tensor("wbr", [1024, D], F32, kind="ExternalInput").ap()
    wo = nc.dram_tensor("wo", [D, D], F32, kind="ExternalInput").ap()
    ln1 = nc.dram_tensor("ln1", [D], F32, kind="ExternalInput").ap()
    ln2 = nc.dram_tensor("ln2", [D], F32, kind="ExternalInput").ap()
    wgu = nc.dram_tensor("wgu", [D, 2 * DFF], F32, kind="ExternalInput").ap()
    wd = nc.dram_tensor("wd", [DFF, D], F32, kind="ExternalInput").ap()
    ident = nc.dram_tensor("ident", [128, 128], F32, kind="ExternalInput").ap()
    xmid = nc.dram_tensor("xmid", [ntok, D], F32, kind="Internal").ap()
    x_out = nc.dram_tensor("x_out", [ntok, D], F32, kind="ExternalOutput").ap()
    emit_R1(nc, x_in, oT_in, wg, wbr, wo, ln1, xmid, ident, ntok)
    emit_R2(nc, xmid, wgu, wd, ln2, x_out, ident, ntok)
    return nc

import contextlib
import numpy as np
import concourse.bass as bass
import concourse.mybir as mybir

AX = mybir.AxisListType
NOWN = 1696
SEQ = 8192


def rms_groups(S, tag, src, srckey, G, d, gain, gainkey, out, outkey, tmp, st, stcol):
    sk = "st_%s_%d" % (tag, stcol)
    srck = list(srckey) if isinstance(srckey, (list, tuple)) else [srckey]
    stv = st[:, stcol:stcol + G]
    for g in range(G):
        S.op("act", lambda e, g=g: e.activation(out=tmp[:, 0:d], in_=src[:, g * d:(g + 1) * d], func=AF.Square,
                                                accum_out=st[:, stcol + g:stcol + g + 1]),
             reads=srck, writes=["tmp", sk + "_%d" % g])
    sks = [sk + "_%d" % g for g in range(G)]
    S.op("dve", lambda e: e.tensor_scalar(out=stv, in0=stv, scalar1=1.0 / d, scalar2=EPS, op0=ALU.mult, op1=ALU.add),
         reads=sks, writes=[sk])
    S.op("act", lambda e: e.activation(out=stv, in_=stv, func=AF.Ln), reads=[sk], writes=[sk])
    S.op("act", lambda e: e.activation(out=stv, in_=stv, func=AF.Exp, scale=-0.5), reads=[sk], writes=[sk])
    for g in range(G):
        S.op("dve", lambda e, g=g: e.scalar_tensor_tensor(out=out[:, g * d:(g + 1) * d], in0=src[:, g * d:(g + 1) * d],
                                                          scalar=st[:, stcol + g:stcol + g + 1], in1=gain, op0=ALU.mult, op1=ALU.mult),
             reads=srck + [sk, gainkey], writes=[outkey])


def rope_groups(S, tag, src, srckey, G, d, off, cs, cskey, out, outkey, tmp, dout=None):
    tk = "rtmp"
    srck = list(srckey) if isinstance(srckey, (list, tuple)) else [srckey]
    sv = src.rearrange("p (g d) -> p g d", g=G)
    ov = out.rearrange("p (g d) -> p g d", g=G) if dout is None else out.rearrange("p (g d) -> p g d", d=dout)
    t1 = sv[:, :, off:off + 16]
    t2 = sv[:, :, off + 16:off + 32]
    c = cs[:, 0:G * 16].rearrange("p (g d) -> p g d", g=G)
    s = cs[:, 64:64 + G * 16].rearrange("p (g d) -> p g d", g=G)
    a = tmp[:, 0:G * 16].rearrange("p (g d) -> p g d", g=G)
    b = tmp[:, 64:64 + G * 16].rearrange("p (g d) -> p g d", g=G)
    o1 = ov[:, :, off:off + 16]
    o2 = ov[:, :, off + 16:off + 32]
    S.op("dve", lambda e: e.tensor_tensor(out=a, in0=t1, in1=c, op=ALU.mult), reads=srck + [cskey], writes=[tk + "a"])
    S.op("dve", lambda e: e.tensor_tensor(out=b, in0=t2, in1=s, op=ALU.mult), reads=srck + [cskey], writes=[tk + "b"])
    S.op("dve", lambda e: e.tensor_tensor(out=o1, in0=a, in1=b, op=ALU.subtract), reads=[tk + "a", tk + "b"], writes=[outkey])
    S.op("dve", lambda e: e.tensor_tensor(out=a, in0=t1, in1=s, op=ALU.mult), reads=srck + [cskey], writes=[tk + "a"])
    S.op("dve", lambda e: e.tensor_tensor(out=b, in0=t2, in1=c, op=ALU.mult), reads=srck + [cskey], writes=[tk + "b"])
    S.op("dve", lambda e: e.tensor_tensor(out=o2, in0=a, in1=b, op=ALU.add), reads=[tk + "a", tk + "b"], writes=[outkey])


def emit_M1(nc, x_full, w_in_own, ln, g_qa, w_uq, g_kva, w_ukv, g_qn, g_kn, dg_qn, dg_kn, cs_tab, ident_d,
            QT, KT, V, DQT, DKT, DV, U, HG, ntok=SEQ):
    PFX = "m1_" + TAG[0]
    ntile = ntok // 128
    with contextlib.ExitStack() as st:
        T = lambda name, shape, dt: st.enter_context(nc.sbuf_tensor(PFX + name, shape, dt))
        Win = T("Win", [128, 8, NOWN], BF16)
        Wuq = T("Wuq", [128, 3, 192], BF16)
        Wukv = T("Wukv", [128, 1, 256], BF16)
        stage = [T("stg0", [128, 2048], F32), T("stg1", [128, 2048], F32)]
        gB = T("gB", [128, D], F32)
        gqa = T("gqa", [128, 384], F32)
        gkva = T("gkva", [128, 128], F32)
        gqn = T("gqn", [128, 96], F32)
        gkn = T("gkn", [128, 96], F32)
        dgq = T("dgq", [128, 32], F32)
        dgk = T("dgk", [128, 32], F32)
        idf = T("idf", [128, 128], F32)
        idb = T("idb", [128, 128], BF16)
        xt = [T("xt0", [128, D], F32), T("xt1", [128, D], F32)]
        cs = [T("cs0", [128, 128], F32), T("cs1", [128, 128], F32)]
        hb = T("hb", [128, D], BF16)
        hT = T("hT", [128, 8, 128], BF16)
        z = T("z", [128, NOWN], F32)
        junk = T("junk", [128, D], BF16)
        stat = T("stat", [128, 32], F32)
        tmp = T("tmp", [128, 384], BF16)
        rtmp = T("rtmp", [128, 128], F32)
        cqn = T("cqn", [128, 384], BF16)
        ckvn = T("ckvn", [128, 128], BF16)
        cT = T("cT", [128, 4, 128], BF16)
        qup = T("qup", [128, 192], F32)
        kcat = T("kcat", [128, 192], F32)
        qn = T("qn", [128, 192], F32)
        kn = T("kn", [128, 192], F32)
        qf = T("qf", [128, 256], BF16)
        kf = T("kf", [128, 256], BF16)
        dqn = T("dqn", [128, 128], F32)
        dkn = T("dkn", [128, 128], F32)
        dqf = T("dqf", [128, 128], BF16)
        dkf = T("dkf", [128, 128], BF16)
        vb = T("vb", [128, 128], BF16)
        dvb = T("dvb", [128, 128], BF16)
        QTs = T("QTs", [128, 2, 512], BF16)
        KTs = T("KTs", [128, 2, 512], BF16)
        DQTs = T("DQTs", [128, 512], BF16)
        DKTs = T("DKTs", [128, 512], BF16)
        pT = st.enter_context(nc.psum_tensor(PFX + "pT", [128, 8, 128], BF16))
        pZ = [st.enter_context(nc.psum_tensor(PFX + "pZ%d" % i, [128, 512], F32)) for i in range(4)]
        pU = st.enter_context(nc.psum_tensor(PFX + "pU", [128, 512], F32))
        pX = st.enter_context(nc.psum_tensor(PFX + "pX", [128, 8, 128], BF16))
        pY = st.enter_context(nc.psum_tensor(PFX + "pY", [128, 8, 128], BF16))
        S = Sched(nc)
        S.dma("act", idf[:], ident_d, "ld_c", writes=["idf"])
        S.op("dve", lambda e: e.tensor_copy(out=idb[:], in_=idf[:]), reads=["idf"], writes=["idb"])

        def bload(tile, vec, key):
            S.dma("act", tile[:], vec.rearrange("(o d) -> o d", o=1).partition_broadcast(128), "ld_c", writes=[key])
        bload(gB, ln, "gB_m1")
        bload(gqa, g_qa, "gqa")
        bload(gkva, g_kva, "gkva")
        bload(gqn, g_qn, "gqn")
        bload(gkn, g_kn, "gkn")
        bload(dgq, dg_qn, "dgq")
        bload(dgk, dg_kn, "dgk")
        load_w_bf16(S, nc, st, "in", w_in_own, 8, NOWN, Win, stage, nblk=212)
        load_w_bf16(S, nc, st, "uq", w_uq, 3, 192, Wuq, stage, nblk=192)
        load_w_bf16(S, nc, st, "ukv", w_ukv, 1, 256, Wukv, stage, nblk=256)

        if callable(x_full):
            class _XV:
                def __getitem__(self, t):
                    return x_full(t)
            xv = _XV()
        else:
            xv = x_full.rearrange("(t p) d -> t p d", p=128)
        csv = cs_tab.rearrange("(t p) d -> t p d", p=128)
        banks = [(0, 512), (512, 1024), (1024, 1536), (1536, NOWN)]
        S.dma("sp", xt[0][:], xv[0], "ld_x0", writes=["xt0"])
        S.dma("sp", cs[0][:], csv[0], "ld_cs0", writes=["cs0"])
        for t in range(ntile):
            sl = t % 2
            xk = "xt%d" % sl
            ck = "cs%d" % sl
            if t + 1 < ntile:
                S.dma("sp", xt[1 - sl][:], xv[t + 1], "ld_x%d" % (1 - sl), writes=["xt%d" % (1 - sl)])
                S.dma("sp", cs[1 - sl][:], csv[t + 1], "ld_cs%d" % (1 - sl), writes=["cs%d" % (1 - sl)])
            rmsnorm_tile(S, "m1", xt[sl][:], xk, gB, hb[:], "hb", junk, stat, 31)
            for c in range(8):
                S.op("pe", lambda e, c=c: e.transpose(out=pT[:, c, :], in_=hb[:, c * 128:(c + 1) * 128], identity=idb[:]),
                     reads=["hb", "idb"], writes=["pT"] if c in (0, 7) else [])
            S.op("dve", lambda e: e.tensor_copy(out=hT[:], in_=pT[:]), reads=["pT"], writes=["hT"])
            for bi, (c0, c1) in enumerate(banks):
                for c in range(8):
                    S.op("pe", lambda e, c=c, bi=bi, c0=c0, c1=c1: e.matmul(out=pZ[bi][:, 0:c1 - c0], lhsT=hT[:, c, :], rhs=Win[:, c, c0:c1],
                                                                          start=(c == 0), stop=(c == 7)),
                         reads=["hT", "w_in"], writes=["pZ%d" % bi] if c in (0, 7) else [])
                eng = "act" if bi % 2 == 0 else "dve"
                if eng == "act":
                    S.op("act", lambda e, bi=bi, c0=c0, c1=c1: e.activation(out=z[:, c0:c1], in_=pZ[bi][:, 0:c1 - c0], func=AF.Copy),
                         reads=["pZ%d" % bi], writes=["z%d" % bi])
                else:
                    S.op("dve", lambda e, bi=bi, c0=c0, c1=c1: e.tensor_copy(out=z[:, c0:c1], in_=pZ[bi][:, 0:c1 - c0]),
                         reads=["pZ%d" % bi], writes=["z%d" % bi])
            if LIM < 1:
                continue
            S.dma("pool", U[t * 128:(t + 1) * 128, :], z[:, 544:672], "st_u", reads=["z1"], final=True)
            S.dma("pool", HG[t * 128:(t + 1) * 128, :], z[:, 1056:1696], "st_hg", reads=["z2", "z3"], final=True)
            if LIM < 2:
                continue
            rms_groups(S, "cq", z[:, 0:384], "z0", 1, 384, gqa[:], "gqa", cqn[:], "cqn", tmp, stat, 0)
            if SUB < 2:
                continue
            rms_groups(S, "ckv", z[:, 384:512], "z0", 1, 128, gkva[:], "gkva", ckvn[:], "ckvn", tmp, stat, 1)
            for c in range(3):
                S.op("pe", lambda e, c=c: e.transpose(out=pX[:, c, :], in_=cqn[:, c * 128:(c + 1) * 128], identity=idb[:]),
                     reads=["cqn", "idb"], writes=["pX"])
            S.op("pe", lambda e: e.transpose(out=pX[:, 3, :], in_=ckvn[:], identity=idb[:]), reads=["ckvn", "idb"], writes=["pX"])
            S.op("dve", lambda e: e.tensor_copy(out=cT[:], in_=pX[:, 0:4, :]), reads=["pX"], writes=["cT"])
            if SUB < 3:
                continue
            for c in range(3):
                S.op("pe", lambda e, c=c: e.matmul(out=pU[:, 0:192], lhsT=cT[:, c, :], rhs=Wuq[:, c, :], start=(c == 0), stop=(c == 2)),
                     reads=["cT", "w_uq"], writes=["pUq"])
            S.op("pe", lambda e: e.matmul(out=pU[:, 192:448], lhsT=cT[:, 3, :], rhs=Wukv[:, 0, :], start=True, stop=True),
                 reads=["cT", "w_ukv"], writes=["pUkv"])
            S.op("act", lambda e: e.activation(out=qup[:], in_=pU[:, 0:192], func=AF.Copy), reads=["pUq"], writes=["qup"])
            if SUB < 4:
                continue
            kv = pU[:, 192:448].rearrange("p (h d) -> p h d", h=2)
            kc3 = kcat[:].rearrange("p (h d) -> p h d", h=2)
            for h in range(2):
                S.op("act", lambda e, h=h: e.activation(out=kcat[:, h * 96:h * 96 + 64], in_=pU[:, 192 + h * 128:192 + h * 128 + 64], func=AF.Copy),
                     reads=["pUkv"], writes=["kcat_a"])
                S.op("act", lambda e, h=h: e.activation(out=vb[:, h * 64:(h + 1) * 64], in_=pU[:, 192 + h * 128 + 64:192 + (h + 1) * 128], func=AF.Copy),
                     reads=["pUkv"], writes=["vb"])
            if SUB < 5:
                continue
            for h in range(2):
                S.op("pool", lambda e, h=h: e.tensor_copy(out=kcat[:, h * 96 + 64:h * 96 + 96], in_=z[:, 512:544]),
                     reads=["z1"], writes=["kcat_b"])
            if LIM < 3:
                continue
            S.dma("pool", V[t * 128:(t + 1) * 128, :], vb[:], "st_v", reads=["vb"], final=True)
            rms_groups(S, "q", qup[:], "qup", 2, 96, gqn[:], "gqn", qn[:], "qn", tmp, stat, 2)
            rms_groups(S, "k", kcat[:], ["kcat_a", "kcat_b"], 2, 96, gkn[:], "gkn", kn[:], "kn", tmp, stat, 4)
            qn3 = qn[:].rearrange("p (h d) -> p h d", h=2)
            kn3 = kn[:].rearrange("p (h d) -> p h d", h=2)
            qf3 = qf[:].rearrange("p (h d) -> p h d", h=2)
            kf3 = kf[:].rearrange("p (h d) -> p h d", h=2)
            if t == 0:
                S.op("pool", lambda e: e.memset(qf[:], 0.0), writes=["qf_a", "qf_b"])
                S.op("pool", lambda e: e.memset(kf[:], 0.0), writes=["kf_a", "kf_b"])
            S.op("pool", lambda e: e.tensor_copy(out=qf3[:, :, 0:64], in_=qn3[:, :, 0:64]), reads=["qn"], writes=["qf_a"])
            S.op("pool", lambda e: e.tensor_copy(out=kf3[:, :, 0:64], in_=kn3[:, :, 0:64]), reads=["kn"], writes=["kf_a"])
            rope_groups(S, "q", qn[:], "qn", 2, 96, 64, cs[sl], ck, qf[:], "qf_b", rtmp, dout=128)
            rope_groups(S, "k", kn[:], "kn", 2, 96, 64, cs[sl], ck, kf[:], "kf_b", rtmp, dout=128)
            if LIM < 4:
                continue
            rms_groups(S, "dq", z[:, 672:800], "z1", 4, 32, dgq[:], "dgq", dqn[:], "dqn", tmp, stat, 8)
            rms_groups(S, "dk", z[:, 800:928], "z1", 4, 32, dgk[:], "dgk", dkn[:], "dkn", tmp, stat, 12)
            rope_groups(S, "dq", dqn[:], "dqn", 4, 32, 0, cs[sl], ck, dqf[:], "dqf", rtmp)
            rope_groups(S, "dk", dkn[:], "dkn", 4, 32, 0, cs[sl], ck, dkf[:], "dkf", rtmp)
            S.op("pool", lambda e: e.tensor_copy(out=dvb[:], in_=z[:, 928:1056]), reads=["z1", "z2"], writes=["dvb"])
            S.dma("pool", DV[t * 128:(t + 1) * 128, :], dvb[:], "st_dv", reads=["dvb"], final=True)
            if LIM < 5:
                continue
            tq = t % 4
            for h in range(2):
                S.op("pe", lambda e, h=h: e.transpose(out=pY[:, h, :], in_=qf[:, h * 128:(h + 1) * 128], identity=idb[:]),
                     reads=["qf_a", "qf_b", "idb"], writes=["pY"])
                S.op("pe", lambda e, h=h: e.transpose(out=pY[:, 2 + h, :], in_=kf[:, h * 128:(h + 1) * 128], identity=idb[:]),
                     reads=["kf_a", "kf_b", "idb"], writes=["pY"])
            S.op("pe", lambda e: e.transpose(out=pY[:, 4, :], in_=dqf[:], identity=idb[:]), reads=["dqf", "idb"], writes=["pY"])
            S.op("pe", lambda e: e.transpose(out=pY[:, 5, :], in_=dkf[:], identity=idb[:]), reads=["dkf", "idb"], writes=["pY"])
            if SUB < 6:
                continue
            for h in range(2):
                S.op("dve", lambda e, tq=tq, h=h: e.tensor_copy(out=QTs[:, h, tq * 128:(tq + 1) * 128], in_=pY[:, h, :]),
                     reads=["pY"], writes=["QTs"])
                S.op("dve", lambda e, tq=tq, h=h: e.tensor_copy(out=KTs[:, h, tq * 128:(tq + 1) * 128], in_=pY[:, 2 + h, :]),
                     reads=["pY"], writes=["KTs"])
            S.op("dve", lambda e, tq=tq: e.tensor_copy(out=DQTs[:, tq * 128:(tq + 1) * 128], in_=pY[:, 4, :]),
                 reads=["pY"], writes=["DQTs"])
            S.op("dve", lambda e, tq=tq: e.tensor_copy(out=DKTs[:, tq * 128:(tq + 1) * 128], in_=pY[:, 5, :]),
                 reads=["pY"], writes=["DKTs"])
            if SUB < 7:
                continue
            if tq == 3:
                t0 = (t - 3) * 128
                for h in range(2):
                    S.dma("sp", QT[h, :, t0:t0 + 512], QTs[:, h, :], "st_q", reads=["QTs"], final=True)
                    S.dma("sp", KT[h, :, t0:t0 + 512], KTs[:, h, :], "st_k", reads=["KTs"], final=True)
                S.dma("sp", DQT.rearrange("h r t -> (h r) t")[:, t0:t0 + 512], DQTs[:], "st_dq", reads=["DQTs"], final=True)
                S.dma("sp", DKT.rearrange("h r t -> (h r) t")[:, t0:t0 + 512], DKTs[:], "st_dk", reads=["DKTs"], final=True)
        S.emit()

import contextlib
import math
import numpy as np
import concourse.bass as bass
import concourse.mybir as mybir

AX = mybir.AxisListType
QB = 512


def _attn_map(S, tag, KT_sb, QT_sb, kkey, qkey, prow0, prows, Vaug, vkey, PT, pS, pO, pOkey, scale, ntok, qb, ctr):
    nch = ntok // 128
    ng = nch // 2
    sls = [(ctr[0] + g) % 2 for g in range(ng)]
    ctr[0] += ng

    def scores(g):
        sl = sls[g]
        for i in range(2):
            kc = g * 2 + i
            S.op("pe", lambda e, kc=kc, i=i, sl=sl: e.matmul(out=pS[sl][:, i * QB:(i + 1) * QB],
                                                             lhsT=KT_sb[prow0:prow0 + prows, kc * 128:(kc + 1) * 128],
                                                             rhs=QT_sb[prow0:prow0 + prows, qb * QB:(qb + 1) * QB], start=True, stop=True),
                 reads=[kkey, qkey], writes=["pS%d_%d" % (sl, i)])

    scores(0)
    for g in range(ng):
        sl = sls[g]
        S.op("act", lambda e, sl=sl: e.activation(out=PT[sl][:], in_=pS[sl][:], func=AF.Exp, scale=scale),
             reads=["pS%d_0" % sl, "pS%d_1" % sl], writes=["PT%d" % sl])
        if g + 1 < ng:
            scores(g + 1)
        for i in range(2):
            kc = g * 2 + i
            S.op("pe", lambda e, kc=kc, i=i, sl=sl: e.matmul(out=pO[0:65, :], lhsT=Vaug[:, kc, :], rhs=PT[sl][:, i * QB:(i + 1) * QB],
                                                             start=(kc == 0), stop=(kc == nch - 1)),
                 reads=["PT%d" % sl, vkey], writes=[pOkey] if kc in (0, nch - 1) else [])


def emit_attn(nc, QT, KT, V, DQT, DKT, DV, lq1, lk1, lq2, lk2, g_sub, oT_out, layer_idx, ntok):
    PFX = "at_" + TAG[0]
    nch = ntok // 128
    nqb = ntok // QB
    lambda_init = 0.8 - 0.6 * math.exp(-0.3 * layer_idx)
    with contextlib.ExitStack() as st:
        T = lambda name, shape, dt: st.enter_context(nc.sbuf_tensor(PFX + name, shape, dt))
        Qs = T("Qs", [128, ntok], BF16)
        Ks = T("Ks", [128, ntok], BF16)
        Va = T("Va", [128, nch, 65], BF16)
        PT = [T("PT0", [128, 2 * QB], BF16), T("PT1", [128, 2 * QB], BF16)]
        O1 = T("O1", [65, QB], F32)
        O2 = T("O2", [65, QB], F32)
        rl = T("rl", [64, QB], F32)
        o1 = T("o1", [64, QB], F32)
        o2 = T("o2", [64, QB], F32)
        sq = T("sq", [64, QB], F32)
        ob = T("ob", [64, QB], BF16)
        sel = T("sel", [65, 64], F32)
        ones = T("ones", [64, 64], F32)
        lt = T("lt", [64, 4, 32], F32)
        lw = T("lw", [64, 8], F32)
        gs = T("gs", [64, 2], F32)
        pS = [st.enter_context(nc.psum_tensor(PFX + "pS%d" % i, [128, 2 * QB], F32)) for i in range(2)]
        pO = [st.enter_context(nc.psum_tensor(PFX + "pO%d" % i, [128, QB], F32)) for i in range(2)]
        pL = st.enter_context(nc.psum_tensor(PFX + "pL", [128, QB], F32))
        S = Sched(nc)
        S.op("pool", lambda e: e.memset(sel[:], 0.0), writes=["sel"])
        S.op("pool", lambda e: e.memset(sel[64:65, :], 1.0), reads=[], writes=["sel"])
        S.op("pool", lambda e: e.memset(ones[:], 1.0), writes=["ones"])
        for i, v in enumerate((lq1, lk1, lq2, lk2)):
            S.dma("act", lt[:, i, :], v.rearrange("(o d) -> o d", o=1).partition_broadcast(64), "ld_c", writes=["lt%d" % i])
        S.dma("act", gs[:, 0:1], g_sub.rearrange("(d o) -> d o", o=1), "ld_c", writes=["gs0"])
        S.op("dve", lambda e: e.tensor_tensor(out=lt[:, 0, :], in0=lt[:, 0, :], in1=lt[:, 1, :], op=ALU.mult), reads=["lt0", "lt1"], writes=["lt0"])
        S.op("dve", lambda e: e.tensor_tensor(out=lt[:, 2, :], in0=lt[:, 2, :], in1=lt[:, 3, :], op=ALU.mult), reads=["lt2", "lt3"], writes=["lt2"])
        S.op("dve", lambda e: e.tensor_reduce(out=lw[:, 0:1], in_=lt[:, 0, :], axis=AX.X, op=ALU.add), reads=["lt0"], writes=["lw0"])
        S.op("dve", lambda e: e.tensor_reduce(out=lw[:, 1:2], in_=lt[:, 2, :], axis=AX.X, op=ALU.add), reads=["lt2"], writes=["lw1"])
        S.op("act", lambda e: e.activation(out=lw[:, 2:4], in_=lw[:, 0:2], func=AF.Exp), reads=["lw0", "lw1"], writes=["lw23"])
        S.op("dve", lambda e: e.tensor_tensor(out=lw[:, 4:5], in0=lw[:, 3:4], in1=lw[:, 2:3], op=ALU.subtract), reads=["lw23"], writes=["lw4"])
        S.op("dve", lambda e: e.tensor_scalar(out=lw[:, 4:5], in0=lw[:, 4:5], scalar1=-lambda_init, scalar2=None, op0=ALU.add),
             reads=["lw4"], writes=["lw4"])
        S.op("dve", lambda e: e.tensor_scalar(out=gs[:, 1:2], in0=gs[:, 0:1], scalar1=(1.0 - lambda_init), scalar2=None, op0=ALU.mult),
             reads=["gs0"], writes=["gs1"])
        ctr = [0]
        ob_n = [0]

        def finalize_l(src, srckey, dst, dstkey):
            S.op("pe", lambda e: e.matmul(out=pL[0:64, :], lhsT=sel[:], rhs=src[:], start=True, stop=True), reads=["sel", srckey], writes=["pL"])
            S.op("dve", lambda e: e.reciprocal(out=rl[:], in_=pL[0:64, :]), reads=["pL"], writes=["rl"])
            S.op("dve", lambda e: e.tensor_tensor(out=dst[:], in0=src[0:64, :], in1=rl[:], op=ALU.mult), reads=[srckey, "rl"], writes=[dstkey])

        for h in range(2):
            S.dma("sp", Qs[:], QT[h], "ld_q", writes=["Qs"])
            S.dma("sp", Ks[:], KT[h], "ld_k", writes=["Ks"])
            S.op("pool", lambda e: e.memset(Va[:], 1.0), writes=["Va"])
            _vv = V[:, h * 64:(h + 1) * 64].rearrange("(c p) d -> p c d", p=128)
            for _i in range(0, nch, 8):
                S.dma("act", Va[:, _i:_i + 8, 0:64], _vv[:, _i:_i + 8, :], "ld_v", writes=["Va"])
            for qb in range(nqb):
                b = qb % 2
                _attn_map(S, "m", Ks, Qs, "Ks", "Qs", 0, 128, Va, "Va", PT, pS, pO[b], "pO%d" % b, 96 ** -0.5, ntok, qb, ctr)
                S.op("dve", lambda e, b=b: e.tensor_copy(out=O1[:], in_=pO[b][0:65, :]), reads=["pO%d" % b], writes=["O1"])
                finalize_l(O1, "O1", o1, "o1")
                S.op("pool", lambda e: e.tensor_copy(out=ob[:], in_=o1[:]), reads=["o1"], writes=["ob"])
                S.dma("sp", oT_out[h * 64:(h + 1) * 64, qb * QB:(qb + 1) * QB], ob[:], "st_o", reads=["ob"], final=True)
        for h in range(2):
            S.dma("sp", Qs[0:64, :], DQT[h], "ld_q", writes=["Qs"])
            S.dma("sp", Ks[0:64, :], DKT[h], "ld_k", writes=["Ks"])
            S.op("pool", lambda e: e.memset(Va[:], 1.0), writes=["Va"])
            _vv = DV[:, h * 64:(h + 1) * 64].rearrange("(c p) d -> p c d", p=128)
            for _i in range(0, nch, 8):
                S.dma("act", Va[:, _i:_i + 8, 0:64], _vv[:, _i:_i + 8, :], "ld_v", writes=["Va"])
            for qb in range(nqb):
                _attn_map(S, "d1", Ks, Qs, "Ks", "Qs", 0, 32, Va, "Va", PT, pS, pO[0], "pO0", 32 ** -0.5, ntok, qb, ctr)
                _attn_map(S, "d2", Ks, Qs, "Ks", "Qs", 32, 32, Va, "Va", PT, pS, pO[1], "pO1", 32 ** -0.5, ntok, qb, ctr)
                S.op("dve", lambda e: e.tensor_copy(out=O1[:], in_=pO[0][0:65, :]), reads=["pO0"], writes=["O1"])
                S.op("dve", lambda e: e.tensor_copy(out=O2[:], in_=pO[1][0:65, :]), reads=["pO1"], writes=["O2"])
                finalize_l(O1, "O1", o1, "o1")
                finalize_l(O2, "O2", o2, "o2")
                S.op("dve", lambda e: e.scalar_tensor_tensor(out=o1[:], in0=o2[:], scalar=lw[:, 4:5], in1=o1[:], op0=ALU.mult, op1=ALU.add),
                     reads=["o1", "o2", "lw4"], writes=["o1"])
                S.op("pool", lambda e: e.tensor_tensor(out=sq[:], in0=o1[:], in1=o1[:], op=ALU.mult), reads=["o1"], writes=["sq"])
                S.op("pe", lambda e: e.matmul(out=pL[0:64, :], lhsT=ones[:], rhs=sq[:], start=True, stop=True), reads=["ones", "sq"], writes=["pL"])
                S.op("dve", lambda e: e.tensor_scalar(out=sq[:], in0=pL[0:64, :], scalar1=1.0 / 64, scalar2=EPS, op0=ALU.mult, op1=ALU.add),
                     reads=["pL"], writes=["sq"])
                S.op("act", lambda e: e.activation(out=sq[:], in_=sq[:], func=AF.Ln), reads=["sq"], writes=["sq"])
                S.op("act", lambda e: e.activation(out=sq[:], in_=sq[:], func=AF.Exp, scale=-0.5), reads=["sq"], writes=["sq"])
                S.op("dve", lambda e: e.scalar_tensor_tensor(out=ob[:], in0=o1[:], scalar=gs[:, 1:2], in1=sq[:], op0=ALU.mult, op1=ALU.mult),
                     reads=["o1", "sq", "gs1"], writes=["ob"])
                S.dma("sp", oT_out[256 + h * 64:256 + (h + 1) * 64, qb * QB:(qb + 1) * QB], ob[:], "st_o", reads=["ob"], final=True)
        S.emit()

import contextlib
import math
import numpy as np
import concourse.bass as bass
import concourse.mybir as mybir

SEQ = 8192
C_C128, C_S128N, C_C128N, C_TC, C_TS, C_CS64, C_C64S, C_S64S = 0, 128, 256, 384, 448, 512, 768, 832
NFC = 896


def fnet_consts():
    f = np.zeros((128, NFC), np.float64)
    a = np.arange(128)
    ang = 2 * np.pi * np.outer(a, a) / 128.0
    f[:, C_C128:C_C128 + 128] = np.cos(ang)
    f[:, C_S128N:C_S128N + 128] = -np.sin(ang)
    f[:, C_C128N:C_C128N + 128] = -np.cos(ang)
    tw = 2 * np.pi * np.outer(a, np.arange(64)) / 8192.0
    f[:, C_TC:C_TC + 64] = np.cos(tw)
    f[:, C_TS:C_TS + 64] = np.sin(tw)
    c = np.arange(64)
    a64 = 2 * np.pi * np.outer(c, c) / 64.0
    for g in range(2):
        f[g * 64:(g + 1) * 64, C_CS64 + g * 64:C_CS64 + (g + 1) * 64] = np.cos(a64)
        f[g * 64:(g + 1) * 64, C_CS64 + 128 + g * 64:C_CS64 + 128 + (g + 1) * 64] = np.sin(a64)
    sc = 1.0 / math.sqrt(8192.0 * 64.0)
    f[0:64, C_C64S:C_C64S + 64] = np.cos(a64) * sc
    f[0:64, C_S64S:C_S64S + 64] = np.sin(a64) * sc
    return f.astype(np.float32)


def emit_fnet(nc, U, fconst, ident_d, UCS, XPd, Yd, oT_out):
    PFX = "fn_" + TAG[0]
    with contextlib.ExitStack() as st:
        T = lambda name, shape, dt: st.enter_context(nc.sbuf_tensor(PFX + name, shape, dt))
        fc = T("fc", [128, NFC], F32)
        fb = T("fb", [128, NFC], BF16)
        idf = T("idf", [128, 128], F32)
        idb = T("idb", [128, 128], BF16)
        ut = [T("ut0", [128, 4, 128], F32), T("ut1", [128, 4, 128], F32)]
        ub = T("ub", [128, 4, 128], BF16)
        uT = T("uT", [128, 4, 128], BF16)
        ucs = [T("ucs0", [128, 4, 256], BF16), T("ucs1", [128, 4, 256], BF16)]
        WB = T("WB", [128, 64, 256], BF16)
        XP = T("XP", [128, 64, 256], BF16)
        XQ = T("XQ", [64, 128, 256], BF16)
        YS = T("YS", [64, 128, 128], BF16)
        t1 = [T("t1a", [128, 128], F32), T("t1b", [128, 128], F32)]
        t2 = [T("t2a", [128, 128], F32), T("t2b", [128, 128], F32)]
        yt = [T("yt0", [128, 4, 128], BF16), T("yt1", [128, 4, 128], BF16)]
        oS = [T("oS0", [128, 512], BF16), T("oS1", [128, 512], BF16)]
        pT = st.enter_context(nc.psum_tensor(PFX + "pT", [128, 8, 128], BF16))
        pA = [st.enter_context(nc.psum_tensor(PFX + "pA%d" % i, [128, 512], F32)) for i in range(2)]
        pR = [st.enter_context(nc.psum_tensor(PFX + "pR%d" % i, [128, 512], F32)) for i in range(2)]
        pI = [st.enter_context(nc.psum_tensor(PFX + "pI%d" % i, [128, 512], F32)) for i in range(2)]
        S = Sched(nc)
        S.dma("act", idf[:], ident_d, "ld_c", writes=["idf"])
        S.op("dve", lambda e: e.tensor_copy(out=idb[:], in_=idf[:]), reads=["idf"], writes=["idb"])
        S.dma("act", fc[:], fconst, "ld_c", writes=["fc"])
        S.op("dve", lambda e: e.tensor_copy(out=fb[:], in_=fc[:]), reads=["fc"], writes=["fb"])
        Uv = U.rearrange("(n j p) c -> n p j c", p=128, j=4)
        UCSv = UCS.rearrange("(n j p) c -> n p j c", p=128, j=4)
        for n in range(SEQ // 512):
            sl = n % 2
            S.dma("sp", ut[sl][:], Uv[n], "ld_u%d" % sl, writes=["ut%d" % sl])
            S.op("pool", lambda e, sl=sl: e.tensor_copy(out=ub[:], in_=ut[sl][:]), reads=["ut%d" % sl], writes=["ub"])
            for j in range(4):
                S.op("pe", lambda e, j=j: e.transpose(out=pT[:, j, :], in_=ub[:, j, :], identity=idb[:]), reads=["ub", "idb"], writes=["pT"])
            S.op("dve", lambda e: e.tensor_copy(out=uT[:], in_=pT[:, 0:4, :]), reads=["pT"], writes=["uT"])
            for j in range(4):
                b = j % 2
                S.op("pe", lambda e, j=j, b=b: e.matmul(out=pA[b][:, 0:256], lhsT=uT[:, j, :], rhs=fb[:, C_CS64:C_CS64 + 256], start=True, stop=True),
                     reads=["uT", "fb"], writes=["pA%d" % b])
                S.op("dve", lambda e, j=j, b=b, sl=sl: e.tensor_copy(out=ucs[sl][:, j, :], in_=pA[b][:, 0:256]),
                     reads=["pA%d" % b], writes=["ucs%d" % sl])
            S.dma("sp", UCSv[n], ucs[sl][:], "st_ucs", reads=["ucs%d" % sl], final=True)
        st_ucs_done = S.chan_ops["st_ucs"][-1]
        S.dma("sp", WB[:], UCS.rearrange("(a b) c -> a b c", b=64), "ld_wb", writes=["WB"], deps=list(S.chan_ops["st_ucs"]))
        k = 0
        for g in range(16):
            b = g % 2
            for a in range(4):
                s2m = g * 4 + a
                oR = pR[b][:, a * 128:(a + 1) * 128]
                oI = pI[b][:, a * 128:(a + 1) * 128]
                rc = WB[:, s2m, 0:128]
                rs = WB[:, s2m, 128:256]
                S.op("pe", lambda e, rc=rc, oR=oR: e.matmul(out=oR, lhsT=fb[:, C_C128:C_C128 + 128], rhs=rc, start=True, stop=False),
                     reads=["WB", "fb"], writes=["pR%d" % b])
                S.op("pe", lambda e, rs=rs, oR=oR: e.matmul(out=oR, lhsT=fb[:, C_S128N:C_S128N + 128], rhs=rs, start=False, stop=True),
                     reads=["WB", "fb"], writes=["pR%d" % b])
                S.op("pe", lambda e, rc=rc, oI=oI: e.matmul(out=oI, lhsT=fb[:, C_S128N:C_S128N + 128], rhs=rc, start=True, stop=False),
                     reads=["WB", "fb"], writes=["pI%d" % b])
                S.op("pe", lambda e, rs=rs, oI=oI: e.matmul(out=oI, lhsT=fb[:, C_C128N:C_C128N + 128], rhs=rs, start=False, stop=True),
                     reads=["WB", "fb"], writes=["pI%d" % b])
            for a in range(4):
                s2 = g * 4 + a
                q = k % 2
                k += 1
                xr = pR[b][:, a * 128:(a + 1) * 128]
                xi = pI[b][:, a * 128:(a + 1) * 128]
                S.op("dve", lambda e, xi=xi, q=q, s2=s2: e.tensor_scalar(out=t1[q][:], in0=xi, scalar1=fc[:, C_TS + s2:C_TS + s2 + 1], scalar2=None, op0=ALU.mult),
                     reads=["pI%d" % b, "fc"], writes=["t1%d" % q])
                S.op("dve", lambda e, xr=xr, q=q, s2=s2: e.scalar_tensor_tensor(out=XP[:, s2, 0:128], in0=xr, scalar=fc[:, C_TC + s2:C_TC + s2 + 1],
                                                                                in1=t1[q][:], op0=ALU.mult, op1=ALU.add),
                     reads=["pR%d" % b, "t1%d" % q, "fc"], writes=["XP"])
                S.op("dve", lambda e, xr=xr, q=q, s2=s2: e.tensor_scalar(out=t2[q][:], in0=xr, scalar1=fc[:, C_TS + s2:C_TS + s2 + 1], scalar2=None, op0=ALU.mult),
                     reads=["pR%d" % b, "fc"], writes=["t2%d" % q])
                S.op("dve", lambda e, xi=xi, q=q, s2=s2: e.scalar_tensor_tensor(out=XP[:, s2, 128:256], in0=xi, scalar=fc[:, C_TC + s2:C_TC + s2 + 1],
                                                                                in1=t2[q][:], op0=ALU.mult, op1=ALU.subtract),
                     reads=["pI%d" % b, "t2%d" % q, "fc"], writes=["XP"])
        S.dma("sp", XPd, XP[:], "st_xp", reads=["XP"], final=True)
        XPv = XPd.rearrange("a b c -> b a c")
        for i in range(8):
            S.dma("sp" if i % 2 == 0 else "act", XQ[:, i * 16:(i + 1) * 16, :], XPv[:, i * 16:(i + 1) * 16, :], "ld_xq%d" % (i % 2), writes=["XQ"],
                  deps=[S.chan_ops["st_xp"][-1]])
        for g in range(32):
            b = g % 2
            for a in range(4):
                s1m = g * 4 + a
                o = pA[b][0:64, a * 128:(a + 1) * 128]
                S.op("pe", lambda e, s1m=s1m, o=o: e.matmul(out=o, lhsT=fb[0:64, C_C64S:C_C64S + 64], rhs=XQ[:, s1m, 0:128], start=True, stop=False),
                     reads=["XQ", "fb"], writes=["pA%d" % b])
                S.op("pe", lambda e, s1m=s1m, o=o: e.matmul(out=o, lhsT=fb[0:64, C_S64S:C_S64S + 64], rhs=XQ[:, s1m, 128:256], start=False, stop=True),
                     reads=["XQ", "fb"], writes=["pA%d" % b])
            S.op("dve", lambda e, g=g, b=b: e.tensor_copy(out=YS[:, g * 4:(g + 1) * 4, :].rearrange("p a c -> p (a c)"), in_=pA[b][0:64, :]),
                 reads=["pA%d" % b], writes=["YS"])
        S.dma("sp", Yd.rearrange("(b a) c -> b a c", a=128), YS[:], "st_y", reads=["YS"], final=True)
        Yv = Yd.rearrange("(n j p) c -> n p j c", p=128, j=4)
        for n in range(SEQ // 512):
            sl = n % 2
            S.dma("act", yt[sl][:], Yv[n], "ld_y%d" % sl, writes=["yt%d" % sl], deps=[S.chan_ops["st_y"][-1]])
            for j in range(4):
                S.op("pe", lambda e, j=j, sl=sl: e.transpose(out=pT[:, 4 + j, :], in_=yt[sl][:, j, :], identity=idb[:]),
                     reads=["yt%d" % sl, "idb"], writes=["pTb"])
            S.op("dve", lambda e, sl=sl: e.tensor_copy(out=oS[sl][:].rearrange("p (j t) -> p j t", j=4), in_=pT[:, 4:8, :]),
                 reads=["pTb"], writes=["oS%d" % sl])
            S.dma("sp", oT_out[128:256, n * 512:(n + 1) * 512], oS[sl][:], "st_o", reads=["oS%d" % sl], final=True)
        S.emit()

import contextlib
import math
import numpy as np
import concourse.bass as bass
import concourse.mybir as mybir

H_MTF, H_MTB, H_INDF, H_INDB, H_MASKF, H_MASKB = 0, 128, 256, 260, 264, 392
NHC = 520


def hgrn_consts():
    f = np.zeros((128, NHC), np.float32)
    s = np.arange(128)[:, None]
    t = np.arange(128)[None, :]
    same = (s // 64) == (t // 64)
    ls, lt = s % 64, t % 64
    f[:, H_MTF:H_MTF + 128] = same * ((ls <= lt).astype(np.float32) - (ls <= 31).astype(np.float32))
    f[:, H_MTB:H_MTB + 128] = same * ((ls >= lt).astype(np.float32) - (ls >= 32).astype(np.float32))
    sv = np.arange(128)
    for a in range(2):
        inch = (sv // 64) == a
        f[:, H_INDF + 2 * a] = inch & ((sv % 64) <= 31)
        f[:, H_INDF + 2 * a + 1] = inch & ((sv % 64) > 31)
        f[:, H_INDB + 2 * a] = inch & ((sv % 64) >= 32)
        f[:, H_INDB + 2 * a + 1] = inch & ((sv % 64) < 32)
    s6 = np.arange(64)[:, None]
    t6 = np.arange(64)[None, :]
    f[0:64, H_MASKF:H_MASKF + 128] = np.tile((s6 <= t6).astype(np.float32), (1, 2))
    f[0:64, H_MASKB:H_MASKB + 128] = np.tile((s6 >= t6).astype(np.float32), (1, 2))
    return f


def emit_hgrn(nc, HG, lbl, g_on, hconst, ident_d, oT_out, layer_idx, ntok):
    PFX = "hg_" + TAG[0]
    ntile = ntok // 128
    n64 = ntok // 64
    with contextlib.ExitStack() as st:
        T = lambda name, shape, dt: st.enter_context(nc.sbuf_tensor(PFX + name, shape, dt))
        hc = T("hc", [128, NHC], F32)
        idf = T("idf", [128, 128], F32)
        idb = T("idb", [128, 128], BF16)
        la = T("la", [128, 2, 128], F32)
        lw = T("lw", [128, 6, 128], F32)
        lb = T("lb", [128, 128], F32)
        oml = T("oml", [128, 128], F32)
        gon = T("gon", [64, 64], F32)
        LF = T("LF", [128, ntile, 128], F32)
        KK = T("KK", [128, ntile, 128], BF16)
        U1 = T("U1", [128, n64], F32)
        U2 = T("U2", [128, n64], F32)
        G = T("G", [128, n64], F32)
        OA = T("OA", [64, n64, 128], F32)
        zt = [T("zt0", [128, 4, 128], F32), T("zt1", [128, 4, 128], F32)]
        w1 = T("w1", [128, 4, 128], F32)
        w2 = T("w2", [128, 4, 128], F32)
        lb4 = T("lb4", [128, 4, 128], F32)
        oml4 = T("oml4", [128, 4, 128], F32)
        qv = [T("qv0", [128, 256], F32), T("qv1", [128, 256], F32)]
        v64 = [T("v64a", [64, 2, 128], F32), T("v64b", [64, 2, 128], F32)]
        Vb2 = [T("Vb0", [128, 128], BF16), T("Vb1", [128, 128], BF16)]
        V62 = [T("V60", [64, 2, 128], BF16), T("V61", [64, 2, 128], BF16)]
        EA = T("EA", [128, 128], F32)
        EnA = T("EnA", [128, 128], F32)
        Qt = T("Qt", [128, 128], BF16)
        Kt2 = [T("Kt0", [128, 128], BF16), T("Kt1", [128, 128], BF16)]
        QKT2 = [T("QKT0", [128, 2, 128], BF16), T("QKT1", [128, 2, 128], BF16)]
        scT = T("scT", [64, 128], BF16)
        Tt = T("Tt", [128, 64], F32)
        Spf = T("Spf", [128, 64], F32)
        Spb = T("Spb", [128, 64], BF16)
        hgt = T("hgt", [64, 8, 128], F32)
        sg = T("sg", [64, 8, 128], F32)
        yy = T("yy", [64, 128], F32)
        yb = T("yb", [64, 128], BF16)
        junk = T("junk", [64, 64], F32)
        fst = T("fst", [64, 4], F32)
        oS = [T("oS0", [128, 512], BF16), T("oS1", [128, 512], BF16)]
        pu = st.enter_context(nc.psum_tensor(PFX + "pu", [128, 512], F32))
        pA = st.enter_context(nc.psum_tensor(PFX + "pA", [128, 512], F32))
        pT = st.enter_context(nc.psum_tensor(PFX + "pT", [128, 8, 128], BF16))
        pSc = st.enter_context(nc.psum_tensor(PFX + "pSc", [128, 512], F32))
        pO = st.enter_context(nc.psum_tensor(PFX + "pO", [128, 512], F32))
        pP = st.enter_context(nc.psum_tensor(PFX + "pP", [128, 512], F32))
        pF = st.enter_context(nc.psum_tensor(PFX + "pF", [128, 8, 128], BF16))
        S = Sched(nc)
        S.dma("act", idf[:], ident_d, "ld_c", writes=["idf"])
        S.op("dve", lambda e: e.tensor_copy(out=idb[:], in_=idf[:]), reads=["idf"], writes=["idb"])
        S.dma("act", hc[:], hconst, "ld_c", writes=["hc"])
        S.dma("act", gon[:], g_on.rearrange("(o d) -> o d", o=1).partition_broadcast(64), "ld_c", writes=["gon"])
        HGv = HG.rearrange("(t p) d -> t p d", p=128)
        HG6 = HG.rearrange("(t a p) d -> t p a d", p=64, a=2)

        for d in range(2):
            fwd = (d == 0)
            zc0 = 256 if fwd else 384
            MT = hc[:, H_MTF:H_MTF + 128] if fwd else hc[:, H_MTB:H_MTB + 128]
            IND = hc[:, H_INDF:H_INDF + 4] if fwd else hc[:, H_INDB:H_INDB + 4]
            MASK = hc[0:64, H_MASKF:H_MASKF + 128] if fwd else hc[0:64, H_MASKB:H_MASKB + 128]
            for l in range(2):
                S.dma("act", la[:, l, :], lbl[d, l].rearrange("(o c) -> o c", o=1).partition_broadcast(128), "ld_c", writes=["la%d" % l])
            S.op("act", lambda e: e.activation(out=lw[:, 0, :], in_=la[:, 0, :], func=AF.Exp), reads=["la0"], writes=["lw0"])
            S.op("act", lambda e: e.activation(out=lw[:, 1, :], in_=la[:, 1, :], func=AF.Exp), reads=["la1"], writes=["lw1"])
            S.op("dve", lambda e: e.tensor_tensor(out=lw[:, 2, :], in0=lw[:, 0, :], in1=lw[:, 1, :], op=ALU.add), reads=["lw0", "lw1"], writes=["lw2"])
            S.op("dve", lambda e: e.reciprocal(out=lw[:, 2, :], in_=lw[:, 2, :]), reads=["lw2"], writes=["lw2"])
            S.op("dve", lambda e: e.tensor_tensor(out=lw[:, 3, :], in0=lw[:, 0, :], in1=lw[:, 2, :], op=ALU.mult), reads=["lw0", "lw2"], writes=["lw3"])
            S.op("dve", lambda e: e.tensor_tensor(out=lw[:, 4, :], in0=lw[:, 1, :], in1=lw[:, 2, :], op=ALU.mult), reads=["lw1", "lw2"], writes=["lw4"])
            if layer_idx == 0:
                S.op("dve", lambda e: e.tensor_tensor(out=lb[:], in0=lw[:, 3, :], in1=lw[:, 3, :], op=ALU.subtract), reads=["lw3"], writes=["lb"])
            else:
                S.op("dve", lambda e: e.tensor_tensor(out=lw[:, 5, :], in0=lw[:, 3, :], in1=lw[:, 4, :], op=ALU.add), reads=["lw3", "lw4"], writes=["lw5"])
                S.op("dve", lambda e: e.tensor_tensor(out=lb[:], in0=lw[:, 5, :], in1=lw[:, 3, :], op=ALU.subtract), reads=["lw5", "lw3"], writes=["lb"])
            S.op("dve", lambda e: e.tensor_scalar(out=oml[:], in0=lb[:], scalar1=-1.0, scalar2=1.0, op0=ALU.mult, op1=ALU.add), reads=["lb"], writes=["oml"])
            for j in range(4):
                S.op("pool", lambda e, j=j: e.tensor_copy(out=lb4[:, j, :], in_=lb[:]), reads=["lb"], writes=["lb4"])
                S.op("pool", lambda e, j=j: e.tensor_copy(out=oml4[:, j, :], in_=oml[:]), reads=["oml"], writes=["oml4"])
            HG4 = HG.rearrange("(g j p) d -> g p j d", p=128, j=4)
            f2 = lambda ap: ap.rearrange("p a c -> p (a c)")
            for g4 in range(ntile // 4):
                sl = g4 % 2
                S.dma("sp", zt[sl][:], HG4[g4][:, :, zc0:zc0 + 128], "ld_z%d" % sl, writes=["zt%d" % sl])
                S.op("act", lambda e, sl=sl: e.activation(out=f2(w1[:]), in_=f2(zt[sl][:]), func=AF.Exp, scale=-1.0), reads=["zt%d" % sl], writes=["w1"])
                S.op("dve", lambda e: e.tensor_scalar(out=f2(w1[:]), in0=f2(w1[:]), scalar1=1.0, scalar2=None, op0=ALU.add), reads=["w1"], writes=["w1"])
                S.op("dve", lambda e: e.reciprocal(out=f2(w1[:]), in_=f2(w1[:])), reads=["w1"], writes=["w1"])
                S.op("dve", lambda e: e.tensor_tensor(out=f2(w1[:]), in0=f2(w1[:]), in1=f2(oml4[:]), op=ALU.mult), reads=["w1", "oml4"], writes=["w1"])
                S.op("dve", lambda e: e.tensor_tensor(out=f2(w2[:]), in0=f2(w1[:]), in1=f2(lb4[:]), op=ALU.add), reads=["w1", "lb4"], writes=["w2"])
                tk = ["LF%d" % (4 * g4 + j) for j in range(4)]
                kk = ["KK%d" % (4 * g4 + j) for j in range(4)]
                S.op("act", lambda e, g4=g4: e.activation(out=f2(LF[:, 4 * g4:4 * g4 + 4, :]), in_=f2(w2[:]), func=AF.Ln), reads=["w2"], writes=tk)
                S.op("dve", lambda e, g4=g4: e.tensor_scalar(out=f2(KK[:, 4 * g4:4 * g4 + 4, :]), in0=f2(w2[:]), scalar1=-1.0, scalar2=1.0,
                                                            op0=ALU.mult, op1=ALU.add), reads=["w2"], writes=kk)
                for j in range(4):
                    t = 4 * g4 + j
                    S.op("pe", lambda e, t=t, IND=IND: e.matmul(out=pu[:, 0:4], lhsT=LF[:, t, :], rhs=IND, start=True, stop=True),
                         reads=["LF%d" % t, "hc"], writes=["pu"])
                    for a in range(2):
                        S.op("dve", lambda e, t=t, a=a: e.tensor_copy(out=U2[:, 2 * t + a:2 * t + a + 1], in_=pu[:, 2 * a:2 * a + 1]),
                             reads=["pu"], writes=["U2"])
                        S.op("dve", lambda e, t=t, a=a: e.tensor_copy(out=U1[:, 2 * t + a:2 * t + a + 1], in_=pu[:, 2 * a + 1:2 * a + 2]),
                             reads=["pu"], writes=["U1"])
            if fwd:
                S.op("dve", lambda e: e.tensor_tensor(out=G[:, 0:n64 - 1], in0=U1[:, 0:n64 - 1], in1=U2[:, 1:n64], op=ALU.add), reads=["U1", "U2"], writes=["G"])
                S.op("act", lambda e: e.activation(out=G[:, 0:n64 - 1], in_=G[:, 0:n64 - 1], func=AF.Exp), reads=["G"], writes=["G"])
            else:
                S.op("dve", lambda e: e.tensor_tensor(out=G[:, 1:n64], in0=U1[:, 1:n64], in1=U2[:, 0:n64 - 1], op=ALU.add), reads=["U1", "U2"], writes=["G"])
                S.op("act", lambda e: e.activation(out=G[:, 1:n64], in_=G[:, 1:n64], func=AF.Exp), reads=["G"], writes=["G"])
            tiles = list(range(ntile)) if fwd else list(range(ntile - 1, -1, -1))
            first = True

            def prep(ti, t, MT=MT):
                sl = ti % 2
                Vb, V6, Kt, QKT = Vb2[sl], V62[sl], Kt2[sl], QKT2[sl]
                S.dma("sp", qv[sl][:], HGv[t][:, 0:256], "ld_qv%d" % sl, writes=["qv%d" % sl])
                S.dma("act", v64[sl][:], HG6[t][:, :, 128:256], "ld_v6%d" % sl, writes=["v64%d" % sl])
                S.op("pool", lambda e: e.tensor_copy(out=Vb[:], in_=qv[sl][:, 128:256]), reads=["qv%d" % sl], writes=["Vb%d" % sl])
                S.op("pool", lambda e: e.tensor_copy(out=V6[:], in_=v64[sl][:]), reads=["v64%d" % sl], writes=["V6%d" % sl])
                S.op("pe", lambda e: e.matmul(out=pA[:, 0:128], lhsT=MT, rhs=LF[:, t, :], start=True, stop=True),
                     reads=["hc", "LF%d" % t], writes=["pA"])
                S.op("act", lambda e: e.activation(out=EA[:], in_=pA[:, 0:128], func=AF.Exp), reads=["pA"], writes=["EA"])
                S.op("act", lambda e: e.activation(out=EnA[:], in_=pA[:, 0:128], func=AF.Exp, scale=-1.0), reads=["pA"], writes=["EnA"])
                S.op("dve", lambda e: e.tensor_tensor(out=Qt[:], in0=qv[sl][:, 0:128], in1=EA[:], op=ALU.mult), reads=["qv%d" % sl, "EA"], writes=["Qt"])
                S.op("pool", lambda e: e.tensor_tensor(out=Kt[:], in0=KK[:, t, :], in1=EnA[:], op=ALU.mult), reads=["KK%d" % t, "EnA"], writes=["Kt%d" % sl])
                S.op("pe", lambda e: e.transpose(out=pT[:, 0, :], in_=Qt[:], identity=idb[:]), reads=["Qt", "idb"], writes=["pT"])
                S.op("pe", lambda e: e.transpose(out=pT[:, 1, :], in_=Kt[:], identity=idb[:]), reads=["Kt%d" % sl, "idb"], writes=["pT"])
                S.op("dve", lambda e: e.tensor_copy(out=QKT[:], in_=pT[:, 0:2, :]), reads=["pT"], writes=["QKT%d" % sl])

            prep(0, tiles[0])
            for ti, t in enumerate(tiles):
                sl = ti % 2
                Vb, V6, Kt, QKT = Vb2[sl], V62[sl], Kt2[sl], QKT2[sl]
                kVb, kV6, kKt, kQKT = "Vb%d" % sl, "V6%d" % sl, "Kt%d" % sl, "QKT%d" % sl
                if ti + 1 < len(tiles):
                    prep(ti + 1, tiles[ti + 1])
                for a in ((0, 1) if fwd else (1, 0)):
                    c64 = 2 * t + a
                    ca = slice(a * 64, (a + 1) * 64)
                    last = (c64 == n64 - 1) if fwd else (c64 == 0)
                    for h in range(2):
                        hp = slice(h * 64, (h + 1) * 64)
                        S.op("pe", lambda e, hp=hp, ca=ca, h=h, QKT=QKT: e.matmul(out=pSc[0:64, h * 64:(h + 1) * 64], lhsT=QKT[hp, 1, ca], rhs=QKT[hp, 0, ca],
                                                                         start=True, stop=True),
                             reads=[kQKT], writes=["pSc"])
                    S.op("dve", lambda e, MASK=MASK: e.tensor_tensor(out=scT[:], in0=pSc[0:64, 0:128], in1=MASK, op=ALU.mult),
                         reads=["pSc", "hc"], writes=["scT"])
                    for h in range(2):
                        hp = slice(h * 64, (h + 1) * 64)
                        S.op("pe", lambda e, hp=hp, a=a, h=h, first=first, V6=V6: e.matmul(out=pO[0:64, h * 64:(h + 1) * 64], lhsT=scT[:, h * 64:(h + 1) * 64],
                                                                                   rhs=V6[:, a, h * 64:(h + 1) * 64], start=True, stop=first),
                             reads=["scT", kV6], writes=["pO"])
                        if not first:
                            S.op("pe", lambda e, hp=hp, ca=ca, h=h, QKT=QKT: e.matmul(out=pO[0:64, h * 64:(h + 1) * 64], lhsT=QKT[hp, 0, ca], rhs=Spb[hp, :],
                                                                             start=False, stop=True),
                                 reads=[kQKT, "Spb"], writes=["pO"])
                    if fwd:
                        S.op("dve", lambda e, c64=c64: e.tensor_copy(out=OA[:, c64, :], in_=pO[0:64, 0:128]), reads=["pO"], writes=["OA%d" % c64])
                    else:
                        S.op("dve", lambda e, c64=c64: e.tensor_tensor(out=OA[:, c64, :], in0=OA[:, c64, :], in1=pO[0:64, 0:128], op=ALU.add),
                             reads=["pO", "OA%d" % c64], writes=["OA%d" % c64])
                    if not last:
                        S.op("pe", lambda e, a=a, Kt=Kt, Vb=Vb: e.matmul(out=pP[:, 0:128], lhsT=Kt[a * 64:(a + 1) * 64, :], rhs=Vb[a * 64:(a + 1) * 64, :],
                                                           start=True, stop=True),
                             reads=[kKt, kVb], writes=["pP"])
                        for h in range(2):
                            hp = slice(h * 64, (h + 1) * 64)
                            if first:
                                S.op("dve", lambda e, hp=hp, h=h: e.tensor_copy(out=Tt[hp, :], in_=pP[hp, h * 64:(h + 1) * 64]), reads=["pP"], writes=["Tt"])
                            else:
                                S.op("dve", lambda e, hp=hp, h=h: e.tensor_tensor(out=Tt[hp, :], in0=Spf[hp, :], in1=pP[hp, h * 64:(h + 1) * 64], op=ALU.add),
                                     reads=["pP", "Spf"], writes=["Tt"])
                        S.op("dve", lambda e, c64=c64: e.tensor_scalar(out=Spf[:], in0=Tt[:], scalar1=G[:, c64:c64 + 1], scalar2=None, op0=ALU.mult),
                             reads=["Tt", "G"], writes=["Spf"])
                        S.op("pool", lambda e: e.tensor_copy(out=Spb[:], in_=Spf[:]), reads=["Spf"], writes=["Spb"])
                    first = False
        HGc = HG.rearrange("(n c p) d -> n p c d", p=64, c=8)
        for n in range(n64 // 8):
            S.dma("sp", hgt[:], HGc[n][:, :, 512:640], "ld_hg", writes=["hgt"])
            sl = n % 2
            g2 = lambda ap: ap.rearrange("p a c -> p (a c)")
            S.op("act", lambda e: e.activation(out=g2(sg[:]), in_=g2(hgt[:]), func=AF.Exp, scale=-1.0), reads=["hgt"], writes=["sg"])
            S.op("dve", lambda e: e.tensor_scalar(out=g2(sg[:]), in0=g2(sg[:]), scalar1=1.0, scalar2=None, op0=ALU.add), reads=["sg"], writes=["sg"])
            S.op("dve", lambda e: e.reciprocal(out=g2(sg[:]), in_=g2(sg[:])), reads=["sg"], writes=["sg"])
            for c in range(8):
                c64 = n * 8 + c
                for h in range(2):
                    S.op("act", lambda e, c64=c64, h=h: e.activation(out=junk[:], in_=OA[:, c64, h * 64:(h + 1) * 64], func=AF.Square,
                                                                     accum_out=fst[:, h:h + 1]),
                         reads=["OA%d" % c64], writes=["junk", "fst%d" % h])
                S.op("dve", lambda e: e.tensor_scalar(out=fst[:, 2:4], in0=fst[:, 0:2], scalar1=1.0 / 64, scalar2=EPS, op0=ALU.mult, op1=ALU.add),
                     reads=["fst0", "fst1"], writes=["fst23"])
                S.op("act", lambda e: e.activation(out=fst[:, 2:4], in_=fst[:, 2:4], func=AF.Ln), reads=["fst23"], writes=["fst23"])
                S.op("act", lambda e: e.activation(out=fst[:, 2:4], in_=fst[:, 2:4], func=AF.Exp, scale=-0.5), reads=["fst23"], writes=["fst23"])
                for h in range(2):
                    S.op("dve", lambda e, c64=c64, h=h: e.scalar_tensor_tensor(out=yy[:, h * 64:(h + 1) * 64], in0=OA[:, c64, h * 64:(h + 1) * 64],
                                                                              scalar=fst[:, 2 + h:3 + h], in1=gon[:], op0=ALU.mult, op1=ALU.mult),
                         reads=["OA%d" % c64, "fst23", "gon"], writes=["yy"])
                S.op("dve", lambda e, c=c: e.tensor_tensor(out=yb[:], in0=yy[:], in1=sg[:, c, :], op=ALU.mult), reads=["yy", "sg"], writes=["yb"])
                S.op("pe", lambda e, c=c: e.transpose(out=pF[:, c, 0:64], in_=yb[:], identity=idb[0:64, 0:64]), reads=["yb", "idb"], writes=["pF"])
                S.op("dve", lambda e, c=c, sl=sl: e.tensor_copy(out=oS[sl][:, c * 64:(c + 1) * 64], in_=pF[:, c, 0:64]), reads=["pF"], writes=["oS%d" % sl])
            S.dma("sp", oT_out[384:512, n * 512:(n + 1) * 512], oS[sl][:], "st_o", reads=["oS%d" % sl], final=True)
        S.emit()

from concourse.bass_utils import run_bass_kernel_spmd

NTOK = 8192
RG_PAIRS = [[0, 1], [2, 3], [4, 5], [6, 7]]
_CACHE = {}
LAYER_VECS = (("ln", D), ("g_qa", 384), ("g_kva", 128), ("g_qn", 96), ("g_kn", 96), ("dg_qn", 32), ("dg_kn", 32),
              ("lq1", 32), ("lk1", 32), ("lq2", 32), ("lk2", 32), ("g_sub", 64), ("g_on", 64), ("ln2", D))
LAYER_MATS = (("w_in", (D, NOWN)), ("w_uq", (384, 192)), ("w_ukv", (128, 256)), ("wg", (D, 4096)), ("wbr", (1024, D)),
              ("wo", (D, D)), ("wgu", (D, 2 * DFF)), ("wd", (DFF, D)))


def own_cols(j):
    c = list(range(0, 544))
    for base in (544, 800, 1056, 1312, 1568, 1824, 2080, 2336, 2592):
        c += list(range(base + 128 * j, base + 128 * j + 128))
    return np.array(c)


def rope_table(n):
    inv = (1.0 / (10000.0 ** (np.arange(0, 32, 2, dtype=np.float32) / 32))).astype(np.float32)
    ang = np.arange(n, dtype=np.float32)[:, None] * inv[None, :]
    return np.concatenate([np.tile(np.cos(ang), (1, 4)), np.tile(np.sin(ang), (1, 4))], -1).astype(np.float32)


def emit_allgather(nc, pairs):
    S = Sched(nc)
    cs_ = []
    for src, dst in pairs:
        cs_.append(S.op("pool", lambda e, src=src, dst=dst: e.collective_compute("AllGather", ALU.bypass, replica_groups=RG_PAIRS,
                                                                              ins=[src], outs=[dst])))
    S.op("pool", lambda e: e.nop(), deps=cs_)
    S.emit()


def build_fused(nl=2):
    nc = bass.Bass("TRN2", target_bir_lowering=False)
    nc.allow_low_precision("bf16 matmul operands, fp32 accumulate")
    I = lambda name, shape, dt=F32: nc.dram_tensor(name, list(shape), dt, kind="ExternalInput").ap()
    N = lambda name, shape, dt: nc.dram_tensor(name, list(shape), dt, kind="Internal").ap()
    x_full = I("x_full", (NTOK, D)); x_my = I("x_my", (NT_R, D)); selmask = I("selmask", (2,))
    cs = I("cs", (NTOK, 128)); ident = I("ident", (128, 128)); fconst = I("fconst", (128, NFC)); hconst = I("hconst", (128, NHC))
    lbl = I("lbl", (2, 2, 128))
    W = []
    for l in range(2):
        d = {}
        for name, n in LAYER_VECS:
            d[name] = I("%s_%d" % (name, l), (n,))
        for name, shp in LAYER_MATS:
            d[name] = I("%s_%d" % (name, l), shp)
        W.append(d)
    QT = N("QT", (2, 128, NTOK), BF16); KT = N("KT", (2, 128, NTOK), BF16); V = N("V", (NTOK, 128), BF16)
    DQT = N("DQT", (2, 64, NTOK), BF16); DKT = N("DKT", (2, 64, NTOK), BF16); DV = N("DV", (NTOK, 128), BF16)
    U = N("U", (NTOK, 128), F32); HG = N("HG", (NTOK, 640), F32)
    UCS = N("UCS", (NTOK, 256), BF16); XPd = N("XPd", (128, 64, 256), BF16); Yd = N("Yd", (NTOK, 128), BF16)
    oT = N("oT", (512, NTOK), BF16); Gd = N("Gd", (4, 256, NTOK), BF16)
    xmid = N("xmid", (NT_R, D), F32); x1_my = N("x1_my", (NT_R, D), F32); XG = N("XG", (8, 2, 512, D), F32)
    x_out = nc.dram_tensor("x_out", [NT_R, D], F32, kind="ExternalOutput").ap()
    for l in range(nl):
        TAG[0] = "L%d_" % l
        w = W[l]
        XGv = XG.rearrange("k r (j p) d -> r k j p d", p=128)
        xf = x_full if l == 0 else (lambda t: XGv[t // 32, (t % 32) // 4, t % 4])
        emit_M1(nc, xf, w["w_in"], w["ln"], w["g_qa"], w["w_uq"], w["g_kva"], w["w_ukv"], w["g_qn"], w["g_kn"], w["dg_qn"], w["dg_kn"],
                cs, ident, QT, KT, V, DQT, DKT, DV, U, HG, ntok=NTOK)
        emit_attn(nc, QT, KT, V, DQT, DKT, DV, w["lq1"], w["lk1"], w["lq2"], w["lk2"], w["g_sub"], oT, l, NTOK)
        emit_fnet(nc, U, fconst, ident, UCS, XPd, Yd, oT)
        emit_hgrn(nc, HG, lbl, w["g_on"], hconst, ident, oT, l, NTOK)
        emit_allgather(nc, [(oT[n * 128:(n + 1) * 128, :], Gd[n]) for n in range(4)])
        xm = x_my if l == 0 else x1_my
        xo = x_out if l == nl - 1 else x1_my
        emit_R1(nc, xm, None, w["wg"], w["wbr"], w["wo"], w["ln"], xmid, ident, NT_R, gathered=(Gd, selmask))
        emit_R2(nc, xmid, w["wgu"], w["wd"], w["ln2"], xo, ident, NT_R)
        if l < nl - 1:
            emit_allgather(nc, [(x1_my[k * 512:(k + 1) * 512, :], XG[k].rearrange("r i d -> (r i) d")) for k in range(8)])
    TAG[0] = ""
    return nc


def kernel(x, ln_mix, w_in, mla_g_qa, mla_w_uq, mla_g_kva, mla_w_ukv, mla_g_qn, mla_g_kn,
           diff_g_qn, diff_g_kn, diff_lq1, diff_lk1, diff_lq2, diff_lk2, diff_g_sub,
           hgrn_lb_logits, hgrn_g_on, w_branch, w_out, ln_ffn, w_gate_up, w_down):
    f32 = lambda a: np.ascontiguousarray(np.asarray(a, dtype=np.float32))
    x = f32(x)
    B = x.shape[0]
    if "F" not in _CACHE:
        _CACHE["F"] = build_fused()
    nc = _CACHE["F"]
    shared = dict(cs=rope_table(NTOK), ident=np.eye(128, dtype=np.float32), fconst=fnet_consts(), hconst=hgrn_consts())
    lbl_all = f32(hgrn_lb_logits)
    per_layer = []
    for l in range(2):
        per_layer.append({
            "ln": f32(ln_mix[l]), "g_qa": f32(mla_g_qa[l]), "g_kva": f32(mla_g_kva[l]), "g_qn": f32(mla_g_qn[l]), "g_kn": f32(mla_g_kn[l]),
            "dg_qn": f32(diff_g_qn[l]), "dg_kn": f32(diff_g_kn[l]), "lq1": f32(diff_lq1[l]), "lk1": f32(diff_lk1[l]),
            "lq2": f32(diff_lq2[l]), "lk2": f32(diff_lk2[l]), "g_sub": f32(diff_g_sub[l]), "g_on": f32(hgrn_g_on[l]), "ln2": f32(ln_ffn[l]),
            "wg": f32(np.asarray(w_in[l])[:, 2848:6944]), "wbr": f32(np.asarray(w_branch[l]).reshape(1024, D)), "wo": f32(w_out[l]),
            "wgu": f32(w_gate_up[l]), "wd": f32(w_down[l])})
    own = {}
    for j in range(2):
        oc = own_cols(j)
        own[j] = [{"w_in": f32(np.asarray(w_in[l])[:, oc]), "w_uq": f32(np.asarray(mla_w_uq[l])[:, 192 * j:192 * j + 192]),
                   "w_ukv": f32(np.asarray(mla_w_ukv[l])[:, 256 * j:256 * j + 256])} for l in range(2)]
    in_maps = []
    cores = [(b, j) for b in range(B) for j in range(2)]
    for (b, j) in cores:
        m = dict(shared)
        m["x_full"] = x[b]
        m["x_my"] = np.ascontiguousarray(x[b, j * NT_R:(j + 1) * NT_R])
        m["selmask"] = np.array([1.0, 0.0] if j == 0 else [0.0, 1.0], dtype=np.float32)
        m["lbl"] = np.ascontiguousarray(lbl_all[:, :, 128 * j:128 * j + 128])
        for l in range(2):
            for k, v in per_layer[l].items():
                m["%s_%d" % (k, l)] = v
            for k, v in own[j][l].items():
                m["%s_%d" % (k, l)] = v
        in_maps.append(m)
    res = run_bass_kernel_spmd(nc, in_maps, core_ids=list(range(8)))
    out = np.empty_like(x)
    for i, (b, j) in enumerate(cores):
        out[b, j * NT_R:(j + 1) * NT_R] = res.results[i]["x_out"]
    return out
```

```python
LIM = 99
SUB = 99

import concourse.bass as bass
import concourse.mybir as mybir

ENGS = ("pe", "act", "dve", "pool", "sp")
SKIP_PE_SELF = [False]
ATTACH = ["none"]


class Op:
    __slots__ = ("eng", "fn", "deps", "signal", "cnt", "is_dma", "chan", "chan_idx", "idx")

    def __init__(self, eng, fn, is_dma=False, chan=None):
        self.eng = eng
        self.fn = fn
        self.deps = []
        self.signal = False
        self.cnt = None
        self.is_dma = is_dma
        self.chan = chan
        self.chan_idx = None
        self.idx = None


class Sched:
    _uid = [0]

    def __init__(self, nc):
        self.nc = nc
        Sched._uid[0] += 1
        self.uid = Sched._uid[0]
        self.ops = {e: [] for e in ENGS}
        self.last_w = {}
        self.readers = {}
        self.chan_ops = {}
        self.final_dma = []

    def _track(self, op, reads, writes, deps):
        ds = list(deps)
        for k in reads:
            w = self.last_w.get(k)
            if w is not None:
                ds.append(w)
        for k in writes:
            w = self.last_w.get(k)
            if w is not None:
                ds.append(w)
            ds.extend(self.readers.get(k, ()))
        for k in writes:
            self.last_w[k] = op
            self.readers[k] = []
        for k in reads:
            self.readers.setdefault(k, []).append(op)
        seen = set()
        for d in ds:
            if d is op or id(d) in seen:
                continue
            seen.add(id(d))
            op.deps.append(d)
            if not (SKIP_PE_SELF[0] and op.eng == "pe" and d.eng == "pe" and not d.is_dma):
                d.signal = True

    def op(self, eng, fn, reads=(), writes=(), deps=()):
        o = Op(eng, fn)
        o.idx = len(self.ops[eng])
        self.ops[eng].append(o)
        self._track(o, reads, writes, deps)
        return o

    def dma(self, eng, out, in_, chan, reads=(), writes=(), deps=(), final=False, **kw):
        def fn(e, out=out, in_=in_, kw=kw):
            return e.dma_start(out=out, in_=in_, **kw)
        o = Op(eng, fn, is_dma=True, chan=chan)
        o.idx = len(self.ops[eng])
        lst = self.chan_ops.setdefault(chan, [])
        o.chan_idx = len(lst)
        prev = lst[-1] if lst else None
        lst.append(o)
        self.ops[eng].append(o)
        self._track(o, reads, writes, list(deps) + ([prev] if prev is not None else []))
        o.signal = True
        if final:
            self.final_dma.append(o)
        return o

    def emit(self):
        nc = self.nc
        import contextlib
        with contextlib.ExitStack() as st:
            esem = {e: st.enter_context(nc.semaphore("s%d_%s" % (self.uid, e))) for e in ENGS if e != "sp"}
            csem = {c: st.enter_context(nc.semaphore("c%d_%s" % (self.uid, c))) for c in self.chan_ops}
            for e in ENGS:
                c = 0
                for o in self.ops[e]:
                    if o.is_dma:
                        continue
                    if o.signal:
                        c += 1
                        o.cnt = c
            with nc.Block() as b0:
                @b0.gpsimd
                def _(g):
                    for s_ in list(esem.values()) + list(csem.values()):
                        g.sem_clear(s_)
            block = st.enter_context(nc.Block())

            def event(o):
                if o.is_dma:
                    return csem[o.chan], 16 * (o.chan_idx + 1)
                return esem[o.eng], o.cnt

            def run(e, engobj):
                waited = {}
                for o in self.ops[e]:
                    need = {}
                    for d in o.deps:
                        if SKIP_PE_SELF[0] and e == "pe" and d.eng == "pe" and not d.is_dma:
                            continue
                        s, v = event(d)
                        if waited.get(s.name, 0) >= v:
                            continue
                        if s.name not in need or need[s.name][1] < v:
                            need[s.name] = (s, v)
                    lst = list(need.values())
                    attach = None
                    if lst and not o.is_dma and (ATTACH[0] == "all" or (ATTACH[0] == "pe" and e == "pe")):
                        attach = lst.pop()
                    for i, (s, v) in enumerate(lst):
                        engobj.wait_ge(s, v)
                        waited[s.name] = v
                        rem = len(lst) - (i + 1)
                        if rem >= 1 and (i % 2 == 1):
                            engobj.nop()
                    ins = o.fn(engobj)
                    if attach is not None:
                        ins._wait_ge(attach[0], attach[1])
                        waited[attach[0].name] = attach[1]
                    if o.is_dma:
                        ins.then_inc(csem[o.chan], 16)
                    elif o.signal:
                        ins.then_inc(esem[e], 1)
                if e == "sp":
                    for o in self.final_dma:
                        s, v = event(o)
                        engobj.wait_ge(s, v)

            @block.tensor
            def _(t):
                run("pe", t)

            @block.scalar
            def _(t):
                run("act", t)

            @block.vector
            def _(t):
                run("dve", t)

            @block.gpsimd
            def _(t):
                run("pool", t)

            @block.sync
            def _(t):
                run("sp", t)

import contextlib
import numpy as np
import concourse.bass as bass
import concourse.mybir as mybir

F32 = mybir.dt.float32
BF16 = mybir.dt.bfloat16
AF = mybir.ActivationFunctionType
ALU = mybir.AluOpType

D = 1024
DFF = 2816
NT_R = 4096
ST = 512
EPS = 1e-6
TAG = [""]


def load_w_bf16(S, nc, st, name, dram, kc, n, wb, stage, q="sp", cast="pool", nblk=256):
    v = dram.rearrange("(c p) n -> p c n", p=128)
    i = 0
    for n0 in range(0, n, nblk):
        nb = min(nblk, n - n0)
        sl = i % 2
        sv = stage[sl][:, 0:kc * nb].rearrange("p (c n) -> p c n", c=kc)
        S.dma(q if sl == 0 else "act", sv, v[:, :, n0:n0 + nb], "ldw_%d" % sl, writes=["stg%d" % sl])
        S.op(cast, lambda e, sv=sv, n0=n0, nb=nb: e.tensor_copy(out=wb[:, :, n0:n0 + nb], in_=sv),
             reads=["stg%d" % sl], writes=["w_" + name])
        i += 1


def rmsnorm_tile(S, tag, x_ap, xkey, gB, h_ap, hkey, junk, stat, col):
    sk = "stat_%s_%d" % (tag, col)
    S.op("act", lambda e: e.activation(out=junk[:], in_=x_ap, func=AF.Square, accum_out=stat[:, col:col + 1]),
         reads=[xkey], writes=["junk_" + tag, sk])
    S.op("dve", lambda e: e.tensor_scalar(out=stat[:, col:col + 1], in0=stat[:, col:col + 1], scalar1=1.0 / D, scalar2=EPS,
                                           op0=ALU.mult, op1=ALU.add), reads=[sk], writes=[sk])
    S.op("act", lambda e: e.activation(out=stat[:, col:col + 1], in_=stat[:, col:col + 1], func=AF.Ln), reads=[sk], writes=[sk])
    S.op("act", lambda e: e.activation(out=stat[:, col:col + 1], in_=stat[:, col:col + 1], func=AF.Exp, scale=-0.5),
         reads=[sk], writes=[sk])
    S.op("dve", lambda e: e.scalar_tensor_tensor(out=h_ap, in0=x_ap, scalar=stat[:, col:col + 1], in1=gB[:],
                                                  op0=ALU.mult, op1=ALU.mult), reads=[xkey, sk, "gB_" + tag], writes=[hkey])


def emit_R1(nc, x_in, oT_in, wg, wbr, wo, ln, xmid_out, ident_d, ntok=NT_R, gathered=None):
    nst = ntok // ST
    PFX = "r1_" + TAG[0]
    with contextlib.ExitStack() as st:
        T = lambda name, shape, dt: st.enter_context(nc.sbuf_tensor(PFX + name, shape, dt))
        Wg = T("Wg", [128, 8, 4096], BF16)
        Wb = T("Wb", [128, 8, 1024], BF16)
        Wo = T("Wo", [128, 8, 1024], BF16)
        stage = [T("stg0", [128, 2048], F32), T("stg1", [128, 2048], F32)]
        gB = T("gB", [128, D], F32)
        idf = T("idf", [128, 128], F32)
        idb = T("idb", [128, 128], BF16)
        xt = T("xt", [128, 4, D], F32)
        hb = T("hb", [128, D], BF16)
        hT = T("hT", [128, 8, ST], BF16)
        oT = T("oT", [128, 8, ST], BF16)
        if gathered is not None:
            Ga = T("Ga", [128, 8, ST], BF16)
            Gb = T("Gb", [128, 8, ST], BF16)
            msk = T("msk", [128, 2], F32)
        mg = T("mg", [128, 8, ST], F32)
        mgb = T("mgb", [128, 8, ST], BF16)
        gt = [T("gt0", [128, ST], F32), T("gt1", [128, ST], F32)]
        junk = T("junk", [128, D], BF16)
        stat = T("stat", [128, 4], F32)
        pT = st.enter_context(nc.psum_tensor(PFX + "pT", [128, 8, 128], BF16))
        pG = [st.enter_context(nc.psum_tensor(PFX + "pG%d" % i, [128, ST], F32)) for i in range(2)]
        pB = [st.enter_context(nc.psum_tensor(PFX + "pB%d" % i, [128, ST], F32)) for i in range(2)]
        pO = [st.enter_context(nc.psum_tensor(PFX + "pO%d" % i, [128, ST], F32)) for i in range(2)]
        S = Sched(nc)
        S.dma("act", idf[:], ident_d, "ld_id", writes=["idf"])
        S.op("dve", lambda e: e.tensor_copy(out=idb[:], in_=idf[:]), reads=["idf"], writes=["idb"])
        S.dma("act", gB[:], ln.rearrange("(o d) -> o d", o=1).partition_broadcast(128), "ld_g", writes=["gB_r1"])
        load_w_bf16(S, nc, st, "g", wg, 8, 4096, Wg, stage)
        load_w_bf16(S, nc, st, "b", wbr, 8, 1024, Wb, stage)
        load_w_bf16(S, nc, st, "o", wo, 8, 1024, Wo, stage)
        xv = x_in.rearrange("(s j p) d -> s p j d", p=128, j=4)
        ov = xmid_out.rearrange("(s j p) d -> s p j d", p=128, j=4)
        if gathered is None:
            oTv = oT_in.rearrange("(c p) t -> p c t", p=128)
        else:
            Gd, selmask = gathered
            S.dma("act", msk[:], selmask.rearrange("(o d) -> o d", o=1).partition_broadcast(128), "ld_g", writes=["msk"])
        k = 0
        for s in range(nst):
            S.dma("sp", xt[:], xv[s], "ld_x", writes=["xt"])
            if gathered is None:
                S.dma("act", oT[:], oTv[:, :, s * ST:(s + 1) * ST], "ld_o", writes=["oT"])
            else:
                for hf, Gt, gk in ((0, Ga, "Ga"), (1, Gb, "Gb")):
                    for n in range(4):
                        c0 = hf * ntok + s * ST
                        S.dma("act", Gt[:, 2 * n:2 * n + 2, :], Gd[n, :, c0:c0 + ST].rearrange("(r p) t -> p r t", p=128),
                              "ld_o%d" % hf, writes=[gk])
                S.op("dve", lambda e: e.tensor_scalar(out=oT[:], in0=Ga[:], scalar1=msk[:, 0:1], scalar2=None, op0=ALU.mult),
                     reads=["Ga", "msk"], writes=["oT"])
                S.op("dve", lambda e: e.scalar_tensor_tensor(out=oT[:], in0=Gb[:], scalar=msk[:, 1:2], in1=oT[:], op0=ALU.mult, op1=ALU.add),
                     reads=["Gb", "msk", "oT"], writes=["oT"])
            for j in range(4):
                rmsnorm_tile(S, "r1", xt[:, j, :], "xt", gB, hb[:], "hb", junk, stat, j)
                for c in range(8):
                    S.op("pe", lambda e, c=c: e.transpose(out=pT[:, c, :], in_=hb[:, c * 128:(c + 1) * 128], identity=idb[:]),
                         reads=["hb", "idb"], writes=["pT"] if c in (0, 7) else [])
                S.op("dve", lambda e, j=j: e.tensor_copy(out=hT[:, :, j * 128:(j + 1) * 128], in_=pT[:]),
                     reads=["pT"], writes=["hT"])
            for cc in range(8):
                for n in range(4):
                    b = k % 2
                    k += 1
                    col0 = n * 1024 + cc * 128
                    for c in range(8):
                        S.op("pe", lambda e, c=c, b=b, col0=col0: e.matmul(out=pG[b][:], lhsT=Wg[:, c, col0:col0 + 128], rhs=hT[:, c, :],
                                                                         start=(c == 0), stop=(c == 7)),
                             reads=["hT", "w_g"], writes=["pG%d" % b] if c in (0, 7) else [])
                    for c2 in range(2):
                        S.op("pe", lambda e, c2=c2, b=b, n=n, cc=cc: e.matmul(out=pB[b][:], lhsT=Wb[:, n * 2 + c2, cc * 128:(cc + 1) * 128],
                                                                            rhs=oT[:, n * 2 + c2, :], start=(c2 == 0), stop=(c2 == 1)),
                             reads=["oT", "w_b"], writes=["pB%d" % b])
                    S.op("act", lambda e, b=b: e.activation(out=gt[b][:], in_=pG[b][:], func=AF.Sigmoid),
                         reads=["pG%d" % b], writes=["gt%d" % b])
                    if n == 0:
                        S.op("dve", lambda e, b=b, cc=cc: e.tensor_tensor(out=mg[:, cc, :], in0=gt[b][:], in1=pB[b][:], op=ALU.mult),
                             reads=["gt%d" % b, "pB%d" % b], writes=["mg%d" % cc])
                    else:
                        S.op("dve", lambda e, b=b: e.tensor_tensor(out=gt[b][:], in0=gt[b][:], in1=pB[b][:], op=ALU.mult),
                             reads=["gt%d" % b, "pB%d" % b], writes=["gt%d" % b])
                        S.op("pool", lambda e, b=b, cc=cc: e.tensor_tensor(out=mg[:, cc, :], in0=mg[:, cc, :], in1=gt[b][:], op=ALU.add),
                             reads=["gt%d" % b, "mg%d" % cc], writes=["mg%d" % cc])
                S.op("pool", lambda e, cc=cc: e.tensor_copy(out=mgb[:, cc, :], in_=mg[:, cc, :]), reads=["mg%d" % cc], writes=["mgb%d" % cc])
            for j in range(4):
                for hf in range(2):
                    b = k % 2
                    k += 1
                    for c in range(8):
                        S.op("pe", lambda e, c=c, b=b, j=j, hf=hf: e.matmul(out=pO[b][:], lhsT=mgb[:, c, j * 128:(j + 1) * 128],
                                                                          rhs=Wo[:, c, hf * 512:(hf + 1) * 512], start=(c == 0), stop=(c == 7)),
                             reads=["mgb%d" % c, "w_o"], writes=["pO%d" % b] if c in (0, 7) else [])
                    S.op("dve", lambda e, b=b, j=j, hf=hf: e.tensor_tensor(out=xt[:, j, hf * 512:(hf + 1) * 512], in0=xt[:, j, hf * 512:(hf + 1) * 512],
                                                                         in1=pO[b][:], op=ALU.add),
                         reads=["pO%d" % b, "xt"], writes=["xt"])
            S.dma("sp", ov[s], xt[:], "st_x", reads=["xt"], final=True)
        S.emit()


def emit_R2(nc, xmid, wgu, wd, ln, x_out, ident_d, ntok=NT_R):
    nst = ntok // ST
    NF = DFF // 128
    PFX = "r2_" + TAG[0]
    with contextlib.ExitStack() as st:
        T = lambda name, shape, dt: st.enter_context(nc.sbuf_tensor(PFX + name, shape, dt))
        Wgu = T("Wgu", [128, 8, 2 * DFF], BF16)
        Wd = T("Wd", [128, NF, D], BF16)
        stage = [T("stg0", [128, 1408], F32), T("stg1", [128, 1408], F32)]
        gB = T("gB", [128, D], F32)
        idf = T("idf", [128, 128], F32)
        idb = T("idb", [128, 128], BF16)
        xt = T("xt", [128, 4, D], F32)
        hb = T("hb", [128, D], BF16)
        hT = T("hT", [128, 8, ST], BF16)
        aT = T("aT", [128, NF, ST], BF16)
        sg = [T("sg0", [128, ST], F32), T("sg1", [128, ST], F32)]
        junk = T("junk", [128, D], BF16)
        stat = T("stat", [128, 4], F32)
        pT = st.enter_context(nc.psum_tensor(PFX + "pT", [128, 8, 128], BF16))
        pG = [st.enter_context(nc.psum_tensor(PFX + "pG%d" % i, [128, ST], F32)) for i in range(2)]
        pU = [st.enter_context(nc.psum_tensor(PFX + "pU%d" % i, [128, ST], F32)) for i in range(2)]
        pO = [st.enter_context(nc.psum_tensor(PFX + "pO%d" % i, [128, ST], F32)) for i in range(2)]
        S = Sched(nc)
        S.dma("act", idf[:], ident_d, "ld_id", writes=["idf"])
        S.op("dve", lambda e: e.tensor_copy(out=idb[:], in_=idf[:]), reads=["idf"], writes=["idb"])
        S.dma("act", gB[:], ln.rearrange("(o d) -> o d", o=1).partition_broadcast(128), "ld_g", writes=["gB_r2"])
        load_w_bf16(S, nc, st, "gu", wgu, 8, 2 * DFF, Wgu, stage, nblk=176)
        load_w_bf16(S, nc, st, "d", wd, NF, D, Wd, stage, nblk=64)
        xv = xmid.rearrange("(s j p) d -> s p j d", p=128, j=4)
        ov = x_out.rearrange("(s j p) d -> s p j d", p=128, j=4)
        k = 0
        for s in range(nst):
            S.dma("sp", xt[:], xv[s], "ld_x", writes=["xt"])
            for j in range(4):
                rmsnorm_tile(S, "r2", xt[:, j, :], "xt", gB, hb[:], "hb", junk, stat, j)
                for c in range(8):
                    S.op("pe", lambda e, c=c: e.transpose(out=pT[:, c, :], in_=hb[:, c * 128:(c + 1) * 128], identity=idb[:]),
                         reads=["hb", "idb"], writes=["pT"] if c in (0, 7) else [])
                S.op("dve", lambda e, j=j: e.tensor_copy(out=hT[:, :, j * 128:(j + 1) * 128], in_=pT[:]),
                     reads=["pT"], writes=["hT"])
            for fc in range(NF):
                b = k % 2
                k += 1
                for c in range(8):
                    S.op("pe", lambda e, c=c, b=b, fc=fc: e.matmul(out=pG[b][:], lhsT=Wgu[:, c, fc * 128:(fc + 1) * 128], rhs=hT[:, c, :],
                                                                 start=(c == 0), stop=(c == 7)),
                         reads=["hT", "w_gu"], writes=["pG%d" % b] if c in (0, 7) else [])
                for c in range(8):
                    S.op("pe", lambda e, c=c, b=b, fc=fc: e.matmul(out=pU[b][:], lhsT=Wgu[:, c, DFF + fc * 128:DFF + (fc + 1) * 128], rhs=hT[:, c, :],
                                                                 start=(c == 0), stop=(c == 7)),
                         reads=["hT", "w_gu"], writes=["pU%d" % b] if c in (0, 7) else [])
                S.op("act", lambda e, b=b: e.activation(out=sg[b][:], in_=pG[b][:], func=AF.Silu), reads=["pG%d" % b], writes=["sg%d" % b])
                S.op("dve", lambda e, b=b, fc=fc: e.tensor_tensor(out=aT[:, fc, :], in0=sg[b][:], in1=pU[b][:], op=ALU.mult),
                     reads=["sg%d" % b, "pU%d" % b], writes=["aT%d" % fc])
            for j in range(4):
                for hf in range(2):
                    b = k % 2
                    k += 1
                    for fc in range(NF):
                        S.op("pe", lambda e, fc=fc, b=b, j=j, hf=hf: e.matmul(out=pO[b][:], lhsT=aT[:, fc, j * 128:(j + 1) * 128],
                                                                            rhs=Wd[:, fc, hf * 512:(hf + 1) * 512], start=(fc == 0), stop=(fc == NF - 1)),
                             reads=["aT%d" % fc, "w_d"], writes=["pO%d" % b] if fc in (0, NF - 1) else [])
                    S.op("dve", lambda e, b=b, j=j, hf=hf: e.tensor_tensor(out=xt[:, j, hf * 512:(hf + 1) * 512], in0=xt[:, j, hf * 512:(hf + 1) * 512],
                                                                         in1=pO[b][:], op=ALU.add),
                         reads=["pO%d" % b, "xt"], writes=["xt"])
            S.dma("sp", ov[s], xt[:], "st_x", reads=["xt"], final=True)
        S.emit()


def build_R(ntok=NT_R):
    nc = bass.Bass("TRN2", target_bir_lowering=False)
    nc.allow_low_precision("bf16 matmul operands, fp32 accumulate (reference tolerance is bf16-level)")
    x_in = nc.dram_tensor("x_in", [ntok, D], F32, kind="ExternalInput").ap()
    oT_in = nc.dram_tensor("oT_in", [1024, ntok], BF16, kind="ExternalInput").ap()
    wg = nc.dram_tensor("wg", [D, 4096], F32, kind="ExternalInput").ap()
    wbr = nc.dram_tensor("wbr", [1024, D], F32, kind="ExternalInput").ap()
    wo = nc.dram_tensor("wo", [D, D], F32, kind="ExternalInput").ap()
    ln1 = nc.dram_tensor("ln1", [D], F32, kind="ExternalInput").ap()
    ln2 = nc.dram_tensor("ln2", [D], F32, kind="ExternalInput").ap()
    wgu = nc.dram_tensor("wgu", [D, 2 * DFF], F32, kind="ExternalInput").ap()
    wd = nc.dram_tensor("wd", [DFF, D], F32, kind="ExternalInput").ap()
    ident = nc.dram_tensor("ident", [128, 128], F32, kind="ExternalInput").ap()
    xmid = nc.dram_tensor("xmid", [ntok, D], F32, kind="Internal").ap()
    x_out = nc.dram_tensor("x_out", [ntok, D], F32, kind="ExternalOutput").ap()
    emit_R1(nc, x_in, oT_in, wg, wbr, wo, ln1, xmid, ident, ntok)
    emit_R2(nc, xmid, wgu, wd, ln2, x_out, ident, ntok)
    return nc

import contextlib
import numpy as np
import concourse.bass as bass
import concourse.mybir as mybir

AX = mybir.AxisListType
NOWN = 1696
SEQ = 8192


def rms_groups(S, tag, src, srckey, G, d, gain, gainkey, out, outkey, tmp, st, stcol):
    sk = "st_%s_%d" % (tag, stcol)
    srck = list(srckey) if isinstance(srckey, (list, tuple)) else [srckey]
    stv = st[:, stcol:stcol + G]
    for g in range(G):
        S.op("act", lambda e, g=g: e.activation(out=tmp[:, 0:d], in_=src[:, g * d:(g + 1) * d], func=AF.Square,
                                                accum_out=st[:, stcol + g:stcol + g + 1]),
             reads=srck, writes=["tmp", sk + "_%d" % g])
    sks = [sk + "_%d" % g for g in range(G)]
    S.op("dve", lambda e: e.tensor_scalar(out=stv, in0=stv, scalar1=1.0 / d, scalar2=EPS, op0=ALU.mult, op1=ALU.add),
         reads=sks, writes=[sk])
    S.op("act", lambda e: e.activation(out=stv, in_=stv, func=AF.Ln), reads=[sk], writes=[sk])
    S.op("act", lambda e: e.activation(out=stv, in_=stv, func=AF.Exp, scale=-0.5), reads=[sk], writes=[sk])
    for g in range(G):
        S.op("dve", lambda e, g=g: e.scalar_tensor_tensor(out=out[:, g * d:(g + 1) * d], in0=src[:, g * d:(g + 1) * d],
                                                          scalar=st[:, stcol + g:stcol + g + 1], in1=gain, op0=ALU.mult, op1=ALU.mult),
             reads=srck + [sk, gainkey], writes=[outkey])


def rope_groups(S, tag, src, srckey, G, d, off, cs, cskey, out, outkey, tmp, dout=None):
    tk = "rtmp"
    srck = list(srckey) if isinstance(srckey, (list, tuple)) else [srckey]
    sv = src.rearrange("p (g d) -> p g d", g=G)
    ov = out.rearrange("p (g d) -> p g d", g=G) if dout is None else out.rearrange("p (g d) -> p g d", d=dout)
    t1 = sv[:, :, off:off + 16]
    t2 = sv[:, :, off + 16:off + 32]
    c = cs[:, 0:G * 16].rearrange("p (g d) -> p g d", g=G)
    s = cs[:, 64:64 + G * 16].rearrange("p (g d) -> p g d", g=G)
    a = tmp[:, 0:G * 16].rearrange("p (g d) -> p g d", g=G)
    b = tmp[:, 64:64 + G * 16].rearrange("p (g d) -> p g d", g=G)
    o1 = ov[:, :, off:off + 16]
    o2 = ov[:, :, off + 16:off + 32]
    S.op("dve", lambda e: e.tensor_tensor(out=a, in0=t1, in1=c, op=ALU.mult), reads=srck + [cskey], writes=[tk + "a"])
    S.op("dve", lambda e: e.tensor_tensor(out=b, in0=t2, in1=s, op=ALU.mult), reads=srck + [cskey], writes=[tk + "b"])
    S.op("dve", lambda e: e.tensor_tensor(out=o1, in0=a, in1=b, op=ALU.subtract), reads=[tk + "a", tk + "b"], writes=[outkey])
    S.op("dve", lambda e: e.tensor_tensor(out=a, in0=t1, in1=s, op=ALU.mult), reads=srck + [cskey], writes=[tk + "a"])
    S.op("dve", lambda e: e.tensor_tensor(out=b, in0=t2, in1=c, op=ALU.mult), reads=srck + [cskey], writes=[tk + "b"])
    S.op("dve", lambda e: e.tensor_tensor(out=o2, in0=a, in1=b, op=ALU.add), reads=[tk + "a", tk + "b"], writes=[outkey])


def emit_M1(nc, x_full, w_in_own, ln, g_qa, w_uq, g_kva, w_ukv, g_qn, g_kn, dg_qn, dg_kn, cs_tab, ident_d,
            QT, KT, V, DQT, DKT, DV, U, HG, ntok=SEQ):
    PFX = "m1_" + TAG[0]
    ntile = ntok // 128
    with contextlib.ExitStack() as st:
        T = lambda name, shape, dt: st.enter_context(nc.sbuf_tensor(PFX + name, shape, dt))
        Win = T("Win", [128, 8, NOWN], BF16)
        Wuq = T("Wuq", [128, 3, 192], BF16)
        Wukv = T("Wukv", [128, 1, 256], BF16)
        stage = [T("stg0", [128, 2048], F32), T("stg1", [128, 2048], F32)]
        gB = T("gB", [128, D], F32)
        gqa = T("gqa", [128, 384], F32)
        gkva = T("gkva", [128, 128], F32)
        gqn = T("gqn", [128, 96], F32)
        gkn = T("gkn", [128, 96], F32)
        dgq = T("dgq", [128, 32], F32)
        dgk = T("dgk", [128, 32], F32)
        idf = T("idf", [128, 128], F32)
        idb = T("idb", [128, 128], BF16)
        xt = [T("xt0", [128, D], F32), T("xt1", [128, D], F32)]
        cs = [T("cs0", [128, 128], F32), T("cs1", [128, 128], F32)]
        hb = T("hb", [128, D], BF16)
        hT = T("hT", [128, 8, 128], BF16)
        z = T("z", [128, NOWN], F32)
        junk = T("junk", [128, D], BF16)
        stat = T("stat", [128, 32], F32)
        tmp = T("tmp", [128, 384], BF16)
        rtmp = T("rtmp", [128, 128], F32)
        cqn = T("cqn", [128, 384], BF16)
        ckvn = T("ckvn", [128, 128], BF16)
        cT = T("cT", [128, 4, 128], BF16)
        qup = T("qup", [128, 192], F32)
        kcat = T("kcat", [128, 192], F32)
        qn = T("qn", [128, 192], F32)
        kn = T("kn", [128, 192], F32)
        qf = T("qf", [128, 256], BF16)
        kf = T("kf", [128, 256], BF16)
        dqn = T("dqn", [128, 128], F32)
        dkn = T("dkn", [128, 128], F32)
        dqf = T("dqf", [128, 128], BF16)
        dkf = T("dkf", [128, 128], BF16)
        vb = T("vb", [128, 128], BF16)
        dvb = T("dvb", [128, 128], BF16)
        QTs = T("QTs", [128, 2, 512], BF16)
        KTs = T("KTs", [128, 2, 512], BF16)
        DQTs = T("DQTs", [128, 512], BF16)
        DKTs = T("DKTs", [128, 512], BF16)
        pT = st.enter_context(nc.psum_tensor(PFX + "pT", [128, 8, 128], BF16))
        pZ = [st.enter_context(nc.psum_tensor(PFX + "pZ%d" % i, [128, 512], F32)) for i in range(4)]
        pU = st.enter_context(nc.psum_tensor(PFX + "pU", [128, 512], F32))
        pX = st.enter_context(nc.psum_tensor(PFX + "pX", [128, 8, 128], BF16))
        pY = st.enter_context(nc.psum_tensor(PFX + "pY", [128, 8, 128], BF16))
        S = Sched(nc)
        S.dma("act", idf[:], ident_d, "ld_c", writes=["idf"])
        S.op("dve", lambda e: e.tensor_copy(out=idb[:], in_=idf[:]), reads=["idf"], writes=["idb"])

        def bload(tile, vec, key):
            S.dma("act", tile[:], vec.rearrange("(o d) -> o d", o=1).partition_broadcast(128), "ld_c", writes=[key])
        bload(gB, ln, "gB_m1")
        bload(gqa, g_qa, "gqa")
        bload(gkva, g_kva, "gkva")
        bload(gqn, g_qn, "gqn")
        bload(gkn, g_kn, "gkn")
        bload(dgq, dg_qn, "dgq")
        bload(dgk, dg_kn, "dgk")
        load_w_bf16(S, nc, st, "in", w_in_own, 8, NOWN, Win, stage, nblk=212)
        load_w_bf16(S, nc, st, "uq", w_uq, 3, 192, Wuq, stage, nblk=192)
        load_w_bf16(S, nc, st, "ukv", w_ukv, 1, 256, Wukv, stage, nblk=256)

        if callable(x_full):
            class _XV:
                def __getitem__(self, t):
                    return x_full(t)
            xv = _XV()
        else:
            xv = x_full.rearrange("(t p) d -> t p d", p=128)
        csv = cs_tab.rearrange("(t p) d -> t p d", p=128)
        banks = [(0, 512), (512, 1024), (1024, 1536), (1536, NOWN)]
        S.dma("sp", xt[0][:], xv[0], "ld_x0", writes=["xt0"])
        S.dma("sp", cs[0][:], csv[0], "ld_cs0", writes=["cs0"])
        for t in range(ntile):
            sl = t % 2
            xk = "xt%d" % sl
            ck = "cs%d" % sl
            if t + 1 < ntile:
                S.dma("sp", xt[1 - sl][:], xv[t + 1], "ld_x%d" % (1 - sl), writes=["xt%d" % (1 - sl)])
                S.dma("sp", cs[1 - sl][:], csv[t + 1], "ld_cs%d" % (1 - sl), writes=["cs%d" % (1 - sl)])
            rmsnorm_tile(S, "m1", xt[sl][:], xk, gB, hb[:], "hb", junk, stat, 31)
            for c in range(8):
                S.op("pe", lambda e, c=c: e.transpose(out=pT[:, c, :], in_=hb[:, c * 128:(c + 1) * 128], identity=idb[:]),
                     reads=["hb", "idb"], writes=["pT"] if c in (0, 7) else [])
            S.op("dve", lambda e: e.tensor_copy(out=hT[:], in_=pT[:]), reads=["pT"], writes=["hT"])
            for bi, (c0, c1) in enumerate(banks):
                for c in range(8):
                    S.op("pe", lambda e, c=c, bi=bi, c0=c0, c1=c1: e.matmul(out=pZ[bi][:, 0:c1 - c0], lhsT=hT[:, c, :], rhs=Win[:, c, c0:c1],
                                                                          start=(c == 0), stop=(c == 7)),
                         reads=["hT", "w_in"], writes=["pZ%d" % bi] if c in (0, 7) else [])
                eng = "act" if bi % 2 == 0 else "dve"
                if eng == "act":
                    S.op("act", lambda e, bi=bi, c0=c0, c1=c1: e.activation(out=z[:, c0:c1], in_=pZ[bi][:, 0:c1 - c0], func=AF.Copy),
                         reads=["pZ%d" % bi], writes=["z%d" % bi])
                else:
                    S.op("dve", lambda e, bi=bi, c0=c0, c1=c1: e.tensor_copy(out=z[:, c0:c1], in_=pZ[bi][:, 0:c1 - c0]),
                         reads=["pZ%d" % bi], writes=["z%d" % bi])
            if LIM < 1:
                continue
            S.dma("pool", U[t * 128:(t + 1) * 128, :], z[:, 544:672], "st_u", reads=["z1"], final=True)
            S.dma("pool", HG[t * 128:(t + 1) * 128, :], z[:, 1056:1696], "st_hg", reads=["z2", "z3"], final=True)
            if LIM < 2:
                continue
            rms_groups(S, "cq", z[:, 0:384], "z0", 1, 384, gqa[:], "gqa", cqn[:], "cqn", tmp, stat, 0)
            if SUB < 2:
                continue
            rms_groups(S, "ckv", z[:, 384:512], "z0", 1, 128, gkva[:], "gkva", ckvn[:], "ckvn", tmp, stat, 1)
            for c in range(3):
                S.op("pe", lambda e, c=c: e.transpose(out=pX[:, c, :], in_=cqn[:, c * 128:(c + 1) * 128], identity=idb[:]),
                     reads=["cqn", "idb"], writes=["pX"])
            S.op("pe", lambda e: e.transpose(out=pX[:, 3, :], in_=ckvn[:], identity=idb[:]), reads=["ckvn", "idb"], writes=["pX"])
            S.op("dve", lambda e: e.tensor_copy(out=cT[:], in_=pX[:, 0:4, :]), reads=["pX"], writes=["cT"])
            if SUB < 3:
                continue
            for c in range(3):
                S.op("pe", lambda e, c=c: e.matmul(out=pU[:, 0:192], lhsT=cT[:, c, :], rhs=Wuq[:, c, :], start=(c == 0), stop=(c == 2)),
                     reads=["cT", "w_uq"], writes=["pUq"])
            S.op("pe", lambda e: e.matmul(out=pU[:, 192:448], lhsT=cT[:, 3, :], rhs=Wukv[:, 0, :], start=True, stop=True),
                 reads=["cT", "w_ukv"], writes=["pUkv"])
            S.op("act", lambda e: e.activation(out=qup[:], in_=pU[:, 0:192], func=AF.Copy), reads=["pUq"], writes=["qup"])
            if SUB < 4:
                continue
            kv = pU[:, 192:448].rearrange("p (h d) -> p h d", h=2)
            kc3 = kcat[:].rearrange("p (h d) -> p h d", h=2)
            for h in range(2):
                S.op("act", lambda e, h=h: e.activation(out=kcat[:, h * 96:h * 96 + 64], in_=pU[:, 192 + h * 128:192 + h * 128 + 64], func=AF.Copy),
                     reads=["pUkv"], writes=["kcat_a"])
                S.op("act", lambda e, h=h: e.activation(out=vb[:, h * 64:(h + 1) * 64], in_=pU[:, 192 + h * 128 + 64:192 + (h + 1) * 128], func=AF.Copy),
                     reads=["pUkv"], writes=["vb"])
            if SUB < 5:
                continue
            for h in range(2):
                S.op("pool", lambda e, h=h: e.tensor_copy(out=kcat[:, h * 96 + 64:h * 96 + 96], in_=z[:, 512:544]),
                     reads=["z1"], writes=["kcat_b"])
            if LIM < 3:
                continue
            S.dma("pool", V[t * 128:(t + 1) * 128, :], vb[:], "st_v", reads=["vb"], final=True)
            rms_groups(S, "q", qup[:], "qup", 2, 96, gqn[:], "gqn", qn[:], "qn", tmp, stat, 2)
            rms_groups(S, "k", kcat[:], ["kcat_a", "kcat_b"], 2, 96, gkn[:], "gkn", kn[:], "kn", tmp, stat, 4)
            qn3 = qn[:].rearrange("p (h d) -> p h d", h=2)
            kn3 = kn[:].rearrange("p (h d) -> p h d", h=2)
            qf3 = qf[:].rearrange("p (h d) -> p h d", h=2)
            kf3 = kf[:].rearrange("p (h d) -> p h d", h=2)
            if t == 0:
                S.op("pool", lambda e: e.memset(qf[:], 0.0), writes=["qf_a", "qf_b"])
                S.op("pool", lambda e: e.memset(kf[:], 0.0), writes=["kf_a", "kf_b"])
            S.op("pool", lambda e: e.tensor_copy(out=qf3[:, :, 0:64], in_=qn3[:, :, 0:64]), reads=["qn"], writes=["qf_a"])
            S.op("pool", lambda e: e.tensor_copy(out=kf3[:, :, 0:64], in_=kn3[:, :, 0:64]), reads=["kn"], writes=["kf_a"])
            rope_groups(S, "q", qn[:], "qn", 2, 96, 64, cs[sl], ck, qf[:], "qf_b", rtmp, dout=128)
            rope_groups(S, "k", kn[:], "kn", 2, 96, 64, cs[sl], ck, kf[:], "kf_b", rtmp, dout=128)
            if LIM < 4:
                continue
            rms_groups(S, "dq", z[:, 672:800], "z1", 4, 32, dgq[:], "dgq", dqn[:], "dqn", tmp, stat, 8)
            rms_groups(S, "dk", z[:, 800:928], "z1", 4, 32, dgk[:], "dgk", dkn[:], "dkn", tmp, stat, 12)
            rope_groups(S, "dq", dqn[:], "dqn", 4, 32, 0, cs[sl], ck, dqf[:], "dqf", rtmp)
            rope_groups(S, "dk", dkn[:], "dkn", 4, 32, 0, cs[sl], ck, dkf[:], "dkf", rtmp)
            S.op("pool", lambda e: e.tensor_copy(out=dvb[:], in_=z[:, 928:1056]), reads=["z1", "z2"], writes=["dvb"])
            S.dma("pool", DV[t * 128:(t + 1) * 128, :], dvb[:], "st_dv", reads=["dvb"], final=True)
            if LIM < 5:
                continue
            tq = t % 4
            for h in range(2):
                S.op("pe", lambda e, h=h: e.transpose(out=pY[:, h, :], in_=qf[:, h * 128:(h + 1) * 128], identity=idb[:]),
                     reads=["qf_a", "qf_b", "idb"], writes=["pY"])
                S.op("pe", lambda e, h=h: e.transpose(out=pY[:, 2 + h, :], in_=kf[:, h * 128:(h + 1) * 128], identity=idb[:]),
                     reads=["kf_a", "kf_b", "idb"], writes=["pY"])
            S.op("pe", lambda e: e.transpose(out=pY[:, 4, :], in_=dqf[:], identity=idb[:]), reads=["dqf", "idb"], writes=["pY"])
            S.op("pe", lambda e: e.transpose(out=pY[:, 5, :], in_=dkf[:], identity=idb[:]), reads=["dkf", "idb"], writes=["pY"])
            if SUB < 6:
                continue
            for h in range(2):
                S.op("dve", lambda e, tq=tq, h=h: e.tensor_copy(out=QTs[:, h, tq * 128:(tq + 1) * 128], in_=pY[:, h, :]),
                     reads=["pY"], writes=["QTs"])
                S.op("dve", lambda e, tq=tq, h=h: e.tensor_copy(out=KTs[:, h, tq * 128:(tq + 1) * 128], in_=pY[:, 2 + h, :]),
                     reads=["pY"], writes=["KTs"])
            S.op("dve", lambda e, tq=tq: e.tensor_copy(out=DQTs[:, tq * 128:(tq + 1) * 128], in_=pY[:, 4, :]),
                 reads=["pY"], writes=["DQTs"])
            S.op("dve", lambda e, tq=tq: e.tensor_copy(out=DKTs[:, tq * 128:(tq + 1) * 128], in_=pY[:, 5, :]),
                 reads=["pY"], writes=["DKTs"])
            if SUB < 7:
                continue
            if tq == 3:
                t0 = (t - 3) * 128
                for h in range(2):
                    S.dma("sp", QT[h, :, t0:t0 + 512], QTs[:, h, :], "st_q", reads=["QTs"], final=True)
                    S.dma("sp", KT[h, :, t0:t0 + 512], KTs[:, h, :], "st_k", reads=["KTs"], final=True)
                S.dma("sp", DQT.rearrange("h r t -> (h r) t")[:, t0:t0 + 512], DQTs[:], "st_dq", reads=["DQTs"], final=True)
                S.dma("sp", DKT.rearrange("h r t -> (h r) t")[:, t0:t0 + 512], DKTs[:], "st_dk", reads=["DKTs"], final=True)
        S.emit()

import contextlib
import math
import numpy as np
import concourse.bass as bass
import concourse.mybir as mybir

AX = mybir.AxisListType
QB = 512


def _attn_map(S, tag, KT_sb, QT_sb, kkey, qkey, prow0, prows, Vaug, vkey, PT, pS, pO, pOkey, scale, ntok, qb, ctr):
    nch = ntok // 128
    ng = nch // 2
    sls = [(ctr[0] + g) % 2 for g in range(ng)]
    ctr[0] += ng

    def scores(g):
        sl = sls[g]
        for i in range(2):
            kc = g * 2 + i
            S.op("pe", lambda e, kc=kc, i=i, sl=sl: e.matmul(out=pS[sl][:, i * QB:(i + 1) * QB],
                                                             lhsT=KT_sb[prow0:prow0 + prows, kc * 128:(kc + 1) * 128],
                                                             rhs=QT_sb[prow0:prow0 + prows, qb * QB:(qb + 1) * QB], start=True, stop=True),
                 reads=[kkey, qkey], writes=["pS%d_%d" % (sl, i)])

    scores(0)
    for g in range(ng):
        sl = sls[g]
        S.op("act", lambda e, sl=sl: e.activation(out=PT[sl][:], in_=pS[sl][:], func=AF.Exp, scale=scale),
             reads=["pS%d_0" % sl, "pS%d_1" % sl], writes=["PT%d" % sl])
        if g + 1 < ng:
            scores(g + 1)
        for i in range(2):
            kc = g * 2 + i
            S.op("pe", lambda e, kc=kc, i=i, sl=sl: e.matmul(out=pO[0:65, :], lhsT=Vaug[:, kc, :], rhs=PT[sl][:, i * QB:(i + 1) * QB],
                                                             start=(kc == 0), stop=(kc == nch - 1)),
                 reads=["PT%d" % sl, vkey], writes=[pOkey] if kc in (0, nch - 1) else [])


def emit_attn(nc, QT, KT, V, DQT, DKT, DV, lq1, lk1, lq2, lk2, g_sub, oT_out, layer_idx, ntok):
    PFX = "at_" + TAG[0]
    nch = ntok // 128
    nqb = ntok // QB
    lambda_init = 0.8 - 0.6 * math.exp(-0.3 * layer_idx)
    with contextlib.ExitStack() as st:
        T = lambda name, shape, dt: st.enter_context(nc.sbuf_tensor(PFX + name, shape, dt))
        Qs = T("Qs", [128, ntok], BF16)
        Ks = T("Ks", [128, ntok], BF16)
        Va = T("Va", [128, nch, 65], BF16)
        PT = [T("PT0", [128, 2 * QB], BF16), T("PT1", [128, 2 * QB], BF16)]
        O1 = T("O1", [65, QB], F32)
        O2 = T("O2", [65, QB], F32)
        rl = T("rl", [64, QB], F32)
        o1 = T("o1", [64, QB], F32)
        o2 = T("o2", [64, QB], F32)
        sq = T("sq", [64, QB], F32)
        ob = T("ob", [64, QB], BF16)
        sel = T("sel", [65, 64], F32)
        ones = T("ones", [64, 64], F32)
        lt = T("lt", [64, 4, 32], F32)
        lw = T("lw", [64, 8], F32)
        gs = T("gs", [64, 2], F32)
        pS = [st.enter_context(nc.psum_tensor(PFX + "pS%d" % i, [128, 2 * QB], F32)) for i in range(2)]
        pO = [st.enter_context(nc.psum_tensor(PFX + "pO%d" % i, [128, QB], F32)) for i in range(2)]
        pL = st.enter_context(nc.psum_tensor(PFX + "pL", [128, QB], F32))
        S = Sched(nc)
        S.op("pool", lambda e: e.memset(sel[:], 0.0), writes=["sel"])
        S.op("pool", lambda e: e.memset(sel[64:65, :], 1.0), reads=[], writes=["sel"])
        S.op("pool", lambda e: e.memset(ones[:], 1.0), writes=["ones"])
        for i, v in enumerate((lq1, lk1, lq2, lk2)):
            S.dma("act", lt[:, i, :], v.rearrange("(o d) -> o d", o=1).partition_broadcast(64), "ld_c", writes=["lt%d" % i])
        S.dma("act", gs[:, 0:1], g_sub.rearrange("(d o) -> d o", o=1), "ld_c", writes=["gs0"])
        S.op("dve", lambda e: e.tensor_tensor(out=lt[:, 0, :], in0=lt[:, 0, :], in1=lt[:, 1, :], op=ALU.mult), reads=["lt0", "lt1"], writes=["lt0"])
        S.op("dve", lambda e: e.tensor_tensor(out=lt[:, 2, :], in0=lt[:, 2, :], in1=lt[:, 3, :], op=ALU.mult), reads=["lt2", "lt3"], writes=["lt2"])
        S.op("dve", lambda e: e.tensor_reduce(out=lw[:, 0:1], in_=lt[:, 0, :], axis=AX.X, op=ALU.add), reads=["lt0"], writes=["lw0"])
        S.op("dve", lambda e: e.tensor_reduce(out=lw[:, 1:2], in_=lt[:, 2, :], axis=AX.X, op=ALU.add), reads=["lt2"], writes=["lw1"])
        S.op("act", lambda e: e.activation(out=lw[:, 2:4], in_=lw[:, 0:2], func=AF.Exp), reads=["lw0", "lw1"], writes=["lw23"])
        S.op("dve", lambda e: e.tensor_tensor(out=lw[:, 4:5], in0=lw[:, 3:4], in1=lw[:, 2:3], op=ALU.subtract), reads=["lw23"], writes=["lw4"])
        S.op("dve", lambda e: e.tensor_scalar(out=lw[:, 4:5], in0=lw[:, 4:5], scalar1=-lambda_init, scalar2=None, op0=ALU.add),
             reads=["lw4"], writes=["lw4"])
        S.op("dve", lambda e: e.tensor_scalar(out=gs[:, 1:2], in0=gs[:, 0:1], scalar1=(1.0 - lambda_init), scalar2=None, op0=ALU.mult),
             reads=["gs0"], writes=["gs1"])
        ctr = [0]
        ob_n = [0]

        def finalize_l(src, srckey, dst, dstkey):
            S.op("pe", lambda e: e.matmul(out=pL[0:64, :], lhsT=sel[:], rhs=src[:], start=True, stop=True), reads=["sel", srckey], writes=["pL"])
            S.op("dve", lambda e: e.reciprocal(out=rl[:], in_=pL[0:64, :]), reads=["pL"], writes=["rl"])
            S.op("dve", lambda e: e.tensor_tensor(out=dst[:], in0=src[0:64, :], in1=rl[:], op=ALU.mult), reads=[srckey, "rl"], writes=[dstkey])

        for h in range(2):
            S.dma("sp", Qs[:], QT[h], "ld_q", writes=["Qs"])
            S.dma("sp", Ks[:], KT[h], "ld_k", writes=["Ks"])
            S.op("pool", lambda e: e.memset(Va[:], 1.0), writes=["Va"])
            _vv = V[:, h * 64:(h + 1) * 64].rearrange("(c p) d -> p c d", p=128)
            for _i in range(0, nch, 8):
                S.dma("act", Va[:, _i:_i + 8, 0:64], _vv[:, _i:_i + 8, :], "ld_v", writes=["Va"])
            for qb in range(nqb):
                b = qb % 2
                _attn_map(S, "m", Ks, Qs, "Ks", "Qs", 0, 128, Va, "Va", PT, pS, pO[b], "pO%d" % b, 96 ** -0.5, ntok, qb, ctr)
                S.op("dve", lambda e, b=b: e.tensor_copy(out=O1[:], in_=pO[b][0:65, :]), reads=["pO%d" % b], writes=["O1"])
                finalize_l(O1, "O1", o1, "o1")
                S.op("pool", lambda e: e.tensor_copy(out=ob[:], in_=o1[:]), reads=["o1"], writes=["ob"])
                S.dma("sp", oT_out[h * 64:(h + 1) * 64, qb * QB:(qb + 1) * QB], ob[:], "st_o", reads=["ob"], final=True)
        for h in range(2):
            S.dma("sp", Qs[0:64, :], DQT[h], "ld_q", writes=["Qs"])
            S.dma("sp", Ks[0:64, :], DKT[h], "ld_k", writes=["Ks"])
            S.op("pool", lambda e: e.memset(Va[:], 1.0), writes=["Va"])
            _vv = DV[:, h * 64:(h + 1) * 64].rearrange("(c p) d -> p c d", p=128)
            for _i in range(0, nch, 8):
                S.dma("act", Va[:, _i:_i + 8, 0:64], _vv[:, _i:_i + 8, :], "ld_v", writes=["Va"])
            for qb in range(nqb):
                _attn_map(S, "d1", Ks, Qs, "Ks", "Qs", 0, 32, Va, "Va", PT, pS, pO[0], "pO0", 32 ** -0.5, ntok, qb, ctr)
                _attn_map(S, "d2", Ks, Qs, "Ks", "Qs", 32, 32, Va, "Va", PT, pS, pO[1], "pO1", 32 ** -0.5, ntok, qb, ctr)
                S.op("dve", lambda e: e.tensor_copy(out=O1[:], in_=pO[0][0:65, :]), reads=["pO0"], writes=["O1"])
                S.op("dve", lambda e: e.tensor_copy(out=O2[:], in_=pO[1][0:65, :]), reads=["pO1"], writes=["O2"])
                finalize_l(O1, "O1", o1, "o1")
                finalize_l(O2, "O2", o2, "o2")
                S.op("dve", lambda e: e.scalar_tensor_tensor(out=o1[:], in0=o2[:], scalar=lw[:, 4:5], in1=o1[:], op0=ALU.mult, op1=ALU.add),
                     reads=["o1", "o2", "lw4"], writes=["o1"])
                S.op("pool", lambda e: e.tensor_tensor(out=sq[:], in0=o1[:], in1=o1[:], op=ALU.mult), reads=["o1"], writes=["sq"])
                S.op("pe", lambda e: e.matmul(out=pL[0:64, :], lhsT=ones[:], rhs=sq[:], start=True, stop=True), reads=["ones", "sq"], writes=["pL"])
                S.op("dve", lambda e: e.tensor_scalar(out=sq[:], in0=pL[0:64, :], scalar1=1.0 / 64, scalar2=EPS, op0=ALU.mult, op1=ALU.add),
                     reads=["pL"], writes=["sq"])
                S.op("act", lambda e: e.activation(out=sq[:], in_=sq[:], func=AF.Ln), reads=["sq"], writes=["sq"])
                S.op("act", lambda e: e.activation(out=sq[:], in_=sq[:], func=AF.Exp, scale=-0.5), reads=["sq"], writes=["sq"])
                S.op("dve", lambda e: e.scalar_tensor_tensor(out=ob[:], in0=o1[:], scalar=gs[:, 1:2], in1=sq[:], op0=ALU.mult, op1=ALU.mult),
                     reads=["o1", "sq", "gs1"], writes=["ob"])
                S.dma("sp", oT_out[256 + h * 64:256 + (h + 1) * 64, qb * QB:(qb + 1) * QB], ob[:], "st_o", reads=["ob"], final=True)
        S.emit()

import contextlib
import math
import numpy as np
import concourse.bass as bass
import concourse.mybir as mybir

SEQ = 8192
C_C128, C_S128N, C_C128N, C_TC, C_TS, C_CS64, C_C64S, C_S64S = 0, 128, 256, 384, 448, 512, 768, 832
NFC = 896


def fnet_consts():
    f = np.zeros((128, NFC), np.float64)
    a = np.arange(128)
    ang = 2 * np.pi * np.outer(a, a) / 128.0
    f[:, C_C128:C_C128 + 128] = np.cos(ang)
    f[:, C_S128N:C_S128N + 128] = -np.sin(ang)
    f[:, C_C128N:C_C128N + 128] = -np.cos(ang)
    tw = 2 * np.pi * np.outer(a, np.arange(64)) / 8192.0
    f[:, C_TC:C_TC + 64] = np.cos(tw)
    f[:, C_TS:C_TS + 64] = np.sin(tw)
    c = np.arange(64)
    a64 = 2 * np.pi * np.outer(c, c) / 64.0
    for g in range(2):
        f[g * 64:(g + 1) * 64, C_CS64 + g * 64:C_CS64 + (g + 1) * 64] = np.cos(a64)
        f[g * 64:(g + 1) * 64, C_CS64 + 128 + g * 64:C_CS64 + 128 + (g + 1) * 64] = np.sin(a64)
    sc = 1.0 / math.sqrt(8192.0 * 64.0)
    f[0:64, C_C64S:C_C64S + 64] = np.cos(a64) * sc
    f[0:64, C_S64S:C_S64S + 64] = np.sin(a64) * sc
    return f.astype(np.float32)


def emit_fnet(nc, U, fconst, ident_d, UCS, XPd, Yd, oT_out):
    PFX = "fn_" + TAG[0]
    with contextlib.ExitStack() as st:
        T = lambda name, shape, dt: st.enter_context(nc.sbuf_tensor(PFX + name, shape, dt))
        fc = T("fc", [128, NFC], F32)
        fb = T("fb", [128, NFC], BF16)
        idf = T("idf", [128, 128], F32)
        idb = T("idb", [128, 128], BF16)
        ut = [T("ut0", [128, 4, 128], F32), T("ut1", [128, 4, 128], F32)]
        ub = T("ub", [128, 4, 128], BF16)
        uT = T("uT", [128, 4, 128], BF16)
        ucs = [T("ucs0", [128, 4, 256], BF16), T("ucs1", [128, 4, 256], BF16)]
        WB = T("WB", [128, 64, 256], BF16)
        XP = T("XP", [128, 64, 256], BF16)
        XQ = T("XQ", [64, 128, 256], BF16)
        YS = T("YS", [64, 128, 128], BF16)
        t1 = [T("t1a", [128, 128], F32), T("t1b", [128, 128], F32)]
        t2 = [T("t2a", [128, 128], F32), T("t2b", [128, 128], F32)]
        yt = [T("yt0", [128, 4, 128], BF16), T("yt1", [128, 4, 128], BF16)]
        oS = [T("oS0", [128, 512], BF16), T("oS1", [128, 512], BF16)]
        pT = st.enter_context(nc.psum_tensor(PFX + "pT", [128, 8, 128], BF16))
        pA = [st.enter_context(nc.psum_tensor(PFX + "pA%d" % i, [128, 512], F32)) for i in range(2)]
        pR = [st.enter_context(nc.psum_tensor(PFX + "pR%d" % i, [128, 512], F32)) for i in range(2)]
        pI = [st.enter_context(nc.psum_tensor(PFX + "pI%d" % i, [128, 512], F32)) for i in range(2)]
        S = Sched(nc)
        S.dma("act", idf[:], ident_d, "ld_c", writes=["idf"])
        S.op("dve", lambda e: e.tensor_copy(out=idb[:], in_=idf[:]), reads=["idf"], writes=["idb"])
        S.dma("act", fc[:], fconst, "ld_c", writes=["fc"])
        S.op("dve", lambda e: e.tensor_copy(out=fb[:], in_=fc[:]), reads=["fc"], writes=["fb"])
        Uv = U.rearrange("(n j p) c -> n p j c", p=128, j=4)
        UCSv = UCS.rearrange("(n j p) c -> n p j c", p=128, j=4)
        for n in range(SEQ // 512):
            sl = n % 2
            S.dma("sp", ut[sl][:], Uv[n], "ld_u%d" % sl, writes=["ut%d" % sl])
            S.op("pool", lambda e, sl=sl: e.tensor_copy(out=ub[:], in_=ut[sl][:]), reads=["ut%d" % sl], writes=["ub"])
            for j in range(4):
                S.op("pe", lambda e, j=j: e.transpose(out=pT[:, j, :], in_=ub[:, j, :], identity=idb[:]), reads=["ub", "idb"], writes=["pT"])
            S.op("dve", lambda e: e.tensor_copy(out=uT[:], in_=pT[:, 0:4, :]), reads=["pT"], writes=["uT"])
            for j in range(4):
                b = j % 2
                S.op("pe", lambda e, j=j, b=b: e.matmul(out=pA[b][:, 0:256], lhsT=uT[:, j, :], rhs=fb[:, C_CS64:C_CS64 + 256], start=True, stop=True),
                     reads=["uT", "fb"], writes=["pA%d" % b])
                S.op("dve", lambda e, j=j, b=b, sl=sl: e.tensor_copy(out=ucs[sl][:, j, :], in_=pA[b][:, 0:256]),
                     reads=["pA%d" % b], writes=["ucs%d" % sl])
            S.dma("sp", UCSv[n], ucs[sl][:], "st_ucs", reads=["ucs%d" % sl], final=True)
        st_ucs_done = S.chan_ops["st_ucs"][-1]
        S.dma("sp", WB[:], UCS.rearrange("(a b) c -> a b c", b=64), "ld_wb", writes=["WB"], deps=list(S.chan_ops["st_ucs"]))
        k = 0
        for g in range(16):
            b = g % 2
            for a in range(4):
                s2m = g * 4 + a
                oR = pR[b][:, a * 128:(a + 1) * 128]
                oI = pI[b][:, a * 128:(a + 1) * 128]
                rc = WB[:, s2m, 0:128]
                rs = WB[:, s2m, 128:256]
                S.op("pe", lambda e, rc=rc, oR=oR: e.matmul(out=oR, lhsT=fb[:, C_C128:C_C128 + 128], rhs=rc, start=True, stop=False),
                     reads=["WB", "fb"], writes=["pR%d" % b])
                S.op("pe", lambda e, rs=rs, oR=oR: e.matmul(out=oR, lhsT=fb[:, C_S128N:C_S128N + 128], rhs=rs, start=False, stop=True),
                     reads=["WB", "fb"], writes=["pR%d" % b])
                S.op("pe", lambda e, rc=rc, oI=oI: e.matmul(out=oI, lhsT=fb[:, C_S128N:C_S128N + 128], rhs=rc, start=True, stop=False),
                     reads=["WB", "fb"], writes=["pI%d" % b])
                S.op("pe", lambda e, rs=rs, oI=oI: e.matmul(out=oI, lhsT=fb[:, C_C128N:C_C128N + 128], rhs=rs, start=False, stop=True),
                     reads=["WB", "fb"], writes=["pI%d" % b])
            for a in range(4):
                s2 = g * 4 + a
                q = k % 2
                k += 1
                xr = pR[b][:, a * 128:(a + 1) * 128]
                xi = pI[b][:, a * 128:(a + 1) * 128]
                S.op("dve", lambda e, xi=xi, q=q, s2=s2: e.tensor_scalar(out=t1[q][:], in0=xi, scalar1=fc[:, C_TS + s2:C_TS + s2 + 1], scalar2=None, op0=ALU.mult),
                     reads=["pI%d" % b, "fc"], writes=["t1%d" % q])
                S.op("dve", lambda e, xr=xr, q=q, s2=s2: e.scalar_tensor_tensor(out=XP[:, s2, 0:128], in0=xr, scalar=fc[:, C_TC + s2:C_TC + s2 + 1],
                                                                                in1=t1[q][:], op0=ALU.mult, op1=ALU.add),
                     reads=["pR%d" % b, "t1%d" % q, "fc"], writes=["XP"])
                S.op("dve", lambda e, xr=xr, q=q, s2=s2: e.tensor_scalar(out=t2[q][:], in0=xr, scalar1=fc[:, C_TS + s2:C_TS + s2 + 1], scalar2=None, op0=ALU.mult),
                     reads=["pR%d" % b, "fc"], writes=["t2%d" % q])
                S.op("dve", lambda e, xi=xi, q=q, s2=s2: e.scalar_tensor_tensor(out=XP[:, s2, 128:256], in0=xi, scalar=fc[:, C_TC + s2:C_TC + s2 + 1],
                                                                                in1=t2[q][:], op0=ALU.mult, op1=ALU.subtract),
                     reads=["pI%d" % b, "t2%d" % q, "fc"], writes=["XP"])
        S.dma("sp", XPd, XP[:], "st_xp", reads=["XP"], final=True)
        XPv = XPd.rearrange("a b c -> b a c")
        for i in range(8):
            S.dma("sp" if i % 2 == 0 else "act", XQ[:, i * 16:(i + 1) * 16, :], XPv[:, i * 16:(i + 1) * 16, :], "ld_xq%d" % (i % 2), writes=["XQ"],
                  deps=[S.chan_ops["st_xp"][-1]])
        for g in range(32):
            b = g % 2
            for a in range(4):
                s1m = g * 4 + a
                o = pA[b][0:64, a * 128:(a + 1) * 128]
                S.op("pe", lambda e, s1m=s1m, o=o: e.matmul(out=o, lhsT=fb[0:64, C_C64S:C_C64S + 64], rhs=XQ[:, s1m, 0:128], start=True, stop=False),
                     reads=["XQ", "fb"], writes=["pA%d" % b])
                S.op("pe", lambda e, s1m=s1m, o=o: e.matmul(out=o, lhsT=fb[0:64, C_S64S:C_S64S + 64], rhs=XQ[:, s1m, 128:256], start=False, stop=True),
                     reads=["XQ", "fb"], writes=["pA%d" % b])
            S.op("dve", lambda e, g=g, b=b: e.tensor_copy(out=YS[:, g * 4:(g + 1) * 4, :].rearrange("p a c -> p (a c)"), in_=pA[b][0:64, :]),
                 reads=["pA%d" % b], writes=["YS"])
        S.dma("sp", Yd.rearrange("(b a) c -> b a c", a=128), YS[:], "st_y", reads=["YS"], final=True)
        Yv = Yd.rearrange("(n j p) c -> n p j c", p=128, j=4)
        for n in range(SEQ // 512):
            sl = n % 2
            S.dma("act", yt[sl][:], Yv[n], "ld_y%d" % sl, writes=["yt%d" % sl], deps=[S.chan_ops["st_y"][-1]])
            for j in range(4):
                S.op("pe", lambda e, j=j, sl=sl: e.transpose(out=pT[:, 4 + j, :], in_=yt[sl][:, j, :], identity=idb[:]),
                     reads=["yt%d" % sl, "idb"], writes=["pTb"])
            S.op("dve", lambda e, sl=sl: e.tensor_copy(out=oS[sl][:].rearrange("p (j t) -> p j t", j=4), in_=pT[:, 4:8, :]),
                 reads=["pTb"], writes=["oS%d" % sl])
            S.dma("sp", oT_out[128:256, n * 512:(n + 1) * 512], oS[sl][:], "st_o", reads=["oS%d" % sl], final=True)
        S.emit()

import contextlib
import math
import numpy as np
import concourse.bass as bass
import concourse.mybir as mybir

H_MTF, H_MTB, H_INDF, H_INDB, H_MASKF, H_MASKB = 0, 128, 256, 260, 264, 392
NHC = 520


def hgrn_consts():
    f = np.zeros((128, NHC), np.float32)
    s = np.arange(128)[:, None]
    t = np.arange(128)[None, :]
    same = (s // 64) == (t // 64)
    ls, lt = s % 64, t % 64
    f[:, H_MTF:H_MTF + 128] = same * ((ls <= lt).astype(np.float32) - (ls <= 31).astype(np.float32))
    f[:, H_MTB:H_MTB + 128] = same * ((ls >= lt).astype(np.float32) - (ls >= 32).astype(np.float32))
    sv = np.arange(128)
    for a in range(2):
        inch = (sv // 64) == a
        f[:, H_INDF + 2 * a] = inch & ((sv % 64) <= 31)
        f[:, H_INDF + 2 * a + 1] = inch & ((sv % 64) > 31)
        f[:, H_INDB + 2 * a] = inch & ((sv % 64) >= 32)
        f[:, H_INDB + 2 * a + 1] = inch & ((sv % 64) < 32)
    s6 = np.arange(64)[:, None]
    t6 = np.arange(64)[None, :]
    f[0:64, H_MASKF:H_MASKF + 128] = np.tile((s6 <= t6).astype(np.float32), (1, 2))
    f[0:64, H_MASKB:H_MASKB + 128] = np.tile((s6 >= t6).astype(np.float32), (1, 2))
    return f


def emit_hgrn(nc, HG, lbl, g_on, hconst, ident_d, oT_out, layer_idx, ntok):
    PFX = "hg_" + TAG[0]
    ntile = ntok // 128
    n64 = ntok // 64
    with contextlib.ExitStack() as st:
        T = lambda name, shape, dt: st.enter_context(nc.sbuf_tensor(PFX + name, shape, dt))
        hc = T("hc", [128, NHC], F32)
        idf = T("idf", [128, 128], F32)
        idb = T("idb", [128, 128], BF16)
        la = T("la", [128, 2, 128], F32)
        lw = T("lw", [128, 6, 128], F32)
        lb = T("lb", [128, 128], F32)
        oml = T("oml", [128, 128], F32)
        gon = T("gon", [64, 64], F32)
        LF = T("LF", [128, ntile, 128], F32)
        KK = T("KK", [128, ntile, 128], BF16)
        U1 = T("U1", [128, n64], F32)
        U2 = T("U2", [128, n64], F32)
        G = T("G", [128, n64], F32)
        OA = T("OA", [64, n64, 128], F32)
        zt = [T("zt0", [128, 4, 128], F32), T("zt1", [128, 4, 128], F32)]
        w1 = T("w1", [128, 4, 128], F32)
        w2 = T("w2", [128, 4, 128], F32)
        lb4 = T("lb4", [128, 4, 128], F32)
        oml4 = T("oml4", [128, 4, 128], F32)
        qv = [T("qv0", [128, 256], F32), T("qv1", [128, 256], F32)]
        v64 = [T("v64a", [64, 2, 128], F32), T("v64b", [64, 2, 128], F32)]
        Vb2 = [T("Vb0", [128, 128], BF16), T("Vb1", [128, 128], BF16)]
        V62 = [T("V60", [64, 2, 128], BF16), T("V61", [64, 2, 128], BF16)]
        EA = T("EA", [128, 128], F32)
        EnA = T("EnA", [128, 128], F32)
        Qt = T("Qt", [128, 128], BF16)
        Kt2 = [T("Kt0", [128, 128], BF16), T("Kt1", [128, 128], BF16)]
        QKT2 = [T("QKT0", [128, 2, 128], BF16), T("QKT1", [128, 2, 128], BF16)]
        scT2 = [T("scT0", [64, 128], BF16), T("scT1", [64, 128], BF16)]
        Tt = T("Tt", [128, 64], F32)
        Spf = T("Spf", [128, 64], F32)
        Spb = T("Spb", [128, 64], BF16)
        hgt = T("hgt", [64, 8, 128], F32)
        sg = T("sg", [64, 8, 128], F32)
        yy = T("yy", [64, 128], F32)
        yb = T("yb", [64, 128], BF16)
        junk = T("junk", [64, 64], F32)
        fst = T("fst", [64, 4], F32)
        oS = [T("oS0", [128, 512], BF16), T("oS1", [128, 512], BF16)]
        pu = st.enter_context(nc.psum_tensor(PFX + "pu", [128, 512], F32))
        pA = st.enter_context(nc.psum_tensor(PFX + "pA", [128, 512], F32))
        pT = st.enter_context(nc.psum_tensor(PFX + "pT", [128, 8, 128], BF16))
        pSc2 = [st.enter_context(nc.psum_tensor(PFX + "pSc%d" % i, [128, 512], F32)) for i in range(2)]
        pO = st.enter_context(nc.psum_tensor(PFX + "pO", [128, 512], F32))
        pP = st.enter_context(nc.psum_tensor(PFX + "pP", [128, 512], F32))
        pF = st.enter_context(nc.psum_tensor(PFX + "pF", [128, 8, 128], BF16))
        S = Sched(nc)
        S.dma("act", idf[:], ident_d, "ld_c", writes=["idf"])
        S.op("dve", lambda e: e.tensor_copy(out=idb[:], in_=idf[:]), reads=["idf"], writes=["idb"])
        S.dma("act", hc[:], hconst, "ld_c", writes=["hc"])
        S.dma("act", gon[:], g_on.rearrange("(o d) -> o d", o=1).partition_broadcast(64), "ld_c", writes=["gon"])
        HGv = HG.rearrange("(t p) d -> t p d", p=128)
        HG6 = HG.rearrange("(t a p) d -> t p a d", p=64, a=2)

        for d in range(2):
            fwd = (d == 0)
            zc0 = 256 if fwd else 384
            MT = hc[:, H_MTF:H_MTF + 128] if fwd else hc[:, H_MTB:H_MTB + 128]
            IND = hc[:, H_INDF:H_INDF + 4] if fwd else hc[:, H_INDB:H_INDB + 4]
            MASK = hc[0:64, H_MASKF:H_MASKF + 128] if fwd else hc[0:64, H_MASKB:H_MASKB + 128]
            for l in range(2):
                S.dma("act", la[:, l, :], lbl[d, l].rearrange("(o c) -> o c", o=1).partition_broadcast(128), "ld_c", writes=["la%d" % l])
            S.op("act", lambda e: e.activation(out=lw[:, 0, :], in_=la[:, 0, :], func=AF.Exp), reads=["la0"], writes=["lw0"])
            S.op("act", lambda e: e.activation(out=lw[:, 1, :], in_=la[:, 1, :], func=AF.Exp), reads=["la1"], writes=["lw1"])
            S.op("dve", lambda e: e.tensor_tensor(out=lw[:, 2, :], in0=lw[:, 0, :], in1=lw[:, 1, :], op=ALU.add), reads=["lw0", "lw1"], writes=["lw2"])
            S.op("dve", lambda e: e.reciprocal(out=lw[:, 2, :], in_=lw[:, 2, :]), reads=["lw2"], writes=["lw2"])
            S.op("dve", lambda e: e.tensor_tensor(out=lw[:, 3, :], in0=lw[:, 0, :], in1=lw[:, 2, :], op=ALU.mult), reads=["lw0", "lw2"], writes=["lw3"])
            S.op("dve", lambda e: e.tensor_tensor(out=lw[:, 4, :], in0=lw[:, 1, :], in1=lw[:, 2, :], op=ALU.mult), reads=["lw1", "lw2"], writes=["lw4"])
            if layer_idx == 0:
                S.op("dve", lambda e: e.tensor_tensor(out=lb[:], in0=lw[:, 3, :], in1=lw[:, 3, :], op=ALU.subtract), reads=["lw3"], writes=["lb"])
            else:
                S.op("dve", lambda e: e.tensor_tensor(out=lw[:, 5, :], in0=lw[:, 3, :], in1=lw[:, 4, :], op=ALU.add), reads=["lw3", "lw4"], writes=["lw5"])
                S.op("dve", lambda e: e.tensor_tensor(out=lb[:], in0=lw[:, 5, :], in1=lw[:, 3, :], op=ALU.subtract), reads=["lw5", "lw3"], writes=["lb"])
            S.op("dve", lambda e: e.tensor_scalar(out=oml[:], in0=lb[:], scalar1=-1.0, scalar2=1.0, op0=ALU.mult, op1=ALU.add), reads=["lb"], writes=["oml"])
            for j in range(4):
                S.op("pool", lambda e, j=j: e.tensor_copy(out=lb4[:, j, :], in_=lb[:]), reads=["lb"], writes=["lb4"])
                S.op("pool", lambda e, j=j: e.tensor_copy(out=oml4[:, j, :], in_=oml[:]), reads=["oml"], writes=["oml4"])
            HG4 = HG.rearrange("(g j p) d -> g p j d", p=128, j=4)
            f2 = lambda ap: ap.rearrange("p a c -> p (a c)")
            for g4 in range(ntile // 4):
                sl = g4 % 2
                S.dma("sp", zt[sl][:], HG4[g4][:, :, zc0:zc0 + 128], "ld_z%d" % sl, writes=["zt%d" % sl])
                S.op("act", lambda e, sl=sl: e.activation(out=f2(w1[:]), in_=f2(zt[sl][:]), func=AF.Exp, scale=-1.0), reads=["zt%d" % sl], writes=["w1"])
                S.op("dve", lambda e: e.tensor_scalar(out=f2(w1[:]), in0=f2(w1[:]), scalar1=1.0, scalar2=None, op0=ALU.add), reads=["w1"], writes=["w1"])
                S.op("dve", lambda e: e.reciprocal(out=f2(w1[:]), in_=f2(w1[:])), reads=["w1"], writes=["w1"])
                S.op("dve", lambda e: e.tensor_tensor(out=f2(w1[:]), in0=f2(w1[:]), in1=f2(oml4[:]), op=ALU.mult), reads=["w1", "oml4"], writes=["w1"])
                S.op("dve", lambda e: e.tensor_tensor(out=f2(w2[:]), in0=f2(w1[:]), in1=f2(lb4[:]), op=ALU.add), reads=["w1", "lb4"], writes=["w2"])
                tk = ["LF%d" % (4 * g4 + j) for j in range(4)]
                kk = ["KK%d" % (4 * g4 + j) for j in range(4)]
                S.op("act", lambda e, g4=g4: e.activation(out=f2(LF[:, 4 * g4:4 * g4 + 4, :]), in_=f2(w2[:]), func=AF.Ln), reads=["w2"], writes=tk)
                S.op("dve", lambda e, g4=g4: e.tensor_scalar(out=f2(KK[:, 4 * g4:4 * g4 + 4, :]), in0=f2(w2[:]), scalar1=-1.0, scalar2=1.0,
                                                            op0=ALU.mult, op1=ALU.add), reads=["w2"], writes=kk)
                for j in range(4):
                    t = 4 * g4 + j
                    S.op("pe", lambda e, t=t, IND=IND: e.matmul(out=pu[:, 0:4], lhsT=LF[:, t, :], rhs=IND, start=True, stop=True),
                         reads=["LF%d" % t, "hc"], writes=["pu"])
                    for a in range(2):
                        S.op("dve", lambda e, t=t, a=a: e.tensor_copy(out=U2[:, 2 * t + a:2 * t + a + 1], in_=pu[:, 2 * a:2 * a + 1]),
                             reads=["pu"], writes=["U2"])
                        S.op("dve", lambda e, t=t, a=a: e.tensor_copy(out=U1[:, 2 * t + a:2 * t + a + 1], in_=pu[:, 2 * a + 1:2 * a + 2]),
                             reads=["pu"], writes=["U1"])
            if fwd:
                S.op("dve", lambda e: e.tensor_tensor(out=G[:, 0:n64 - 1], in0=U1[:, 0:n64 - 1], in1=U2[:, 1:n64], op=ALU.add), reads=["U1", "U2"], writes=["G"])
                S.op("act", lambda e: e.activation(out=G[:, 0:n64 - 1], in_=G[:, 0:n64 - 1], func=AF.Exp), reads=["G"], writes=["G"])
            else:
                S.op("dve", lambda e: e.tensor_tensor(out=G[:, 1:n64], in0=U1[:, 1:n64], in1=U2[:, 0:n64 - 1], op=ALU.add), reads=["U1", "U2"], writes=["G"])
                S.op("act", lambda e: e.activation(out=G[:, 1:n64], in_=G[:, 1:n64], func=AF.Exp), reads=["G"], writes=["G"])
            tiles = list(range(ntile)) if fwd else list(range(ntile - 1, -1, -1))
            first = True

            def prep(ti, t, MT=MT):
                sl = ti % 2
                Vb, V6, Kt, QKT = Vb2[sl], V62[sl], Kt2[sl], QKT2[sl]
                S.dma("sp", qv[sl][:], HGv[t][:, 0:256], "ld_qv%d" % sl, writes=["qv%d" % sl])
                S.dma("act", v64[sl][:], HG6[t][:, :, 128:256], "ld_v6%d" % sl, writes=["v64%d" % sl])
                S.op("pool", lambda e: e.tensor_copy(out=Vb[:], in_=qv[sl][:, 128:256]), reads=["qv%d" % sl], writes=["Vb%d" % sl])
                S.op("pool", lambda e: e.tensor_copy(out=V6[:], in_=v64[sl][:]), reads=["v64%d" % sl], writes=["V6%d" % sl])
                S.op("pe", lambda e: e.matmul(out=pA[:, 0:128], lhsT=MT, rhs=LF[:, t, :], start=True, stop=True),
                     reads=["hc", "LF%d" % t], writes=["pA"])
                S.op("act", lambda e: e.activation(out=EA[:], in_=pA[:, 0:128], func=AF.Exp), reads=["pA"], writes=["EA"])
                S.op("act", lambda e: e.activation(out=EnA[:], in_=pA[:, 0:128], func=AF.Exp, scale=-1.0), reads=["pA"], writes=["EnA"])
                S.op("dve", lambda e: e.tensor_tensor(out=Qt[:], in0=qv[sl][:, 0:128], in1=EA[:], op=ALU.mult), reads=["qv%d" % sl, "EA"], writes=["Qt"])
                S.op("pool", lambda e: e.tensor_tensor(out=Kt[:], in0=KK[:, t, :], in1=EnA[:], op=ALU.mult), reads=["KK%d" % t, "EnA"], writes=["Kt%d" % sl])
                S.op("pe", lambda e: e.transpose(out=pT[:, 0, :], in_=Qt[:], identity=idb[:]), reads=["Qt", "idb"], writes=["pT"])
                S.op("pe", lambda e: e.transpose(out=pT[:, 1, :], in_=Kt[:], identity=idb[:]), reads=["Kt%d" % sl, "idb"], writes=["pT"])
                S.op("dve", lambda e: e.tensor_copy(out=QKT[:], in_=pT[:, 0:2, :]), reads=["pT"], writes=["QKT%d" % sl])

            order = (0, 1) if fwd else (1, 0)
            chunks = [(ti, t, a) for ti, t in enumerate(tiles) for a in order]
            nck = len(chunks)

            def pre(k, MASK=MASK):
                ti, t, a = chunks[k]
                sl = ti % 2
                QKT = QKT2[sl]
                pSc, scT = pSc2[k % 2], scT2[k % 2]
                ca = slice(a * 64, (a + 1) * 64)
                for h_ in range(2):
                    hp = slice(h_ * 64, (h_ + 1) * 64)
                    S.op("pe", lambda e, hp=hp, h_=h_: e.matmul(out=pSc[0:64, h_ * 64:(h_ + 1) * 64], lhsT=QKT[hp, 1, ca], rhs=QKT[hp, 0, ca],
                                                               start=True, stop=True),
                         reads=["QKT%d" % sl], writes=["pSc%d" % (k % 2)])
                S.op("dve", lambda e: e.tensor_tensor(out=scT[:], in0=pSc[0:64, 0:128], in1=MASK, op=ALU.mult),
                     reads=["pSc%d" % (k % 2), "hc"], writes=["scT%d" % (k % 2)])

            def post(k):
                ti, t, a = chunks[k]
                sl = ti % 2
                Vb, V6, Kt, QKT = Vb2[sl], V62[sl], Kt2[sl], QKT2[sl]
                kVb, kV6, kKt, kQKT = "Vb%d" % sl, "V6%d" % sl, "Kt%d" % sl, "QKT%d" % sl
                scT, kscT = scT2[k % 2], "scT%d" % (k % 2)
                c64 = 2 * t + a
                ca = slice(a * 64, (a + 1) * 64)
                first = (k == 0)
                last = (k == nck - 1)
                for h_ in range(2):
                    hp = slice(h_ * 64, (h_ + 1) * 64)
                    S.op("pe", lambda e, hp=hp, h_=h_: e.matmul(out=pO[0:64, h_ * 64:(h_ + 1) * 64], lhsT=scT[:, h_ * 64:(h_ + 1) * 64],
                                                               rhs=V6[:, a, h_ * 64:(h_ + 1) * 64], start=True, stop=first),
                         reads=[kscT, kV6], writes=["pO"])
                    if not first:
                        S.op("pe", lambda e, hp=hp, h_=h_: e.matmul(out=pO[0:64, h_ * 64:(h_ + 1) * 64], lhsT=QKT[hp, 0, ca], rhs=Spb[hp, :],
                                                                   start=False, stop=True),
                             reads=[kQKT, "Spb"], writes=["pO"])
                if fwd:
                    S.op("dve", lambda e: e.tensor_copy(out=OA[:, c64, :], in_=pO[0:64, 0:128]), reads=["pO"], writes=["OA%d" % c64])
                else:
                    S.op("dve", lambda e: e.tensor_tensor(out=OA[:, c64, :], in0=OA[:, c64, :], in1=pO[0:64, 0:128], op=ALU.add),
                         reads=["pO", "OA%d" % c64], writes=["OA%d" % c64])
                if not last:
                    S.op("pe", lambda e: e.matmul(out=pP[:, 0:128], lhsT=Kt[a * 64:(a + 1) * 64, :], rhs=Vb[a * 64:(a + 1) * 64, :],
                                                  start=True, stop=True),
                         reads=[kKt, kVb], writes=["pP"])
                    for h_ in range(2):
                        hp = slice(h_ * 64, (h_ + 1) * 64)
                        if first:
                            S.op("dve", lambda e, hp=hp, h_=h_: e.tensor_copy(out=Tt[hp, :], in_=pP[hp, h_ * 64:(h_ + 1) * 64]), reads=["pP"], writes=["Tt"])
                        else:
                            S.op("dve", lambda e, hp=hp, h_=h_: e.tensor_tensor(out=Tt[hp, :], in0=Spf[hp, :], in1=pP[hp, h_ * 64:(h_ + 1) * 64], op=ALU.add),
                                 reads=["pP", "Spf"], writes=["Tt"])
                    S.op("dve", lambda e: e.tensor_scalar(out=Spf[:], in0=Tt[:], scalar1=G[:, c64:c64 + 1], scalar2=None, op0=ALU.mult),
                         reads=["Tt", "G"], writes=["Spf"])
                    S.op("pool", lambda e: e.tensor_copy(out=Spb[:], in_=Spf[:]), reads=["Spf"], writes=["Spb"])

            prep(0, tiles[0])
            pre(0)
            for k in range(nck):
                ti, t, a = chunks[k]
                if k % 2 == 0 and ti + 1 < len(tiles):
                    prep(ti + 1, tiles[ti + 1])
                if k + 1 < nck:
                    pre(k + 1)
                post(k)
        HGc = HG.rearrange("(n c p) d -> n p c d", p=64, c=8)
        for n in range(n64 // 8):
            S.dma("sp", hgt[:], HGc[n][:, :, 512:640], "ld_hg", writes=["hgt"])
            sl = n % 2
            g2 = lambda ap: ap.rearrange("p a c -> p (a c)")
            S.op("act", lambda e: e.activation(out=g2(sg[:]), in_=g2(hgt[:]), func=AF.Exp, scale=-1.0), reads=["hgt"], writes=["sg"])
            S.op("dve", lambda e: e.tensor_scalar(out=g2(sg[:]), in0=g2(sg[:]), scalar1=1.0, scalar2=None, op0=ALU.add), reads=["sg"], writes=["sg"])
            S.op("dve", lambda e: e.reciprocal(out=g2(sg[:]), in_=g2(sg[:])), reads=["sg"], writes=["sg"])
            for c in range(8):
                c64 = n * 8 + c
                for h in range(2):
                    S.op("act", lambda e, c64=c64, h=h: e.activation(out=junk[:], in_=OA[:, c64, h * 64:(h + 1) * 64], func=AF.Square,
                                                                     accum_out=fst[:, h:h + 1]),
                         reads=["OA%d" % c64], writes=["junk", "fst%d" % h])
                S.op("dve", lambda e: e.tensor_scalar(out=fst[:, 2:4], in0=fst[:, 0:2], scalar1=1.0 / 64, scalar2=EPS, op0=ALU.mult, op1=ALU.add),
                     reads=["fst0", "fst1"], writes=["fst23"])
                S.op("act", lambda e: e.activation(out=fst[:, 2:4], in_=fst[:, 2:4], func=AF.Ln), reads=["fst23"], writes=["fst23"])
                S.op("act", lambda e: e.activation(out=fst[:, 2:4], in_=fst[:, 2:4], func=AF.Exp, scale=-0.5), reads=["fst23"], writes=["fst23"])
                for h in range(2):
                    S.op("dve", lambda e, c64=c64, h=h: e.scalar_tensor_tensor(out=yy[:, h * 64:(h + 1) * 64], in0=OA[:, c64, h * 64:(h + 1) * 64],
                                                                              scalar=fst[:, 2 + h:3 + h], in1=gon[:], op0=ALU.mult, op1=ALU.mult),
                         reads=["OA%d" % c64, "fst23", "gon"], writes=["yy"])
                S.op("dve", lambda e, c=c: e.tensor_tensor(out=yb[:], in0=yy[:], in1=sg[:, c, :], op=ALU.mult), reads=["yy", "sg"], writes=["yb"])
                S.op("pe", lambda e, c=c: e.transpose(out=pF[:, c, 0:64], in_=yb[:], identity=idb[0:64, 0:64]), reads=["yb", "idb"], writes=["pF"])
                S.op("dve", lambda e, c=c, sl=sl: e.tensor_copy(out=oS[sl][:, c * 64:(c + 1) * 64], in_=pF[:, c, 0:64]), reads=["pF"], writes=["oS%d" % sl])
            S.dma("sp", oT_out[384:512, n * 512:(n + 1) * 512], oS[sl][:], "st_o", reads=["oS%d" % sl], final=True)
        S.emit()

from concourse.bass_utils import run_bass_kernel_spmd

NTOK = 8192
RG_PAIRS = [[0, 1], [2, 3], [4, 5], [6, 7]]
_CACHE = {}
LAYER_VECS = (("ln", D), ("g_qa", 384), ("g_kva", 128), ("g_qn", 96), ("g_kn", 96), ("dg_qn", 32), ("dg_kn", 32),
              ("lq1", 32), ("lk1", 32), ("lq2", 32), ("lk2", 32), ("g_sub", 64), ("g_on", 64), ("ln2", D))
LAYER_MATS = (("w_in", (D, NOWN)), ("w_uq", (384, 192)), ("w_ukv", (128, 256)), ("wg", (D, 4096)), ("wbr", (1024, D)),
              ("wo", (D, D)), ("wgu", (D, 2 * DFF)), ("wd", (DFF, D)))


def own_cols(j):
    c = list(range(0, 544))
    for base in (544, 800, 1056, 1312, 1568, 1824, 2080, 2336, 2592):
        c += list(range(base + 128 * j, base + 128 * j + 128))
    return np.array(c)


def rope_table(n):
    inv = (1.0 / (10000.0 ** (np.arange(0, 32, 2, dtype=np.float32) / 32))).astype(np.float32)
    ang = np.arange(n, dtype=np.float32)[:, None] * inv[None, :]
    return np.concatenate([np.tile(np.cos(ang), (1, 4)), np.tile(np.sin(ang), (1, 4))], -1).astype(np.float32)


def emit_allgather(nc, pairs):
    S = Sched(nc)
    cs_ = []
    for src, dst in pairs:
        cs_.append(S.op("pool", lambda e, src=src, dst=dst: e.collective_compute("AllGather", ALU.bypass, replica_groups=RG_PAIRS,
                                                                              ins=[src], outs=[dst])))
    S.op("pool", lambda e: e.nop(), deps=cs_)
    S.emit()


def build_fused(nl=2):
    nc = bass.Bass("TRN2", target_bir_lowering=False)
    nc.allow_low_precision("bf16 matmul operands, fp32 accumulate")
    I = lambda name, shape, dt=F32: nc.dram_tensor(name, list(shape), dt, kind="ExternalInput").ap()
    N = lambda name, shape, dt: nc.dram_tensor(name, list(shape), dt, kind="Internal").ap()
    x_full = I("x_full", (NTOK, D)); x_my = I("x_my", (NT_R, D)); selmask = I("selmask", (2,))
    cs = I("cs", (NTOK, 128)); ident = I("ident", (128, 128)); fconst = I("fconst", (128, NFC)); hconst = I("hconst", (128, NHC))
    lbl = I("lbl", (2, 2, 128))
    W = []
    for l in range(2):
        d = {}
        for name, n in LAYER_VECS:
            d[name] = I("%s_%d" % (name, l), (n,))
        for name, shp in LAYER_MATS:
            d[name] = I("%s_%d" % (name, l), shp)
        W.append(d)
    QT = N("QT", (2, 128, NTOK), BF16); KT = N("KT", (2, 128, NTOK), BF16); V = N("V", (NTOK, 128), BF16)
    DQT = N("DQT", (2, 64, NTOK), BF16); DKT = N("DKT", (2, 64, NTOK), BF16); DV = N("DV", (NTOK, 128), BF16)
    U = N("U", (NTOK, 128), F32); HG = N("HG", (NTOK, 640), F32)
    UCS = N("UCS", (NTOK, 256), BF16); XPd = N("XPd", (128, 64, 256), BF16); Yd = N("Yd", (NTOK, 128), BF16)
    oT = N("oT", (512, NTOK), BF16); Gd = N("Gd", (4, 256, NTOK), BF16)
    xmid = N("xmid", (NT_R, D), F32); x1_my = N("x1_my", (NT_R, D), F32); XG = N("XG", (8, 2, 512, D), F32)
    x_out = nc.dram_tensor("x_out", [NT_R, D], F32, kind="ExternalOutput").ap()
    for l in range(nl):
        TAG[0] = "L%d_" % l
        w = W[l]
        XGv = XG.rearrange("k r (j p) d -> r k j p d", p=128)
        xf = x_full if l == 0 else (lambda t: XGv[t // 32, (t % 32) // 4, t % 4])
        emit_M1(nc, xf, w["w_in"], w["ln"], w["g_qa"], w["w_uq"], w["g_kva"], w["w_ukv"], w["g_qn"], w["g_kn"], w["dg_qn"], w["dg_kn"],
                cs, ident, QT, KT, V, DQT, DKT, DV, U, HG, ntok=NTOK)
        emit_attn(nc, QT, KT, V, DQT, DKT, DV, w["lq1"], w["lk1"], w["lq2"], w["lk2"], w["g_sub"], oT, l, NTOK)
        emit_fnet(nc, U, fconst, ident, UCS, XPd, Yd, oT)
        emit_hgrn(nc, HG, lbl, w["g_on"], hconst, ident, oT, l, NTOK)
        emit_allgather(nc, [(oT[n * 128:(n + 1) * 128, :], Gd[n]) for n in range(4)])
        xm = x_my if l == 0 else x1_my
        xo = x_out if l == nl - 1 else x1_my
        emit_R1(nc, xm, None, w["wg"], w["wbr"], w["wo"], w["ln"], xmid, ident, NT_R, gathered=(Gd, selmask))
        emit_R2(nc, xmid, w["wgu"], w["wd"], w["ln2"], xo, ident, NT_R)
        if l < nl - 1:
            emit_allgather(nc, [(x1_my[k * 512:(k + 1) * 512, :], XG[k].rearrange("r i d -> (r i) d")) for k in range(8)])
    TAG[0] = ""
    return nc


def kernel(x, ln_mix, w_in, mla_g_qa, mla_w_uq, mla_g_kva, mla_w_ukv, mla_g_qn, mla_g_kn,
           diff_g_qn, diff_g_kn, diff_lq1, diff_lk1, diff_lq2, diff_lk2, diff_g_sub,
           hgrn_lb_logits, hgrn_g_on, w_branch, w_out, ln_ffn, w_gate_up, w_down):
    f32 = lambda a: np.ascontiguousarray(np.asarray(a, dtype=np.float32))
    x = f32(x)
    B = x.shape[0]
    if "F" not in _CACHE:
        _CACHE["F"] = build_fused()
    nc = _CACHE["F"]
    shared = dict(cs=rope_table(NTOK), ident=np.eye(128, dtype=np.float32), fconst=fnet_consts(), hconst=hgrn_consts())
    lbl_all = f32(hgrn_lb_logits)
    per_layer = []
    for l in range(2):
        per_layer.append({
            "ln": f32(ln_mix[l]), "g_qa": f32(mla_g_qa[l]), "g_kva": f32(mla_g_kva[l]), "g_qn": f32(mla_g_qn[l]), "g_kn": f32(mla_g_kn[l]),
            "dg_qn": f32(diff_g_qn[l]), "dg_kn": f32(diff_g_kn[l]), "lq1": f32(diff_lq1[l]), "lk1": f32(diff_lk1[l]),
            "lq2": f32(diff_lq2[l]), "lk2": f32(diff_lk2[l]), "g_sub": f32(diff_g_sub[l]), "g_on": f32(hgrn_g_on[l]), "ln2": f32(ln_ffn[l]),
            "wg": f32(np.asarray(w_in[l])[:, 2848:6944]), "wbr": f32(np.asarray(w_branch[l]).reshape(1024, D)), "wo": f32(w_out[l]),
            "wgu": f32(w_gate_up[l]), "wd": f32(w_down[l])})
    own = {}
    for j in range(2):
        oc = own_cols(j)
        own[j] = [{"w_in": f32(np.asarray(w_in[l])[:, oc]), "w_uq": f32(np.asarray(mla_w_uq[l])[:, 192 * j:192 * j + 192]),
                   "w_ukv": f32(np.asarray(mla_w_ukv[l])[:, 256 * j:256 * j + 256])} for l in range(2)]
    in_maps = []
    cores = [(b, j) for b in range(B) for j in range(2)]
    for (b, j) in cores:
        m = dict(shared)
        m["x_full"] = x[b]
        m["x_my"] = np.ascontiguousarray(x[b, j * NT_R:(j + 1) * NT_R])
        m["selmask"] = np.array([1.0, 0.0] if j == 0 else [0.0, 1.0], dtype=np.float32)
        m["lbl"] = np.ascontiguousarray(lbl_all[:, :, 128 * j:128 * j + 128])
        for l in range(2):
            for k, v in per_layer[l].items():
                m["%s_%d" % (k, l)] = v
            for k, v in own[j][l].items():
                m["%s_%d" % (k, l)] = v
        in_maps.append(m)
    res = run_bass_kernel_spmd(nc, in_maps, core_ids=list(range(8)))
    out = np.empty_like(x)
    for i, (b, j) in enumerate(cores):
        out[b, j * NT_R:(j + 1) * NT_R] = res.results[i]["x_out"]
    return out
```
